# Optimizing a Trainium2 kernel written in Bass

```python
import jax
import jax.numpy as jnp
from jax import lax
import numpy as np

D_MODEL = 2048
BATCH = 4
SEQ = 4096
DEPTH = 1
DEC_BATCH = 1
DEC_SEQ = 16384
PAST_LEN = 128

N_META = 16
HG_HEADS = 8
HG_DK = 128
HG_DV = 128
HG_KDIM = HG_HEADS * HG_DK
HG_WIDTH = HG_HEADS * HG_DV
HG_CHUNK = 64
HG_PAD = HG_CHUNK - N_META
N_Q_HEADS = 16
N_KV_HEADS = 4
HEAD_DIM = 64
GROUP = N_Q_HEADS // N_KV_HEADS
ATT_WIDTH = N_Q_HEADS * HEAD_DIM
KV_WIDTH = N_KV_HEADS * HEAD_DIM
WINDOW = 128
ATT_BLOCK = 128
D_FF = 4 * D_MODEL
EPS = 1e-6
IN_WIDTHS = (HG_KDIM, HG_WIDTH, HG_KDIM, HG_KDIM, HG_WIDTH, ATT_WIDTH, KV_WIDTH, KV_WIDTH, D_MODEL, D_MODEL)
IN_COLS = 3 * HG_KDIM + 2 * HG_WIDTH + ATT_WIDTH + 2 * KV_WIDTH + 2 * D_MODEL

kernel_name = "hybrid_hgrn2_swa_meta_encoder"


def rmsnorm(x, gain):
    x32 = x.astype(jnp.float32)
    y = x32 * lax.rsqrt(jnp.mean(x32 * x32, axis=-1, keepdims=True) + EPS)
    return (y * gain.astype(jnp.float32)).astype(x.dtype)


def split_columns(p):
    offsets = []
    acc = 0
    for w in IN_WIDTHS[:-1]:
        acc += w
        offsets.append(acc)
    return jnp.split(p, offsets, axis=-1)


def alibi_slopes():
    return jnp.exp2(-8.0 * jnp.arange(1, N_Q_HEADS + 1, dtype=jnp.float32) / N_Q_HEADS)


def gla_chunk_scan(q, k, v, logf):
    B, T, H, DK = q.shape
    DV = v.shape[-1]
    C = HG_CHUNK
    N = T // C
    q = q.reshape(B, N, C, H, DK)
    k = k.reshape(B, N, C, H, DK)
    v = v.reshape(B, N, C, H, DV)
    b = jnp.cumsum(logf.reshape(B, N, C, H, DK), axis=2)
    b_last = b[:, :, -1:]
    q_dec = q * jnp.exp(b)
    k_inv = k * jnp.exp(-b)
    k_end = k * jnp.exp(b_last - b)
    scores = jnp.einsum('bnthd,bnshd->bnhts', q_dec, k_inv)
    lower = jnp.tril(jnp.ones((C, C), dtype=bool))
    scores = jnp.where(lower, scores, 0.0)
    o_intra = jnp.einsum('bnhts,bnshv->bnthv', scores, v)
    inc = jnp.einsum('bnshd,bnshv->bnhdv', k_end, v)
    decay = jnp.exp(b_last[:, :, 0])

    def step(state, xs):
        dec, upd = xs
        return dec[..., None] * state + upd, state

    init = jnp.zeros((B, H, DK, DV), q.dtype)
    _, s_prev = lax.scan(step, init, (jnp.moveaxis(decay, 1, 0), jnp.moveaxis(inc, 1, 0)))
    o_inter = jnp.einsum('bnthd,nbhdv->bnthv', q_dec, s_prev)
    return (o_intra + o_inter).reshape(B, T, H, DV)


def hgrn2_mixer(q, i, f_fwd, f_bwd, g, lb_fwd, lb_bwd, out_gain):
    B, L, _ = q.shape
    dt = q.dtype
    f32 = jnp.float32
    qh = (jax.nn.silu(q.astype(f32)) * (HG_DK ** -0.5)).reshape(B, L, HG_HEADS, HG_DK)
    ih = i.astype(f32).reshape(B, L, HG_HEADS, HG_DV)

    def gates(f, lb):
        fg = lb + (1.0 - lb) * jax.nn.sigmoid(f.astype(f32))
        return (1.0 - fg).reshape(B, L, HG_HEADS, HG_DK), jnp.log(fg).reshape(B, L, HG_HEADS, HG_DK)

    def pad_front(a):
        return jnp.pad(a, ((0, 0), (HG_PAD, 0), (0, 0), (0, 0)))

    def flip(a):
        return jnp.flip(a, axis=1)

    k_f, logf_f = gates(f_fwd, lb_fwd)
    k_b, logf_b = gates(f_bwd, lb_bwd)
    qp, ip = pad_front(qh), pad_front(ih)
    o_f = gla_chunk_scan(qp, pad_front(k_f), ip, pad_front(logf_f))
    o_b = flip(gla_chunk_scan(flip(qp), flip(pad_front(k_b)), flip(ip), flip(pad_front(logf_b))))
    o = (o_f + o_b)[:, HG_PAD:]
    o = o * lax.rsqrt(jnp.mean(o * o, axis=-1, keepdims=True) + EPS)
    o = o.reshape(B, L, HG_WIDTH) * out_gain.astype(f32) * jax.nn.silu(g.astype(f32))
    return o.astype(dt)


def sink_softmax(scores, sink):
    m = jnp.maximum(jnp.max(scores, axis=-1, keepdims=True), sink)
    p = jnp.exp(scores - m)
    return p / (jnp.sum(p, axis=-1, keepdims=True) + jnp.exp(sink - m))


def window_attention(q, k, v, sink):
    B, L, _ = q.shape
    S = L - N_META
    nb = S // ATT_BLOCK
    dt = q.dtype
    f32 = jnp.float32
    q = (q.astype(f32) * (HEAD_DIM ** -0.5)).reshape(B, L, N_KV_HEADS, GROUP, HEAD_DIM)
    k = k.astype(f32).reshape(B, L, N_KV_HEADS, HEAD_DIM)
    v = v.astype(f32).reshape(B, L, N_KV_HEADS, HEAD_DIM)
    slopes = alibi_slopes().reshape(N_KV_HEADS, GROUP)
    sink = sink.astype(f32).reshape(N_KV_HEADS, GROUP)
    q_meta, q_tok = q[:, :N_META], q[:, N_META:]
    k_meta, k_tok = k[:, :N_META], k[:, N_META:]
    v_meta, v_tok = v[:, :N_META], v[:, N_META:]

    qb = q_tok.reshape(B, nb, ATT_BLOCK, N_KV_HEADS, GROUP, HEAD_DIM)

    def neighbours(a):
        ab = jnp.pad(a, ((0, 0), (ATT_BLOCK, ATT_BLOCK), (0, 0), (0, 0)))
        ab = ab.reshape(B, nb + 2, ATT_BLOCK, N_KV_HEADS, HEAD_DIM)
        return jnp.concatenate([ab[:, :-2], ab[:, 1:-1], ab[:, 2:]], axis=2)

    kw, vw = neighbours(k_tok), neighbours(v_tok)
    s_win = jnp.einsum('bnihgd,bnjhd->bhgnij', qb, kw)
    s_mk = jnp.einsum('bnihgd,bmhd->bhgnim', qb, k_meta)
    r = jnp.arange(ATT_BLOCK)[:, None]
    u = jnp.arange(3 * ATT_BLOCK)[None, :]
    dist = jnp.abs(u - ATT_BLOCK - r)
    key_pos = (jnp.arange(nb)[:, None, None] - 1) * ATT_BLOCK + u[None]
    valid = (dist <= WINDOW)[None] & (key_pos >= 0) & (key_pos < S)
    bias = -slopes[:, :, None, None, None] * dist.astype(f32)
    s_win = jnp.where(valid, s_win + bias, -jnp.inf)
    p = sink_softmax(jnp.concatenate([s_mk, s_win], axis=-1), sink[None, :, :, None, None, None])
    o_tok = (jnp.einsum('bhgnim,bmhd->bnihgd', p[..., :N_META], v_meta)
             + jnp.einsum('bhgnij,bnjhd->bnihgd', p[..., N_META:], vw))
    o_tok = o_tok.reshape(B, S, ATT_WIDTH)

    k_first, v_first = k_tok[:, :ATT_BLOCK], v_tok[:, :ATT_BLOCK]
    sm_m = jnp.einsum('bihgd,bmhd->bhgim', q_meta, k_meta)
    sm_t = jnp.einsum('bihgd,bjhd->bhgij', q_meta, k_first)
    mdist = N_META + jnp.arange(ATT_BLOCK)[None, :] - jnp.arange(N_META)[:, None]
    sm_t = jnp.where(mdist <= WINDOW, sm_t - slopes[:, :, None, None] * mdist.astype(f32), -jnp.inf)
    pm = sink_softmax(jnp.concatenate([sm_m, sm_t], axis=-1), sink[None, :, :, None, None])
    o_meta = (jnp.einsum('bhgim,bmhd->bihgd', pm[..., :N_META], v_meta)
              + jnp.einsum('bhgij,bjhd->bihgd', pm[..., N_META:], v_first))
    o_meta = o_meta.reshape(B, N_META, ATT_WIDTH)
    return jnp.concatenate([o_meta, o_tok], axis=1).astype(dt)


def encoder_layer(x, w_in, w_proj_hg, w_proj_att, w_out, w_ff1, w_ff2,
                  g_pre_mix, g_post_mix, g_pre_ff, g_post_ff, lb_fwd, lb_bwd, hg_out_gain, attn_sink):
    h = rmsnorm(x, g_pre_mix)
    (hq, hi, hf_f, hf_b, hg, aq, ak, av, gate_a, gate_b) = split_columns(h @ w_in)
    y_hg = hgrn2_mixer(hq, hi, hf_f, hf_b, hg, lb_fwd, lb_bwd, hg_out_gain) @ w_proj_hg
    y_att = window_attention(aq, ak, av, attn_sink) @ w_proj_att
    merged = jax.nn.sigmoid(gate_a) * y_hg + jax.nn.sigmoid(gate_b) * y_att
    x = x + rmsnorm(merged @ w_out, g_post_mix)
    h = rmsnorm(x, g_pre_ff)
    ff = jnp.square(jax.nn.relu(h @ w_ff1)) @ w_ff2
    return x + rmsnorm(ff, g_post_ff)


def encoder_trunk(x, meta_tokens, w_in, w_proj_hg, w_proj_att, w_out, w_ff1, w_ff2,
                  g_pre_mix, g_post_mix, g_pre_ff, g_post_ff, lb_logits, hg_out_gain, attn_sink):
    B = x.shape[0]
    meta = jnp.broadcast_to(meta_tokens[None].astype(x.dtype), (B, N_META, D_MODEL))
    x = jnp.concatenate([meta, x], axis=1)
    lbs = jnp.cumsum(jax.nn.softmax(lb_logits.astype(jnp.float32), axis=0), axis=0)
    for l in range(DEPTH):
        x = encoder_layer(x, w_in[l], w_proj_hg[l], w_proj_att[l], w_out[l], w_ff1[l], w_ff2[l],
                          g_pre_mix[l], g_post_mix[l], g_pre_ff[l], g_post_ff[l],
                          lbs[l, 0], lbs[l, 1], hg_out_gain[l], attn_sink[l])
    return x[:, N_META:]


def setup_inputs(seed: int = 0) -> dict:
    key = jax.random.key(seed)
    ks = jax.random.split(key, 16)
    f32 = jnp.float32

    def nrm(k, shape, scale):
        return jax.random.normal(k, shape, f32) * scale

    return {
        'x_prompt': nrm(ks[0], (BATCH, SEQ, D_MODEL), 1.0),
        'x_sample': nrm(ks[1], (DEC_BATCH, DEC_SEQ, D_MODEL), 1.0),
        'meta_tokens': nrm(ks[2], (N_META, D_MODEL), 1.0),
        'w_in': nrm(ks[3], (DEPTH, D_MODEL, IN_COLS), D_MODEL ** -0.5),
        'w_proj_hg': nrm(ks[4], (DEPTH, HG_WIDTH, D_MODEL), HG_WIDTH ** -0.5),
        'w_proj_att': nrm(ks[5], (DEPTH, ATT_WIDTH, D_MODEL), ATT_WIDTH ** -0.5),
        'w_out': nrm(ks[6], (DEPTH, D_MODEL, D_MODEL), D_MODEL ** -0.5),
        'w_ff1': nrm(ks[7], (DEPTH, D_MODEL, D_FF), D_MODEL ** -0.5),
        'w_ff2': nrm(ks[8], (DEPTH, D_FF, D_MODEL), D_FF ** -0.5),
        'g_pre_mix': 1.0 + nrm(ks[9], (DEPTH, D_MODEL), 0.01),
        'g_post_mix': 1.0 + nrm(ks[10], (DEPTH, D_MODEL), 0.01),
        'g_pre_ff': 1.0 + nrm(ks[11], (DEPTH, D_MODEL), 0.01),
        'g_post_ff': 1.0 + nrm(ks[12], (DEPTH, D_MODEL), 0.01),
        'lb_logits': nrm(ks[13], (DEPTH + 1, 2, HG_KDIM), 0.01),
        'hg_out_gain': 1.0 + nrm(ks[14], (DEPTH, HG_WIDTH), 0.01),
        'attn_sink': nrm(ks[15], (DEPTH, N_Q_HEADS), 0.5),
    }


def reference(x_prompt, x_sample, meta_tokens, w_in, w_proj_hg, w_proj_att, w_out, w_ff1, w_ff2,
              g_pre_mix, g_post_mix, g_pre_ff, g_post_ff, lb_logits, hg_out_gain, attn_sink):
    y_prompt = encoder_trunk(x_prompt, meta_tokens, w_in, w_proj_hg, w_proj_att, w_out, w_ff1, w_ff2,
                             g_pre_mix, g_post_mix, g_pre_ff, g_post_ff, lb_logits, hg_out_gain, attn_sink)
    y_sample = encoder_trunk(x_sample, meta_tokens, w_in, w_proj_hg, w_proj_att, w_out, w_ff1, w_ff2,
                             g_pre_mix, g_post_mix, g_pre_ff, g_post_ff, lb_logits, hg_out_gain, attn_sink)
    return (y_prompt, y_sample)
```

```python
import numpy as np
from contextlib import ExitStack
import concourse.bass as bass
import concourse.mybir as mybir
from concourse.bass_utils import run_bass_kernel_spmd

F32 = mybir.dt.float32
BF16 = mybir.dt.bfloat16
AF = mybir.ActivationFunctionType
ALU = mybir.AluOpType

D = 2048
KC = 16
NIN = 10752
DFF = 8192
TT = 512
EPS = 1e-6
C_Q, C_I, C_FF, C_FB, C_G, C_AQ, C_AK, C_AV, C_GA, C_GB = 0, 1024, 2048, 3072, 4096, 5120, 6144, 6400, 6656, 8704
NEG = -1.0e9
ENGS = ("pe", "act", "dve", "pool", "sp")


class Op:
    __slots__ = ("eng", "fn", "waits", "idx", "marked", "dma_sem", "dma_val", "val")

    def __init__(self, eng, fn):
        self.eng = eng
        self.fn = fn
        self.waits = []
        self.idx = None
        self.marked = False
        self.dma_sem = None
        self.dma_val = None
        self.val = None


class Prog:
    def __init__(self, same_engine_sync=("act", "dve", "pool")):
        self.ops = {e: [] for e in ENGS}
        self.last_writes = {}
        self.readers = {}
        self.known = {e: {} for e in ENGS}
        self.dma_counts = {}
        self.same_engine_sync = set(same_engine_sync)
        self.pending = {e: [] for e in ENGS}

    @staticmethod
    def _stream(op):
        return ("dma", op.dma_sem) if op.dma_sem is not None else ("eng", op.eng)

    def _need(self, op, dep):
        if dep is None or dep is op:
            return
        st = self._stream(dep)
        if dep.dma_sem is not None:
            v = dep.dma_val
        else:
            if dep.eng == op.eng and dep.eng not in self.same_engine_sync:
                return
            v = dep.idx
        k = self.known[op.eng]
        if k.get(st, -1) >= v:
            return
        k[st] = v
        op.waits = [w for w in op.waits if self._stream(w) != st]
        op.waits.append(dep)

    def emit(self, eng, fn, reads=(), writes=(), dma_sem=None, ndma=1):
        op = Op(eng, fn)
        op.idx = len(self.ops[eng])
        if dma_sem is not None:
            c = self.dma_counts.get(dma_sem, 0) + ndma
            self.dma_counts[dma_sem] = c
            op.dma_sem = dma_sem
            op.dma_val = 16 * c
        for d in self.pending[eng]:
            self._need(op, d)
        self.pending[eng] = []
        for r in reads:
            for d in self.last_writes.get(r, {}).values():
                self._need(op, d)
        for w in writes:
            for d in self.last_writes.get(w, {}).values():
                self._need(op, d)
            for rd in self.readers.get(w, ()):
                self._need(op, rd)
        for r in reads:
            self.readers.setdefault(r, []).append(op)
        for w in writes:
            self.last_writes.setdefault(w, {})[self._stream(op)] = op
            self.readers[w] = []
        self.ops[eng].append(op)
        return op

    def barrier(self, engs=("pe", "act", "dve", "pool")):
        lasts = [self.ops[e][-1] for e in engs if self.ops[e]]
        for e in engs:
            self.pending[e] = list(lasts)

    def replay(self, nc, sems, dma_sems, final_eng="sp"):
        for e in ENGS:
            for op in self.ops[e]:
                for d in op.waits:
                    if d.dma_sem is None:
                        d.marked = True
        for e in ENGS:
            c = 0
            for op in self.ops[e]:
                if op.dma_sem is None and op.marked:
                    c += 1
                    op.val = c
        handles = {"pe": "tensor", "act": "scalar", "dve": "vector", "pool": "gpsimd", "sp": "sync"}
        finals = [(dma_sems[n], 16 * c) for n, c in self.dma_counts.items()]
        with nc.Block() as block:
            for e in ENGS:
                ops = self.ops[e]

                def body(engine, ops=ops, e=e):
                    for op in ops:
                        for d in op.waits:
                            if d.dma_sem is not None:
                                engine.wait_ge(dma_sems[d.dma_sem], d.dma_val)
                            else:
                                engine.wait_ge(sems[d.eng], d.val)
                        ins = op.fn(engine)
                        if op.dma_sem is None and op.marked:
                            ins.then_inc(sems[e], 1)
                    if e == final_eng:
                        for s, v in finals:
                            engine.wait_ge(s, v)
                getattr(block, handles[e])(body)


def build(NT, NPRE, dbg=False):
    NTOK = NT * TT
    nc = bass.Bass("TRN2", target_bir_lowering=False)
    es = ExitStack()
    import os as _os
    P = Prog(same_engine_sync=tuple(x for x in _os.environ.get("SES", "act,dve,pool").split(",") if x))
    dma_sem_names = []

    def din(name, shape, dt=F32):
        return nc.dram_tensor(name, list(shape), dt, kind="ExternalInput").ap()

    x_ext = din("x_ext", [NTOK + 256, D])
    x_pre = din("x_pre", [max(NPRE, 1) * TT, D])
    meta = din("meta", [128, D])
    w_in = din("w_in", [D, NIN])
    w_phg = din("w_phg", [1024, D])
    w_patt = din("w_patt", [1024, D])
    w_out = din("w_out", [D, D])
    w_ff1 = din("w_ff1", [D, DFF])
    w_ff2 = din("w_ff2", [DFF, D])
    gpre = din("gpre", [128, 2, KC])
    gpost = din("gpost", [2, D])
    lbl = din("lbl", [128, 2, 2, 8])
    hgain = din("hgain", [128, 8])
    sink = din("sink", [128, 8])
    premask = din("premask", [128, max(NPRE, 1) * 2])
    dtab = din("dtab", [128, 5, 128])
    hgmask = din("hgmask", [128, 2, 2, 64])
    rowmask = din("rowmask", [128, 2])
    scanmask = din("scanmask", [128, TT])
    identf = din("identf", [128, 128])
    onesc_d = din("onesc", [128, 4, 128])
    y = nc.dram_tensor("y", [NTOK, D], F32, kind="ExternalOutput").ap()
    wb_in = nc.dram_tensor("wb_in", [D, NIN], BF16, kind="Internal").ap()
    wb_kd = nc.dram_tensor("wb_kd", [D, 512], BF16, kind="Internal").ap()
    wb_p = nc.dram_tensor("wb_p", [2048, D], BF16, kind="Internal").ap()
    wb_out = nc.dram_tensor("wb_out", [D, D], BF16, kind="Internal").ap()
    wb_ff1 = nc.dram_tensor("wb_ff1", [D, DFF], BF16, kind="Internal").ap()
    wb_ff2 = nc.dram_tensor("wb_ff2", [DFF, D], BF16, kind="Internal").ap()
    ob_d = nc.dram_tensor("ob_d", [8, 128, NTOK], F32, kind="Internal").ap()
    dbg_out = {}

    with es:
        def sb(name, shape, dt):
            return es.enter_context(nc.sbuf_tensor(name, list(shape), dt))

        def pst(name, shape, dt=F32):
            return es.enter_context(nc.psum_tensor(name, list(shape), dt))

        NSLOT = 3
        wsl = [sb(f"wsl{i}", [128, KC, 512], BF16) for i in range(NSLOT)]
        Sst = sb("Sst", [128, 2, 8, 128], F32)
        ident_f = sb("ident_f", [128, 128], F32)
        ident_b = sb("ident_b", [128, 128], BF16)
        ones_b = sb("ones_b", [128, 128], BF16)
        hgm = sb("hgm", [128, 2, 2, 64], F32)
        rowm = sb("rowm", [128, 2], F32)
        scm = sb("scm", [128, TT], F32)
        dtb = sb("dtb", [128, 5, 128], F32)
        gpre_s = sb("gpre_s", [128, 2, KC], F32)
        lb_s = sb("lb_s", [128, 2, 2, 8], F32)
        oml = sb("oml", [128, 2, 8], F32)
        c1t = sb("c1t", [128, 2, 8], F32)
        nc1t = sb("nc1t", [128, 2, 8], F32)
        hgain_s = sb("hgain_s", [128, 8], F32)
        esink = sb("esink", [128, 8], F32)
        pmask = sb("pmask", [128, max(NPRE, 1) * 2], F32)
        stat = sb("stat", [128, 16], F32)
        kmT = sb("kmT", [128, 4, 128], BF16)
        vm = sb("vm", [128, 4, 2, 128], BF16)
        rowm8 = sb("rowm8", [128, 2], F32)
        arena = sb("arena", [128, 144128], mybir.dt.uint8)

        def carve(off_kb, shape, dt):
            nbytes = int(np.prod(shape[1:])) * (2 if dt == BF16 else 4)
            off = int(off_kb * 1024)
            assert off + nbytes <= 144128, (off_kb, shape)
            v = arena[:, off:off + nbytes].bitcast(dt)
            if len(shape) == 2:
                return v
            names = " ".join(f"a{i}" for i in range(len(shape) - 1))
            kw = {f"a{i}": shape[i + 1] for i in range(len(shape) - 1)}
            return v.rearrange(f"p ({names}) -> p {names}", **kw)

        x1 = carve(0, [128, 4, D], F32)
        xrow = [carve(0, [128, D], F32), carve(8, [128, D], F32)]
        xsf = [carve(16, [128, D], F32), carve(24, [128, D], F32)]
        mergedT = carve(32, [128, KC, TT], BF16)
        xT = carve(48, [128, KC, 768], BF16)
        oT_att = carve(72, [128, 8, TT], BF16)
        oT_hg = carve(80, [128, 8, TT], BF16)
        TB = 88
        qTa = carve(TB, [128, 2, 8, TT], BF16)
        kTd = carve(TB + 16, [128, 4, 768], BF16)
        vat = carve(TB + 22, [128, 6, 4, 2, 128], BF16)
        PT = [carve(TB + 34 + 4 * i, [128, 4, 512], BF16) for i in range(2)]
        scs = [carve(TB + 42 + 3 * i, [128, 3, 256], F32) for i in range(2)]
        rec = [carve(TB + 48 + i, [128, 256], F32) for i in range(2)]
        onesc = carve(TB + 50, [128, 4, 128], BF16)
        mg = [carve(TB + 2 * i, [128, TT], F32) for i in range(8)]
        zbuf = carve(48, [128, 4, D], F32)
        h2T = carve(96, [128, KC, TT], BF16)
        xsf2 = [carve(112, [128, D], F32), carve(120, [128, D], F32)]
        gpo = carve(128, [128, D], F32)
        uT = carve(32, [128, 64, TT], BF16)
        rtmp = [carve(112 + 2 * i, [128, TT], F32) for i in range(4)]
        z2 = carve(96, [128, 4, D], F32)
        junkb = carve(136, [128, D], BF16)

        ps = [pst(f"ps{i}", [128, 512]) for i in range(8)]
        psb = ps[7][:, :].bitcast(BF16)

        sems = {e: es.enter_context(nc.semaphore("s_" + e)) for e in ENGS}
        dsem = {}

        def dma(eng, out, in_, sem, reads=(), writes=()):
            if sem not in dsem:
                dsem[sem] = es.enter_context(nc.semaphore("d_" + sem))
            s = dsem[sem]
            return P.emit(eng, lambda e: e.dma_start(out=out, in_=in_).then_inc(s, 16),
                          reads=reads, writes=writes, dma_sem=sem)

        def dump(name, ap, keys):
            if not dbg:
                return
            t = nc.dram_tensor("dbg_" + name, list(ap.shape), F32, kind="ExternalOutput").ap()
            idx = tuple(slice(None) for _ in ap.shape)
            add(None, lambda slot: dma("pool", t[idx], ap, "dbg_" + name, reads=keys))

        def emit_m(eng, meth, kw, reads=(), writes=()):
            return P.emit(eng, lambda e: getattr(e, meth)(**kw), reads=reads, writes=writes)

        rr = {"ev": 0}

        def evac_eng():
            rr["ev"] ^= 1
            return "act" if rr["ev"] else "dve"

        def copy_op(eng, out, in_, reads, writes, scale=None):
            if eng == "act":
                if scale is None:
                    return emit_m("act", "activation", dict(out=out, in_=in_, func=AF.Copy), reads=reads, writes=writes)
                return emit_m("act", "activation", dict(out=out, in_=in_, func=AF.Copy, scale=scale), reads=reads, writes=writes)
            if scale is None:
                return P.emit(eng, lambda e: e.tensor_copy(out=out, in_=in_), reads=reads, writes=writes)
            return P.emit(eng, lambda e: e.tensor_scalar(out=out, in0=in_, scalar1=scale, scalar2=None, op0=ALU.mult), reads=reads, writes=writes)

        def act(out, in_, func, reads, writes, **kw):
            return emit_m("act", "activation", dict(out=out, in_=in_, func=func, **kw), reads=reads, writes=writes)

        def mm(out, lhsT, rhs, start, stop, reads, writes, **kw):
            return P.emit("pe", lambda e: e.matmul(out, lhsT=lhsT, rhs=rhs, start=start, stop=stop, **kw), reads=reads, writes=writes)

        cload = [
            (ident_f[:], identf[:, :]), (hgm[:], hgmask[:, :, :, :]), (rowm[:], rowmask[:, :]), (scm[:], scanmask[:, :]), (dtb[:], dtab[:, :, :]),
            (gpre_s[:], gpre[:, :, :]), (lb_s[:], lbl[:, :, :, :]), (hgain_s[:], hgain[:, :]), (esink[:], sink[:, :]),
            (pmask[:], premask[:, :]),
        ]
        for i, (o, s) in enumerate(cload):
            dma("sp", o, s, "const", writes=[("c", i)])
        CK = [("c", i) for i in range(len(cload))]

        late_casts = []

        def cast(dst, src, grp):
            if grp.startswith("g1"):
                dma("pool", dst, src, "cast_" + grp, writes=[("wb", grp)])
            else:
                late_casts.append(lambda: dma("pool", dst, src, "cast_" + grp, writes=[("wb", grp)]))

        def emit_casts(n):
            for _ in range(min(n, len(late_casts))):
                late_casts.pop(0)()

        def cast_cols(c0, c1, grp):
            for c in range(c0, c1, 512):
                cast(wb_in[:, c:c + 512], w_in[:, c:c + 512], grp)

        cast_cols(C_I, C_I + 1024, "g1")
        cast_cols(C_FF, C_FF + 2048, "g1")
        cast_cols(C_Q, C_Q + 1024, "g2")
        cast_cols(C_G, C_G + 1024, "g2")
        cast_cols(C_AQ, C_AQ + 1024, "g3")
        cast_cols(C_AK, C_AK + 512, "g1k")
        for g in range(4):
            for dup in range(2):
                cast(wb_kd[:, g * 128 + dup * 64: g * 128 + dup * 64 + 64], w_in[:, C_AK + g * 64: C_AK + g * 64 + 64], "g1k")
        cast_cols(C_GA, C_GA + 4096, "g4")
        for r in range(0, 1024, 512):
            cast(wb_p[r:r + 512, :], w_phg[r:r + 512, :], "g4")
            cast(wb_p[1024 + r:1024 + r + 512, :], w_patt[r:r + 512, :], "g4")
        for r in range(0, D, 512):
            cast(wb_out[r:r + 512, :], w_out[r:r + 512, :], "g5")
        for c in range(0, DFF, 512):
            cast(wb_ff1[:, c:c + 512], w_ff1[:, c:c + 512], "g6")
        for r in range(0, DFF, 512):
            cast(wb_ff2[r:r + 512, :], w_ff2[r:r + 512, :], "g7")

        copy_op("dve", ident_b[:], ident_f[:], CK, ["identb"])
        P.emit("dve", lambda e: e.memset(ones_b[:], 1.0), writes=["ones"])
        emit_m("dve", "tensor_tensor", dict(out=oml[:], in0=lb_s[:, 1], in1=lb_s[:, 0], op=ALU.subtract), reads=CK, writes=["oml"])
        act(oml[:], oml[:], AF.Exp, ["oml"], ["oml"])
        act(oml[:], oml[:], AF.Ln, ["oml"], ["oml"], bias=1.0)
        act(oml[:], oml[:], AF.Exp, ["oml"], ["oml"], scale=-1.0)
        emit_m("dve", "tensor_scalar", dict(out=oml[:], in0=oml[:], scalar1=-1.0, scalar2=1.0, op0=ALU.mult, op1=ALU.add), reads=["oml"], writes=["oml"])
        act(esink[:], esink[:], AF.Exp, CK, ["esink"])
        P.emit("dve", lambda e: e.memset(Sst[:], 0.0), writes=["S"])
        P.emit("pool", lambda e: e.memset(kmT[:], 0.0), writes=["kmT"])
        P.emit("pool", lambda e: e.memset(vm[:], 0.0), writes=["vm"])
        emit_m("dve", "tensor_scalar", dict(out=rowm8[:], in0=rowm[:], scalar1=0.125, scalar2=None, op0=ALU.mult), reads=CK, writes=["rowm8"])

        items = []

        def piece_in(c0, grp, kd=False):
            src = (wb_kd if kd else wb_in)[:, c0:c0 + 512].rearrange("(kc p) n -> p kc n", p=128)
            return {"src": src, "wkey": ("wb", grp), "nk": KC}

        def piece_rows(wb, r0, c0, grp, nk=KC):
            return {"src": wb[r0:r0 + nk * 128, c0:c0 + 512].rearrange("(kc p) n -> p kc n", p=128), "wkey": ("wb", grp), "nk": nk}

        def add(piece, fn):
            items.append((piece, fn))

        def add_barrier():
            items.append(("barrier", None))

        def set_c1(col):
            def fn(slot):
                if col is None:
                    copy_op("dve", c1t[:], oml[:], ["oml"], ["c1"])
                else:
                    for d in range(2):
                        emit_m("dve", "tensor_scalar", dict(out=c1t[:, d], in0=oml[:, d], scalar1=pmask[:, col + d:col + d + 1], scalar2=None, op0=ALU.mult), reads=["oml"] + CK, writes=["c1"])
                emit_m("dve", "tensor_scalar", dict(out=nc1t[:], in0=c1t[:], scalar1=-1.0, scalar2=None, op0=ALU.mult), reads=["c1"], writes=["c1"])
            add(None, fn)

        def prep(src_rows, nsub, col0, gsel, dstT, dkey, rows_key=None, xr=None, xs=None):
            xr = xr or xrow
            xsk = "xsf2" if xs is not None else None
            xs = xs or xsf
            xkey = (lambda b: ("xsf2", b)) if xsk else (lambda b: ("x1", 2 + b))

            def fn(slot):
                if rows_key is None:
                    dma("act", xr[0][:], src_rows[0:128, :], "xrow0", writes=[("x1", 0)])
                for s in range(nsub):
                    b = s % 2
                    if rows_key is None:
                        if s + 1 < nsub:
                            dma("act", xr[1 - b][:], src_rows[(s + 1) * 128:(s + 2) * 128, :], f"xrow{1 - b}", writes=[("x1", 1 - b)])
                        src = xr[b][:]
                        rk = [("x1", b)]
                    else:
                        src = src_rows[:, s, :]
                        rk = [(rows_key, s)]
                    sc = stat[:, 2 * b:2 * b + 1]
                    act(xs[b][:], src, AF.Square, rk, [xkey(b), ("st", b)], accum_out=sc)
                    act(sc, sc, AF.Ln, [("st", b)], [("st", b)], scale=1.0 / D, bias=EPS)
                    act(sc, sc, AF.Exp, [("st", b)], [("st", b)], scale=-0.5)
                    emit_m("dve", "tensor_scalar", dict(out=xs[b][:], in0=src, scalar1=sc, scalar2=None, op0=ALU.mult), reads=rk + [("st", b)], writes=[xkey(b)])
                    for q4 in range(4):
                        pb = ps[q4]
                        for j in range(4):
                            kc = q4 * 4 + j
                            emit_m("pe", "transpose", dict(out=pb[:, j * 128:(j + 1) * 128], in_=xs[b][:, kc * 128:(kc + 1) * 128], identity=ident_f[:]),
                                   reads=[xkey(b)] + CK, writes=[("ps", q4)])
                        for j in range(4):
                            kc = q4 * 4 + j
                            copy_op(evac_eng(), dstT[:, kc, col0 + s * 128: col0 + (s + 1) * 128], pb[:, j * 128:(j + 1) * 128],
                                    [("ps", q4)] + CK, [dkey], scale=gpre_s[:, gsel, kc:kc + 1])
            add(None, fn)

        cnt8 = {"b": 0}

        xTL = [carve(48, [128, KC, TT], BF16), carve(64, [128, KC, TT], BF16)]
        xrL = [carve(40, [128, D], F32), carve(80, [128, D], F32)]
        junkL = carve(24, [128, D], BF16)

        def prep_light(src_rows, dstT, dkey, nxt_rows=None):
            def issue(rows, s):
                dma("act", xrL[s % 2][:], rows[s * 128:(s + 1) * 128, :], f"xrL{s % 2}", writes=[("xrL", s % 2)])

            def front(s):
                b = s % 2
                src = xrL[b][:]
                rk = [("xrL", b)]
                sc = stat[:, 8 + b:9 + b]
                act(junkL[:], src, AF.Square, rk, ["junkL", ("stL", b)], accum_out=sc)
                act(sc, sc, AF.Ln, [("stL", b)], [("stL", b)], scale=1.0 / D, bias=EPS)
                act(sc, sc, AF.Exp, [("stL", b)], [("stL", b)], scale=-0.5)
                emit_m("dve", "tensor_scalar", dict(out=src, in0=src, scalar1=sc, scalar2=None, op0=ALU.mult), reads=rk + [("stL", b)], writes=rk)

            def back(s):
                b = s % 2
                rk = [("xrL", b)]
                for q4 in range(4):
                    bkk = (3, 6)[q4 % 2]
                    pb = ps[bkk]
                    for j in range(4):
                        kc = q4 * 4 + j
                        P.emit("pe", (lambda pb=pb, j=j, kc=kc, b=b: (lambda e: e.transpose(out=pb[:, j * 128:(j + 1) * 128], in_=xrL[b][:, kc * 128:(kc + 1) * 128], identity=ident_f[:])))(),
                               reads=rk + CK, writes=[("ps", bkk)])
                    for j in range(4):
                        kc = q4 * 4 + j
                        copy_op(evac_eng(), dstT[:, kc, s * 128:(s + 1) * 128], pb[:, j * 128:(j + 1) * 128], [("ps", bkk)] + CK, [dkey], scale=gpre_s[:, 0, kc:kc + 1])

            def p0():
                front(0)

            def p1():
                back(0)
                issue(src_rows, 2)
                front(1)

            def p2():
                back(1)
                issue(src_rows, 3)
                front(2)

            def p3():
                back(2)
                front(3)

            def p4():
                back(3)
                if nxt_rows is not None:
                    issue(nxt_rows, 0)
                    issue(nxt_rows, 1)
            return [p0, p1, p2, p3, p4], (lambda: (issue(src_rows, 0), issue(src_rows, 1)))

        def proj_fm(piece, xTv, ncols_tok, dst_fn, xkey, ntile=4, nb8=4):
            def fn(slot):
                for j in range(ntile):
                    cnt8["b"] = (cnt8["b"] + 1) % nb8
                    bk = cnt8["b"]
                    for kc in range(KC):
                        mm(ps[bk][:, :ncols_tok], wsl[slot][:, kc, j * 128:(j + 1) * 128], xTv(kc), kc == 0, kc == KC - 1,
                           [("w", slot), xkey], [("ps", bk)])
                    dst_fn(j, ps[bk], ("ps", bk))
            add(piece, fn)

        def proj_tm(piece, xTv_sub, nsub, dst_fn, xkey, ncol=512, nk=KC, msize=128):
            def fn(slot):
                for s in range(nsub):
                    bk = 4 + (s % 3)
                    for kc in range(nk):
                        mm(ps[bk][:msize, :ncol], xTv_sub(kc, s), wsl[slot][:, kc, :ncol], kc == 0, kc == nk - 1,
                           [("w", slot), xkey], [("ps", bk)])
                    dst_fn(s, ps[bk], ("ps", bk))
            add(piece, fn)

        vpar = [carve(0, [128, 4, 1024], BF16), carve(8, [128, 4, 1024], BF16)]
        kinvT = carve(16, [128, 8, TT], BF16)
        qdecT = carve(24, [128, 8, TT], BF16)
        ktok = carve(32, [128, 8, TT], BF16)
        gsb = carve(40, [128, 8, TT], BF16)
        o_sb = carve(88, [128, 8, TT], F32)
        NTS = 3
        Tset = [[carve(104 + 10 * j + 2 * i, [128, TT], F32) for i in range(5)] for j in range(NTS)]
        scb = [carve(134 + i, [128, 8, 64], BF16) for i in range(2)]
        onesf = carve(134, [128, TT], F32)
        Sbf = [carve(136 + 2 * i, [128, 8, 128], BF16) for i in range(2)]
        decs = carve(140, [128, 8, 8], F32)
        cnt = {"bank": 0, "ts": 0}

        def nbank(m=4):
            cnt["bank"] = (cnt["bank"] + 1) % m
            return cnt["bank"]

        def sigm_neg(dst, src, rk, wk, n, sgn=1.0):
            act(dst[:, :n], src, AF.Exp, rk, [wk], scale=sgn)
            act(dst[:, :n], dst[:, :n], AF.Ln, [wk], [wk], bias=1.0)
            act(dst[:, :n], dst[:, :n], AF.Exp, [wk], [wk], scale=-1.0)

        deferred = []

        def defer(fn):
            deferred.append(fn)

        def flush():
            while deferred:
                deferred.pop(0)()

        def proj_head(piece, j, xv, ntok, consume, xkey="xT", nbk=4, pre=None):
            def fn(slot):
                bk = nbank(nbk)
                for kc in range(KC):
                    mm(ps[bk][:, :ntok], wsl[slot][:, kc, j * 128:(j + 1) * 128], xv(kc), kc == 0, kc == KC - 1, [("w", slot), xkey], [("ps", bk)])
                flush()
                if pre is not None:
                    pre()
                consume(ps[bk], ("ps", bk))
            add(piece, fn)

        def hg_v(xvs, nsub, msz, par, xkey="xT"):
            for half in range(2):
                def dst(s, pb, pk, half=half):
                    if par:
                        copy_op("act", vpar[0][:, s, half * 512:(half + 1) * 512], pb[:, :], [pk] + CK, ["vpar"], scale=rowm[:, 0:1])
                        copy_op("dve", vpar[1][:, s, half * 512:(half + 1) * 512], pb[:, :], [pk] + CK, ["vpar"], scale=rowm[:, 1:2])
                    else:
                        copy_op(evac_eng(), vpar[0][:msz, s, half * 512:(half + 1) * 512], pb[:msz, :], [pk], ["vpar"])
                proj_tm(piece_in(C_I + half * 512, "g1"), xvs, nsub, dst, xkey, msize=msz)

        def hg_pre(xcol0, ntok, dirs, xb=None, inter=()):
            nsub = (ntok + 127) // 128
            msz = min(128, ntok)
            xt_, xkey = (xT, "xT") if xb is None else xb
            nbk = 4 if xb is None else 3
            inter = list(inter)
            nhead = [0]
            xv = lambda kc: xt_[:, kc, xcol0:xcol0 + ntok]
            xvs = lambda kc, s: xt_[:, kc, xcol0 + s * 128: xcol0 + s * 128 + msz]
            add(None, lambda slot: P.emit("pool", lambda e: e.memset(onesf[:], 1.0), writes=["onesf"]))
            if inter:
                add(None, lambda slot, f=inter.pop(0): f())
            hg_v(xvs, nsub, msz, False, xkey=xkey)
            fcols = {0: C_FF, 1: C_FB}
            for hp in range(2):
                for dr in dirs:
                    pf = piece_in(fcols[dr] + hp * 512, "g1")
                    for j in range(4):
                        h = hp * 4 + j

                        def consume(pb, pk, h=h, dr=dr):
                            r = cnt["ts"] = (cnt["ts"] + 1) % NTS
                            T = Tset[r]
                            tk = ("T", r)
                            n = ntok
                            sigm_neg(T[0], pb[:, :n], [pk], tk, n)
                            act(T[1][:, :n], T[0][:, :n], AF.Ln, [tk, "c1"], [tk], scale=nc1t[:, dr, h:h + 1], bias=1.0)
                            emit_m("dve", "tensor_tensor_scan", dict(out=T[2][:, :n], data0=onesf[:, :n], data1=T[1][:, :n], initial=0.0, op0=ALU.mult, op1=ALU.add),
                                   reads=[tk, "onesf"], writes=[tk])
                            act(T[4][:, 0:1], T[2][:, n - 1:n], AF.Exp, [tk], [tk])
                            if dr == 0:
                                act(T[3][:, :n], T[2][:, :n], AF.Exp, [tk], [tk], scale=-1.0, bias=T[2][:, n - 1:n])
                            else:
                                emit_m("dve", "tensor_tensor", dict(out=T[2][:, :n], in0=T[2][:, :n], in1=T[1][:, :n], op=ALU.subtract), reads=[tk], writes=[tk])
                                act(T[3][:, :n], T[2][:, :n], AF.Exp, [tk], [tk])
                            emit_m("dve", "scalar_tensor_tensor", dict(out=kinvT[:, h, :n], in0=T[0][:, :n], scalar=c1t[:, dr, h:h + 1], in1=T[3][:, :n], op0=ALU.mult, op1=ALU.mult),
                                   reads=[tk, "c1"], writes=[("kinvT", h)])
                            def tail(h=h, dr=dr, T=T, tk=tk):
                                for s in range(nsub):
                                    P.emit("pe", (lambda s=s: (lambda e: e.transpose(out=psb[:msz, s * 128:(s + 1) * 128], in_=kinvT[:, h, s * 128:s * 128 + msz], identity=ident_b[:])))(),
                                           reads=[("kinvT", h), "identb"], writes=[("ps", 7)])
                                copy_op(evac_eng(), ktok[:msz, h, :nsub * 128], psb[:msz, :nsub * 128], [("ps", 7)], [("ktok", h)])
                                ib = 4 + (h // 4)
                                for s in range(nsub):
                                    mm(ps[ib][:, (h % 4) * 128:(h % 4 + 1) * 128], ktok[:msz, h, s * 128:(s + 1) * 128], vpar[0][:msz, s, h * 128:(h + 1) * 128], s == 0, s == nsub - 1,
                                       [("ktok", h), "vpar"], [("ps", ib)])
                                Sv = Sst[:, dr, h, :]
                                emit_m("dve", "scalar_tensor_tensor", dict(out=Sv, in0=Sv, scalar=T[4][:, 0:1], in1=ps[ib][:, (h % 4) * 128:(h % 4 + 1) * 128], op0=ALU.mult, op1=ALU.add),
                                       reads=[("S", dr, h), tk, ("ps", ib)], writes=[("S", dr, h)])
                            defer(tail)
                        pre_ = None
                        if inter and nhead[0] > 0 and nhead[0] % 4 == 0:
                            pre_ = inter.pop(0)
                        proj_head(pf, j, xv, ntok, consume, xkey=xkey, nbk=nbk, pre=pre_)
                        nhead[0] += 1
                        if xb is not None and nhead[0] in (2, 7, 12):
                            add(None, lambda slot: emit_casts(-(-(-(-63 // max(1, NPRE - 2))) // 3)))
            add(None, lambda slot: flush())
            for f in inter:
                add(None, lambda slot, f=f: f())

        def hg_main(mode, tok0, xcol0):
            dr = 1 if mode == "bwd" else 0
            ntok = TT
            xv = lambda kc: xT[:, kc, xcol0:xcol0 + ntok]
            xvs = lambda kc, s: xT[:, kc, xcol0 + s * 128: xcol0 + s * 128 + 128]
            if mode == "fwd":
                add(None, lambda slot: dma("pool", o_sb[:], ob_d[:, :, tok0:tok0 + TT].rearrange("h p t -> p h t"), "osb", reads=[("ob_d", tok0)], writes=["osb"]))
            hg_v(xvs, 4, 128, True)
            fcol = C_FB if dr else C_FF
            for hp in range(2):
                pf = piece_in(fcol + hp * 512, "g1")
                pq = piece_in(C_Q + hp * 512, "g2")
                pg = piece_in(C_G + hp * 512, "g2") if mode == "fwd" else None
                hold = {}

                def mk(h, hold=hold):
                        def cf(pb, pk, h=h, hold=hold):
                            r = cnt["ts"] = (cnt["ts"] + 1) % NTS
                            hold[h] = r
                            T = Tset[r]
                            tk = ("T", r)
                            sigm_neg(T[0], pb[:, :], [pk], tk, TT)
                            act(T[1][:], T[0][:], AF.Ln, [tk, "c1"], [tk], scale=nc1t[:, dr, h:h + 1], bias=1.0)
                            emit_m("dve", "tensor_tensor_scan", dict(out=T[2][:], data0=scm[:], data1=T[1][:], initial=0.0, op0=ALU.mult, op1=ALU.add), reads=[tk] + CK, writes=[tk])
                            act(decs[:, h, :], T[2][:, 63::64], AF.Exp, [tk], [("dec", h)])
                            if dr == 0:
                                act(T[3][:], T[2][:], AF.Exp, [tk], [tk])
                                act(T[4][:], T[2][:], AF.Exp, [tk], [tk], scale=-1.0)
                            else:
                                emit_m("dve", "tensor_tensor", dict(out=T[2][:], in0=T[2][:], in1=T[1][:], op=ALU.subtract), reads=[tk], writes=[tk])
                                act(T[4][:], T[2][:], AF.Exp, [tk], [tk])
                                act(T[3][:], T[2][:], AF.Exp, [tk], [tk], scale=-1.0)
                            emit_m("dve", "scalar_tensor_tensor", dict(out=kinvT[:, h, :], in0=T[0][:], scalar=c1t[:, dr, h:h + 1], in1=T[4][:], op0=ALU.mult, op1=ALU.mult),
                                   reads=[tk, "c1"], writes=[("kinvT", h)])
                            def tail(h=h):
                                for s in range(4):
                                    P.emit("pe", (lambda s=s: (lambda e: e.transpose(out=psb[:, s * 128:(s + 1) * 128], in_=kinvT[:, h, s * 128:(s + 1) * 128], identity=ident_b[:])))(),
                                           reads=[("kinvT", h), "identb"], writes=[("ps", 7)])
                                copy_op(evac_eng(), ktok[:, h, :], psb[:, :512], [("ps", 7)], [("ktok", h)])
                            defer(tail)
                        def cq(pb, pk, h=h, hold=hold):
                            T = Tset[hold[h]]
                            tk = ("T", hold[h])
                            sigm_neg(T[1], pb[:, :], [pk], tk, TT, sgn=-1.0)
                            emit_m("dve", "tensor_tensor", dict(out=T[1][:], in0=T[1][:], in1=pb[:, :], op=ALU.mult), reads=[tk, pk], writes=[tk])
                            emit_m("dve", "scalar_tensor_tensor", dict(out=qdecT[:, h, :], in0=T[1][:], scalar=float(128 ** -0.5), in1=T[3][:], op0=ALU.mult, op1=ALU.mult),
                                   reads=[tk], writes=[("qdecT", h)])

                        return cf, cq

                def mkg(h):
                    def cg(pb, pk, h=h):
                        r = cnt["ts"] = (cnt["ts"] + 1) % NTS
                        T = Tset[r]
                        tk = ("T", r)
                        sigm_neg(T[0], pb[:, :], [pk], tk, TT, sgn=-1.0)
                        emit_m("dve", "tensor_tensor", dict(out=gsb[:, h, :], in0=T[0][:], in1=pb[:, :], op=ALU.mult), reads=[tk, pk], writes=[("gs", h)])

                    return cg
                fns = {hp * 4 + j: mk(hp * 4 + j) for j in range(4)}
                for pr2 in range(2):
                    for j in (2 * pr2, 2 * pr2 + 1):
                        proj_head(pf, j, xv, ntok, fns[hp * 4 + j][0])
                    for j in (2 * pr2, 2 * pr2 + 1):
                        proj_head(pq, j, xv, ntok, fns[hp * 4 + j][1])
                if mode == "fwd":
                    for j in range(4):
                        proj_head(pg, j, xv, ntok, mkg(hp * 4 + j))

            add(None, lambda slot: flush())

            def scan(slot):
                order = list(range(8)) if dr == 0 else list(range(7, -1, -1))

                def sc_mask(i):
                    c = order[i]
                    pr, par = c // 2, c % 2
                    bk = i % 2
                    for h in range(8):
                        mm(ps[bk][:, h * 64:(h + 1) * 64], kinvT[:, h, pr * 128:(pr + 1) * 128], qdecT[:, h, c * 64:(c + 1) * 64], True, True,
                           [("kinvT", h), ("qdecT", h)], [("ps", bk)])
                    for h in range(8):
                        emit_m("dve", "tensor_tensor", dict(out=scb[bk][:, h, :], in0=ps[bk][:, h * 64:(h + 1) * 64], in1=hgm[:, dr, par, :], op=ALU.mult),
                               reads=[("ps", bk)] + CK, writes=[("scb", bk)])

                def sbf(i, h):
                    c = order[i]
                    Sv = Sst[:, dr, h, :]
                    if dr == 0:
                        sc_ = None if i == 0 else decs[:, h, order[i - 1]:order[i - 1] + 1]
                    else:
                        sc_ = decs[:, h, c:c + 1]
                    copy_op("act", Sbf[i % 2][:, h, :], Sv, [("S", dr, h), ("dec", h)], [("Sbf", i % 2, h)], scale=sc_)

                for h in range(8):
                    sbf(0, h)
                sc_mask(0)
                for i in range(8):
                    c = order[i]
                    pr, par = c // 2, c % 2
                    if i + 1 < 8:
                        sc_mask(i + 1)
                    ob = 2 + (i % 2)
                    for h in range(8):
                        oo = ps[ob][:, h * 64:(h + 1) * 64]
                        mm(oo, vpar[par][:, pr, h * 128:(h + 1) * 128], scb[i % 2][:, h, :], True, False, ["vpar", ("scb", i % 2)], [("ps", ob)])
                        mm(oo, Sbf[i % 2][:, h, :], qdecT[:, h, c * 64:(c + 1) * 64], False, True, [("Sbf", i % 2, h), ("qdecT", h)], [("ps", ob)])
                    osl = o_sb[:, :, c * 64:(c + 1) * 64]
                    pso = ps[ob][:, :].rearrange("p (h t) -> p h t", h=8)
                    if mode == "fwd":
                        emit_m("dve", "tensor_tensor", dict(out=osl, in0=osl, in1=pso, op=ALU.add), reads=[("ps", ob), "osb"], writes=["osb"])
                    else:
                        copy_op("act", osl, pso, [("ps", ob)], ["osb"])
                    for h in range(8):
                        ib = 4 + (h // 4)
                        mm(ps[ib][:, (h % 4) * 128:(h % 4 + 1) * 128], ktok[:, h, pr * 128:(pr + 1) * 128], vpar[par][:, pr, h * 128:(h + 1) * 128], True, True,
                           [("ktok", h), "vpar"], [("ps", ib)])
                    for h in range(8):
                        Sv = Sst[:, dr, h, :]
                        if dr == 0:
                            sc_ = 1.0 if i == 0 else decs[:, h, order[i - 1]:order[i - 1] + 1]
                        else:
                            sc_ = decs[:, h, c:c + 1]
                        emit_m("dve", "scalar_tensor_tensor", dict(out=Sv, in0=Sv, scalar=sc_, in1=ps[4 + (h // 4)][:, (h % 4) * 128:(h % 4 + 1) * 128], op0=ALU.mult, op1=ALU.add),
                               reads=[("S", dr, h), ("ps", 4 + (h // 4)), ("dec", h)], writes=[("S", dr, h)])
                    if i + 1 < 8:
                        for h in range(8):
                            sbf(i + 1, h)
                if dr == 0:
                    for h in range(8):
                        Sv = Sst[:, dr, h, :]
                        emit_m("dve", "tensor_scalar", dict(out=Sv, in0=Sv, scalar1=decs[:, h, 7:8], scalar2=None, op0=ALU.mult), reads=[("S", dr, h), ("dec", h)], writes=[("S", dr, h)])
            add(None, scan)

            if mode == "bwd":
                add(None, lambda slot: dma("pool", ob_d[:, :, tok0:tok0 + TT].rearrange("h p t -> p h t"), o_sb[:], "osb", reads=["osb"], writes=[("ob_d", tok0)]))
            else:
                def fin(slot):
                    for h in range(8):
                        r = cnt["ts"] = (cnt["ts"] + 1) % NTS
                        T = Tset[r]
                        tk = ("T", r)
                        sqb = scb[0] if h % 2 == 0 else scb[1]
                        sq = kinvT[:, h, :]
                        act(sq, o_sb[:, h, :], AF.Square, ["osb"], [("kinvT", h)])
                        mm(ps[6][:, :], ones_b[:], sq, True, True, ["ones", ("kinvT", h)], [("ps", 6)])
                        act(T[1][:], ps[6][:, :], AF.Ln, [("ps", 6)], [tk], scale=1.0 / 128, bias=EPS)
                        act(T[1][:], T[1][:], AF.Exp, [tk], [tk], scale=-0.5)
                        emit_m("dve", "scalar_tensor_tensor", dict(out=T[0][:], in0=o_sb[:, h, :], scalar=hgain_s[:, h:h + 1], in1=T[1][:], op0=ALU.mult, op1=ALU.mult), reads=[tk, "osb"] + CK, writes=[tk])
                        emit_m("dve", "tensor_tensor", dict(out=oT_hg[:, h, :], in0=T[0][:], in1=gsb[:, h, :], op=ALU.mult), reads=[tk, ("gs", h)], writes=["oT_hg"])
                add(None, fin)

        def hg_tile(mode, tok0, xcol0, ntok=TT):
            if mode == "meta":
                hg_pre(xcol0, ntok, [0])
            elif mode == "pre":
                hg_pre(xcol0, ntok, [0, 1])
            else:
                hg_main(mode, tok0, xcol0)

        def attn_kv(xcol0, ntok, kdst, vdst, vkey, kkey):
            def kd(j, pb, pk, n0, n):
                copy_op(evac_eng(), kdst[:, j, n0:n0 + n], pb[:, :n], [pk], [kkey])
            for n0 in range(0, ntok, 512):
                n = min(512, ntok - n0)
                proj_fm(piece_in(0, "g1k", kd=True), lambda kc, n0=n0, n=n: xT[:, kc, xcol0 + n0: xcol0 + n0 + n], n,
                        lambda j, pb, pk, n0=n0, n=n: kd(j, pb, pk, n0, n), "xT")
            nsub = (ntok + 127) // 128
            msz = min(128, ntok)

            def vd(s, pb, pk):
                src = pb[:msz, 256:512].rearrange("p (g d) -> p g d", g=4)
                if ntok == 16:
                    d0, d1 = vdst[:msz, :, 0, 0:64], vdst[:msz, :, 1, 64:128]
                else:
                    d0, d1 = vdst[:msz, s, :, 0, 0:64], vdst[:msz, s, :, 1, 64:128]
                copy_op("act", d0, src, [pk], [vkey])
                copy_op("dve", d1, src, [pk], [vkey])
            proj_tm(piece_in(C_AV - 256, "g1k"), lambda kc, s: xT[:, kc, xcol0 + s * 128: xcol0 + s * 128 + msz], nsub, vd, "xT", msize=msz)

        slopes = [float(2.0 ** (-8.0 * (h + 1) / 16.0)) for h in range(16)]

        def attn_tile(ti):
            for half in range(2):
                def qd(j, pb, pk, half=half):
                    copy_op("act", qTa[:, 0, half * 4 + j, :], pb[:, :], [pk, "rowm8"], ["qTa"], scale=rowm8[:, 0:1])
                    copy_op("dve", qTa[:, 1, half * 4 + j, :], pb[:, :], [pk, "rowm8"], ["qTa"], scale=rowm8[:, 1:2])
                proj_fm(piece_in(C_AQ + half * 512, "g3"), lambda kc: xT[:, kc, 128:128 + TT], TT, qd, "xT")
            add(None, lambda slot: P.emit("pool", lambda e: e.memset(vat[:], 0.0), writes=["vat"]))
            add(None, lambda slot: dma("pool", onesc[:], onesc_d[:, :, :], "onesc", writes=["onesc"]))
            attn_kv(0, 768, kTd, vat, "vat", "kTd")

            def blocks(slot):
                its = [(qb, g, hl) for qb in range(4) for g in range(4) for hl in range(2)]

                def A(n):
                    qb, g, hl = its[n]
                    p2 = (n % 2) * 2
                    rq = qTa[:, hl, 2 * g:2 * g + 2, qb * 128:(qb + 1) * 128]
                    for kb in range(3):
                        bank = p2 + (kb // 2)
                        co = (kb % 2) * 256
                        mm(ps[bank][:, co:co + 256].rearrange("p (a b) -> p a b", a=2), kTd[:, g, (qb + kb) * 128:(qb + kb + 1) * 128], rq, True, True, ["kTd", "qTa"], [("ps", bank)])
                    mm(ps[p2 + 1][:, 256:512].rearrange("p (a b) -> p a b", a=2), kmT[:, g, :], rq, True, True, ["kmT", "qTa"], [("ps", p2 + 1)])

                def B(n):
                    qb, g, hl = its[n]
                    p2 = (n % 2) * 2
                    first = (ti == 0 and qb == 0)
                    last = (ti == NT - 1 and qb == 3)
                    pi = (qb * 4 + g) % 2
                    pt = PT[pi]
                    sc = scs[n % 2]
                    sk = ("scs", n % 2)
                    for kb in range(3):
                        dsel = kb
                        if kb == 0 and first:
                            dsel = 3
                        if kb == 2 and last:
                            dsel = 4
                        bank = p2 + (kb // 2)
                        co = (kb % 2) * 256
                        for c in range(2):
                            head = 4 * g + 2 * c + hl
                            emit_m("dve", "scalar_tensor_tensor", dict(out=sc[:, kb, c * 128:(c + 1) * 128], in0=dtb[:, dsel, :], scalar=slopes[head],
                                                                        in1=ps[bank][:, co + c * 128: co + (c + 1) * 128], op0=ALU.mult, op1=ALU.add),
                                   reads=[("ps", bank)] + CK, writes=[sk])
                    act(pt[:, 0:3, hl * 256:(hl + 1) * 256], sc[:, :, :], AF.Exp, [sk], [("PT", pi, hl)])
                    act(pt[:, 3, hl * 256:(hl + 1) * 256], ps[p2 + 1][:, 256:512], AF.Exp, [("ps", p2 + 1)], [("PT", pi, hl)])

                def C(n):
                    qb, g, hl = its[n]
                    pi = (qb * 4 + g) % 2
                    pt = PT[pi]
                    ptk = ("PT", pi, hl)
                    nb = 4 + 2 * pi
                    for kb in range(4):
                        if kb < 3:
                            lv = vat[:, qb + kb, g, hl, :]
                        else:
                            lv = vm[:, g, hl, :]
                        mm(ps[nb][:, 0:256], lv, pt[:, kb, hl * 256:(hl + 1) * 256], hl == 0 and kb == 0, hl == 1 and kb == 3, ["vat", "vm", ptk], [("ps", nb)])
                    for kb in range(4):
                        lo = onesc[:, (0 if kb < 3 else 2) + hl, :]
                        mm(ps[nb + 1][:, 0:256], lo, pt[:, kb, hl * 256:(hl + 1) * 256], hl == 0 and kb == 0, hl == 1 and kb == 3, ["onesc", ptk], [("ps", nb + 1)])
                    if hl == 1:
                        rc = rec[pi]
                        rk = ("rec", pi)
                        for c in range(2):
                            ch = 2 * g + c
                            act(rc[:, c * 128:(c + 1) * 128], ps[nb + 1][:, c * 128:(c + 1) * 128], AF.Ln, [("ps", nb + 1), "esink"], [rk], bias=esink[:, ch:ch + 1])
                        act(rc[:], rc[:], AF.Exp, [rk], [rk], scale=-1.0)
                        for c in range(2):
                            ch = 2 * g + c
                            emit_m("dve", "tensor_tensor", dict(out=oT_att[:, ch, qb * 128:(qb + 1) * 128], in0=ps[nb][:, c * 128:(c + 1) * 128], in1=rc[:, c * 128:(c + 1) * 128], op=ALU.mult),
                                   reads=[("ps", nb), rk], writes=["oT_att"])

                N = len(its)
                A(0)
                B(0)
                for n in range(N):
                    if n + 1 < N:
                        A(n + 1)
                        B(n + 1)
                    C(n)
            add(None, blocks)

        def merge_tile():
            xv = lambda kc: xT[:, kc, 128:128 + TT]
            for ng in range(4):
                def fa(slot, ng=ng):
                    pass
                st = {}

                def ga_item(slot, ng=ng, st=st):
                    st["ga"] = slot
                pa_ = piece_in(C_GA + ng * 512, "g4")
                add(pa_, ga_item)

                def gb_item(slot, ng=ng, st=st):
                    st["gb"] = slot
                pb_ = piece_in(C_GB + ng * 512, "g4")
                add(pb_, gb_item)

                def p_item(slot, ng=ng, st=st):
                    sa, sbb, sp_ = st["ga"], st["gb"], slot
                    for j in range(4):
                        nch = ng * 4 + j
                        cj = slice(j * 128, (j + 1) * 128)
                        b0 = 4 * (j % 2)
                        for kc in range(KC):
                            mm(ps[b0][:, :], wsl[sa][:, kc, cj], xv(kc), kc == 0, kc == KC - 1, [("w", sa), "xT"], [("ps", b0)])
                        for kc in range(KC):
                            mm(ps[b0 + 1][:, :], wsl[sbb][:, kc, cj], xv(kc), kc == 0, kc == KC - 1, [("w", sbb), "xT"], [("ps", b0 + 1)])
                        for kc in range(8):
                            mm(ps[b0 + 2][:, :], wsl[sp_][:, kc, cj], oT_hg[:, kc, :], kc == 0, kc == 7, [("w", sp_), "oT_hg"], [("ps", b0 + 2)])
                        for kc in range(8):
                            mm(ps[b0 + 3][:, :], wsl[sp_][:, 8 + kc, cj], oT_att[:, kc, :], kc == 0, kc == 7, [("w", sp_), "oT_att"], [("ps", b0 + 3)])
                        m0 = mg[(j % 2) * 4: (j % 2) * 4 + 4]
                        mk = ("mg", j % 2)
                        for gi in range(2):
                            act(m0[gi][:], ps[b0 + gi][:, :], AF.Exp, [("ps", b0 + gi)], [mk], scale=-1.0)
                            act(m0[gi][:], m0[gi][:], AF.Ln, [mk], [mk], bias=1.0)
                            act(m0[gi][:], m0[gi][:], AF.Exp, [mk], [mk], scale=-1.0)
                        emit_m("dve", "tensor_tensor", dict(out=m0[2][:], in0=m0[0][:], in1=ps[b0 + 2][:, :], op=ALU.mult), reads=[mk, ("ps", b0 + 2)], writes=[mk])
                        emit_m("dve", "tensor_tensor", dict(out=m0[3][:], in0=m0[1][:], in1=ps[b0 + 3][:, :], op=ALU.mult), reads=[mk, ("ps", b0 + 3)], writes=[mk])
                        emit_m("pool", "tensor_tensor", dict(out=mergedT[:, nch, :], in0=m0[2][:], in1=m0[3][:], op=ALU.add), reads=[mk], writes=["mergedT"])
                add({"src": wb_p[:, ng * 512:(ng + 1) * 512].rearrange("(kc p) n -> p kc n", p=128), "wkey": ("wb", "g4"), "nk": KC}, p_item)
                add(pa_, lambda slot: None)
                add(pb_, lambda slot: None)

        def tm_proj_norm(srcT, skey, wb, grp, nkp, zb, zkey, gsel, resid_load, out_store, ti):
            add(None, lambda slot: dma("pool", gpo[:], gpost[gsel:gsel + 1, :].partition_broadcast(128), "gpo", writes=["gpo"]))
            for ng in range(4):
                for kp in range(nkp):
                    def item(slot, ng=ng, kp=kp):
                        for s in range(4):
                            bk = s + (4 * (ng % 2))
                            for kc in range(KC):
                                kk = kp * KC + kc
                                mm(ps[bk][:, :], srcT[:, kk, s * 128:(s + 1) * 128], wsl[slot][:, kc, :], kk == 0, kk == nkp * KC - 1, [("w", slot), skey], [("ps", bk)])
                            if kp == nkp - 1:
                                copy_op(evac_eng(), zb[:, s, ng * 512:(ng + 1) * 512], ps[bk][:, :], [("ps", bk)], [(zkey, s)])
                    add(piece_rows(wb, kp * KC * 128, ng * 512, grp), item)

            def fin(slot):
                for s in range(4):
                    b = s % 2
                    sc = stat[:, 4 + b:5 + b]
                    junk = junkb if zkey == "z2" else xsf2[b]
                    if resid_load:
                        dma("pool", x1[:, s, :], x_ext[128 + ti * TT + s * 128: 128 + ti * TT + (s + 1) * 128, :], f"x1l{s}", writes=[("x1", s)])
                    act(junk[:], zb[:, s, :], AF.Square, [(zkey, s)], [("junkb" if zkey == "z2" else ("xsf2", b)), ("st2", b)], accum_out=sc)
                    act(sc, sc, AF.Ln, [("st2", b)], [("st2", b)], scale=1.0 / D, bias=EPS)
                    act(sc, sc, AF.Exp, [("st2", b)], [("st2", b)], scale=-0.5)
                    emit_m("dve", "scalar_tensor_tensor", dict(out=zb[:, s, :], in0=zb[:, s, :], scalar=sc, in1=gpo[:], op0=ALU.mult, op1=ALU.mult),
                           reads=[(zkey, s), ("st2", b), "gpo"], writes=[(zkey, s)])
                    emit_m("pool", "tensor_tensor", dict(out=x1[:, s, :], in0=x1[:, s, :], in1=zb[:, s, :], op=ALU.add), reads=[(zkey, s), ("x1", s)], writes=[("x1", s)])
                    if out_store:
                        dma("pool", y[ti * TT + s * 128: ti * TT + (s + 1) * 128, :], x1[:, s, :], f"yst{s}", reads=[("x1", s)])
            add(None, fin)

        def ffn1_tile():
            for pc in range(16):
                def dst(j, pb, pk, pc=pc):
                    r = rtmp[j % 4]
                    rk = ("rtmp", j % 4)
                    act(r[:], pb[:, :], AF.Relu, [pk], [rk])
                    emit_m("pool", "tensor_tensor", dict(out=uT[:, pc * 4 + j, :], in0=r[:], in1=r[:], op=ALU.mult), reads=[rk], writes=["uT"])
                proj_fm(piece_rows(wb_ff1, 0, pc * 512, "g6"), lambda kc: h2T[:, kc, :], TT, dst, "h2T", nb8=8)

        prep(meta, 1, 128, 0, xT, "xT")
        add_barrier()
        dump("xT_meta", xT[:, :, 128:144], ["xT"])
        set_c1(None)
        attn_kv(128, 16, kmT, vm, "vm", "kmT")
        hg_tile("meta", 0, 128, ntok=16)
        add_barrier()
        dump("kmT", kmT[:], ["kmT"])
        dump("vm", vm[:], ["vm"])
        dump("oml", oml[:], ["oml"])
        dump("S_meta", Sst[:], [("S", d_, h_) for d_ in range(2) for h_ in range(8)])
        def pre_rows(t):
            return x_pre[t * TT:(t + 1) * TT, :] if t < NPRE else None

        if NPRE > 0:
            cl, pre_issue = prep_light(pre_rows(0), xTL[0], "xTL0", pre_rows(1))
            add(None, lambda slot: pre_issue())
            for f in cl:
                add(None, lambda slot, f=f: f())
        for pt_ in range(NPRE):
            nxt = prep_light(pre_rows(pt_ + 1), xTL[(pt_ + 1) % 2], f"xTL{(pt_ + 1) % 2}", pre_rows(pt_ + 2))[0] if pt_ + 1 < NPRE else []
            set_c1(pt_ * 2)
            hg_pre(0, TT, [0, 1], xb=(xTL[pt_ % 2], f"xTL{pt_ % 2}"), inter=nxt)
        add_barrier()
        add(None, lambda slot: emit_casts(1000))
        dump("S_pre", Sst[:], [("S", d_, h_) for d_ in range(2) for h_ in range(8)])
        set_c1(None)
        STOP = _os.environ.get("STOP", "")
        for ti in (range(NT - 1, -1, -1) if STOP != "pre" else []):
            prep(x_ext[128 + ti * TT: 128 + (ti + 1) * TT, :], 4, 128, 0, xT, "xT")
            add_barrier()
            hg_tile("bwd", ti * TT, 128)
            add_barrier()
        if dbg:
            add_barrier()
            add(None, lambda slot: dma("pool", o_sb[:], ob_d[:, :, 0:TT].rearrange("h p t -> p h t"), "osb", reads=[("ob_d", 0)], writes=["osb"]))
            for h_ in range(8):
                dump(f"ob{h_}", o_sb[:, h_, :], ["osb"])
            add_barrier()
        for ti in (range(NT) if STOP == "" else []):
            prep(x_ext[ti * TT: ti * TT + 768, :], 6, 0, 0, xT, "xT")
            add_barrier()
            attn_tile(ti)
            add_barrier()
            if ti == 0:
                dump("xT0", xT[:, :, :], ["xT"])
                dump("oT_att", oT_att[:], ["oT_att"])
            hg_tile("fwd", ti * TT, 128)
            add_barrier()
            if ti == 0:
                dump("oT_hg", oT_hg[:], ["oT_hg"])
            merge_tile()
            add_barrier()
            if ti == 0:
                dump("mergedT", mergedT[:], ["mergedT"])
            tm_proj_norm(mergedT, "mergedT", wb_out, "g5", 1, zbuf, "z", 0, True, False, ti)
            if ti == 0:
                dump("x1", x1[:], [("x1", s_) for s_ in range(4)])
            prep(x1, 4, 0, 1, h2T, "h2T", rows_key="x1", xs=xsf2)
            add_barrier()
            if ti == 0:
                dump("h2T", h2T[:], ["h2T"])
            ffn1_tile()
            add_barrier()
            tm_proj_norm(uT, "uT", wb_ff2, "g7", 4, z2, "z2", 1, False, True, ti)
            add_barrier()

        order = []
        last_use = {}
        for idx, (p, f) in enumerate(items):
            if isinstance(p, dict):
                if id(p) not in last_use:
                    order.append(p)
                last_use[id(p)] = idx
        pos = {id(p): k for k, p in enumerate(order)}
        state = {"next": 0}

        def issue_load(k):
            p = order[k]
            slot = k % NSLOT
            p["slot"] = slot
            dma("sp", wsl[slot][:, :p["nk"], :], p["src"], f"w{slot}", reads=[p["wkey"]], writes=[("w", slot)])

        for idx, (p, f) in enumerate(items):
            if p == "barrier":
                P.barrier()
                continue
            while state["next"] < len(order) and (state["next"] < NSLOT or last_use[id(order[state["next"] - NSLOT])] < idx):
                issue_load(state["next"])
                state["next"] += 1
            if isinstance(p, dict):
                assert pos[id(p)] < state["next"], "weight slot deadlock"
                f(p["slot"])
            else:
                f(None)

        P.replay(nc, sems, dsem, final_eng="sp")
    return nc


def const_tables():
    j = np.arange(128)[:, None].astype(np.float32)
    r = np.arange(128)[None, :].astype(np.float32)
    dl = np.where(j >= r, -(r + 128 - j), NEG)
    dc = -np.abs(j - r)
    dr = np.where(j <= r, -(j + 128 - r), NEG)
    edge = np.full((128, 128), NEG, np.float32)
    dtab = np.stack([dl, dc, dr, edge, edge], axis=1).astype(np.float32)
    s = np.arange(128)[:, None]
    t = np.arange(128)[None, :]
    same = (s // 64) == (t // 64)
    s3 = np.arange(128)[:, None, None, None]
    d3 = np.arange(2)[None, :, None, None]
    p3 = np.arange(2)[None, None, :, None]
    t3 = np.arange(64)[None, None, None, :]
    hgmask = ((s3 // 64 == p3) & np.where(d3 == 0, (s3 % 64) <= t3, (s3 % 64) >= t3)).astype(np.float32)
    scan = np.ones((128, TT), np.float32)
    scan[:, ::64] = 0.0
    return dtab, hgmask, scan, np.eye(128, dtype=np.float32)


def core_inputs(seq_x, start, ntok, is_first, is_last, npre, meta_tokens, shared):
    S = seq_x.shape[0]
    x_ext = np.zeros((ntok + 256, D), np.float32)
    lo = max(0, start - 128)
    hi = min(S, start + ntok + 128)
    x_ext[128 - (start - lo): 128 - (start - lo) + (hi - lo)] = seq_x[lo:hi]
    x_pre = np.zeros((max(npre, 1) * TT, D), np.float32)
    premask = np.zeros((128, max(npre, 1) * 2), np.float32)
    npf = start // TT
    nsf = (S - start - ntok) // TT
    assert npf + nsf <= npre
    for i in range(npf):
        x_pre[i * TT:(i + 1) * TT] = seq_x[i * TT:(i + 1) * TT]
        premask[:, 2 * i] = 1.0
    for i in range(nsf):
        t0 = S - (i + 1) * TT
        x_pre[(npf + i) * TT:(npf + i + 1) * TT] = seq_x[t0:t0 + TT]
        premask[:, 2 * (npf + i) + 1] = 1.0
    dtab, hgmask, scan, ident = shared["tables"]
    dtab = dtab.copy()
    if not is_first:
        dtab[:, 3] = dtab[:, 0]
    if not is_last:
        dtab[:, 4] = dtab[:, 2]
    m = dict(shared["weights"])
    rowmask = np.zeros((128, 2), np.float32)
    rowmask[:64, 0] = 1.0
    rowmask[64:, 1] = 1.0
    onesc = np.zeros((128, 4, 128), np.float32)
    onesc[:, 0, 0:64] = 1.0
    onesc[:, 1, 64:128] = 1.0
    onesc[:16, 2, 0:64] = 1.0
    onesc[:16, 3, 64:128] = 1.0
    m.update(onesc=onesc)
    m.update(x_ext=x_ext, x_pre=x_pre, premask=premask, dtab=dtab, hgmask=hgmask, scanmask=scan, identf=ident, rowmask=rowmask)
    return m


def shared_inputs(meta_tokens, w_in, w_proj_hg, w_proj_att, w_out, w_ff1, w_ff2, g_pre_mix, g_post_mix, g_pre_ff, g_post_ff,
                  lb_logits, hg_out_gain, attn_sink):
    f = lambda a: np.ascontiguousarray(np.asarray(a, dtype=np.float32))
    meta = np.zeros((128, D), np.float32)
    meta[:16] = f(meta_tokens)
    gpre = np.stack([f(g_pre_mix)[0].reshape(KC, 128).T, f(g_pre_ff)[0].reshape(KC, 128).T], axis=1)
    gpost = np.stack([f(g_post_mix)[0], f(g_post_ff)[0]], axis=0)
    lbl = f(lb_logits).reshape(2, 2, 8, 128).transpose(3, 0, 1, 2)
    hgain = f(hg_out_gain)[0].reshape(8, 128).T
    sk = f(attn_sink)[0]
    sink = np.zeros((128, 8), np.float32)
    sink[:64] = sk[0::2][None, :]
    sink[64:] = sk[1::2][None, :]
    w = dict(meta=meta, w_in=f(w_in)[0], w_phg=f(w_proj_hg)[0], w_patt=f(w_proj_att)[0], w_out=f(w_out)[0], w_ff1=f(w_ff1)[0],
             w_ff2=f(w_ff2)[0], gpre=f(gpre), gpost=f(gpost), lbl=f(lbl), hgain=f(hgain), sink=sink)
    return {"weights": w, "tables": const_tables()}


_CACHE = {}


def run_layout(seqs, core_plan, NT, NPRE, shared, full=False):
    key = (NT, NPRE)
    if key not in _CACHE:
        _CACHE[key] = build(NT, NPRE)
    nc = _CACHE[key]
    in_maps = []
    for (si, start) in core_plan:
        S = seqs[si].shape[0]
        in_maps.append(core_inputs(seqs[si], start, NT * TT, start == 0, start + NT * TT == S, NPRE, None, shared))
    res = run_bass_kernel_spmd(nc, in_maps, core_ids=list(range(len(core_plan))))
    if full:
        return res.results
    return [r["y"] for r in res.results]


def kernel(x_prompt, x_sample, meta_tokens, w_in, w_proj_hg, w_proj_att, w_out, w_ff1, w_ff2,
           g_pre_mix, g_post_mix, g_pre_ff, g_post_ff, lb_logits, hg_out_gain, attn_sink):
    x_prompt = np.asarray(x_prompt, dtype=np.float32)
    x_sample = np.asarray(x_sample, dtype=np.float32)
    shared = shared_inputs(meta_tokens, w_in, w_proj_hg, w_proj_att, w_out, w_ff1, w_ff2, g_pre_mix, g_post_mix,
                           g_pre_ff, g_post_ff, lb_logits, hg_out_gain, attn_sink)
    seqs = [x_prompt[b] for b in range(4)] + [x_sample[0]]
    plan = [(b, 0) for b in range(4)] + [(4, c * 4096) for c in range(4)]
    outs = run_layout(seqs, plan, 8, 24, shared)
    y_prompt = np.stack(outs[:4], axis=0)
    y_sample = np.concatenate(outs[4:], axis=0)[None]
    return (y_prompt, y_sample)
```

```python
import numpy as np
from contextlib import ExitStack
import concourse.bass as bass
import concourse.mybir as mybir
from concourse.bass_utils import run_bass_kernel_spmd

F32 = mybir.dt.float32
BF16 = mybir.dt.bfloat16
AF = mybir.ActivationFunctionType
ALU = mybir.AluOpType

D = 2048
KC = 16
NIN = 10752
DFF = 8192
TT = 512
EPS = 1e-6
C_Q, C_I, C_FF, C_FB, C_G, C_AQ, C_AK, C_AV, C_GA, C_GB = 0, 1024, 2048, 3072, 4096, 5120, 6144, 6400, 6656, 8704
NEG = -1.0e9
ENGS = ("pe", "act", "dve", "pool", "sp")


class Op:
    __slots__ = ("eng", "fn", "waits", "idx", "marked", "dma_sem", "dma_val", "val")

    def __init__(self, eng, fn):
        self.eng = eng
        self.fn = fn
        self.waits = []
        self.idx = None
        self.marked = False
        self.dma_sem = None
        self.dma_val = None
        self.val = None


class Prog:
    def __init__(self, same_engine_sync=("act", "dve", "pool")):
        self.ops = {e: [] for e in ENGS}
        self.last_writes = {}
        self.readers = {}
        self.known = {e: {} for e in ENGS}
        self.dma_counts = {}
        self.same_engine_sync = set(same_engine_sync)
        self.pending = {e: [] for e in ENGS}

    @staticmethod
    def _stream(op):
        return ("dma", op.dma_sem) if op.dma_sem is not None else ("eng", op.eng)

    def _need(self, op, dep):
        if dep is None or dep is op:
            return
        st = self._stream(dep)
        if dep.dma_sem is not None:
            v = dep.dma_val
        else:
            if dep.eng == op.eng and dep.eng not in self.same_engine_sync:
                return
            v = dep.idx
        k = self.known[op.eng]
        if k.get(st, -1) >= v:
            return
        k[st] = v
        op.waits = [w for w in op.waits if self._stream(w) != st]
        op.waits.append(dep)

    def emit(self, eng, fn, reads=(), writes=(), dma_sem=None, ndma=1):
        op = Op(eng, fn)
        op.idx = len(self.ops[eng])
        if dma_sem is not None:
            c = self.dma_counts.get(dma_sem, 0) + ndma
            self.dma_counts[dma_sem] = c
            op.dma_sem = dma_sem
            op.dma_val = 16 * c
        for d in self.pending[eng]:
            self._need(op, d)
        self.pending[eng] = []
        for r in reads:
            for d in self.last_writes.get(r, {}).values():
                self._need(op, d)
        for w in writes:
            for d in self.last_writes.get(w, {}).values():
                self._need(op, d)
            for rd in self.readers.get(w, ()):
                self._need(op, rd)
        for r in reads:
            self.readers.setdefault(r, []).append(op)
        for w in writes:
            self.last_writes.setdefault(w, {})[self._stream(op)] = op
            self.readers[w] = []
        self.ops[eng].append(op)
        return op

    def barrier(self, engs=("pe", "act", "dve", "pool")):
        lasts = [self.ops[e][-1] for e in engs if self.ops[e]]
        for e in engs:
            self.pending[e] = list(lasts)

    def replay(self, nc, sems, dma_sems, final_eng="sp"):
        for e in ENGS:
            for op in self.ops[e]:
                for d in op.waits:
                    if d.dma_sem is None:
                        d.marked = True
        for e in ENGS:
            c = 0
            for op in self.ops[e]:
                if op.dma_sem is None and op.marked:
                    c += 1
                    op.val = c
        handles = {"pe": "tensor", "act": "scalar", "dve": "vector", "pool": "gpsimd", "sp": "sync"}
        finals = [(dma_sems[n], 16 * c) for n, c in self.dma_counts.items()]
        with nc.Block() as block:
            for e in ENGS:
                ops = self.ops[e]

                def body(engine, ops=ops, e=e):
                    for op in ops:
                        for d in op.waits:
                            if d.dma_sem is not None:
                                engine.wait_ge(dma_sems[d.dma_sem], d.dma_val)
                            else:
                                engine.wait_ge(sems[d.eng], d.val)
                        ins = op.fn(engine)
                        if op.dma_sem is None and op.marked:
                            ins.then_inc(sems[e], 1)
                    if e == final_eng:
                        for s, v in finals:
                            engine.wait_ge(s, v)
                getattr(block, handles[e])(body)


def build(NT, NPRE, dbg=False):
    NTOK = NT * TT
    nc = bass.Bass("TRN2", target_bir_lowering=False)
    es = ExitStack()
    import os as _os
    P = Prog(same_engine_sync=tuple(x for x in _os.environ.get("SES", "act,dve,pool").split(",") if x))
    dma_sem_names = []

    def din(name, shape, dt=F32):
        return nc.dram_tensor(name, list(shape), dt, kind="ExternalInput").ap()

    x_ext = din("x_ext", [NTOK + 256, D])
    x_pre = din("x_pre", [max(NPRE, 1) * TT, D])
    meta = din("meta", [128, D])
    w_in = din("w_in", [D, NIN])
    w_phg = din("w_phg", [1024, D])
    w_patt = din("w_patt", [1024, D])
    w_out = din("w_out", [D, D])
    w_ff1 = din("w_ff1", [D, DFF])
    w_ff2 = din("w_ff2", [DFF, D])
    gpre = din("gpre", [128, 2, KC])
    gpost = din("gpost", [2, D])
    lbl = din("lbl", [128, 2, 2, 8])
    hgain = din("hgain", [128, 8])
    sink = din("sink", [128, 8])
    premask = din("premask", [128, max(NPRE, 1) * 2])
    dtab = din("dtab", [128, 5, 128])
    hgmask = din("hgmask", [128, 2, 2, 64])
    rowmask = din("rowmask", [128, 2])
    scanmask = din("scanmask", [128, TT])
    identf = din("identf", [128, 128])
    onesc_d = din("onesc", [128, 4, 128])
    y = nc.dram_tensor("y", [NTOK, D], F32, kind="ExternalOutput").ap()
    wb_in = nc.dram_tensor("wb_in", [D, NIN], BF16, kind="Internal").ap()
    wb_kd = nc.dram_tensor("wb_kd", [D, 512], BF16, kind="Internal").ap()
    wb_p = nc.dram_tensor("wb_p", [2048, D], BF16, kind="Internal").ap()
    wb_out = nc.dram_tensor("wb_out", [D, D], BF16, kind="Internal").ap()
    wb_ff1 = nc.dram_tensor("wb_ff1", [D, DFF], BF16, kind="Internal").ap()
    wb_ff2 = nc.dram_tensor("wb_ff2", [DFF, D], BF16, kind="Internal").ap()
    ob_d = nc.dram_tensor("ob_d", [8, 128, NTOK], F32, kind="Internal").ap()
    dbg_out = {}

    with es:
        def sb(name, shape, dt):
            return es.enter_context(nc.sbuf_tensor(name, list(shape), dt))

        def pst(name, shape, dt=F32):
            return es.enter_context(nc.psum_tensor(name, list(shape), dt))

        NSLOT = 3
        wsl = [sb(f"wsl{i}", [128, KC, 512], BF16) for i in range(NSLOT)]
        Sst = sb("Sst", [128, 2, 8, 128], F32)
        ident_f = sb("ident_f", [128, 128], F32)
        ident_b = sb("ident_b", [128, 128], BF16)
        ones_b = sb("ones_b", [128, 128], BF16)
        hgm = sb("hgm", [128, 2, 2, 64], F32)
        rowm = sb("rowm", [128, 2], F32)
        scm = sb("scm", [128, TT], F32)
        dtb = sb("dtb", [128, 5, 128], F32)
        gpre_s = sb("gpre_s", [128, 2, KC], F32)
        lb_s = sb("lb_s", [128, 2, 2, 8], F32)
        oml = sb("oml", [128, 2, 8], F32)
        c1t = sb("c1t", [128, 2, 8], F32)
        nc1t = sb("nc1t", [128, 2, 8], F32)
        hgain_s = sb("hgain_s", [128, 8], F32)
        esink = sb("esink", [128, 8], F32)
        pmask = sb("pmask", [128, max(NPRE, 1) * 2], F32)
        stat = sb("stat", [128, 16], F32)
        kmT = sb("kmT", [128, 4, 128], BF16)
        vm = sb("vm", [128, 4, 2, 128], BF16)
        rowm8 = sb("rowm8", [128, 2], F32)
        arena = sb("arena", [128, 144128], mybir.dt.uint8)

        def carve(off_kb, shape, dt):
            nbytes = int(np.prod(shape[1:])) * (2 if dt == BF16 else 4)
            off = int(off_kb * 1024)
            assert off + nbytes <= 144128, (off_kb, shape)
            v = arena[:, off:off + nbytes].bitcast(dt)
            if len(shape) == 2:
                return v
            names = " ".join(f"a{i}" for i in range(len(shape) - 1))
            kw = {f"a{i}": shape[i + 1] for i in range(len(shape) - 1)}
            return v.rearrange(f"p ({names}) -> p {names}", **kw)

        x1 = carve(0, [128, 4, D], F32)
        xrow = [carve(0, [128, D], F32), carve(8, [128, D], F32)]
        xsf = [carve(16, [128, D], F32), carve(24, [128, D], F32)]
        mergedT = carve(32, [128, KC, TT], BF16)
        xT = carve(48, [128, KC, 768], BF16)
        oT_att = carve(72, [128, 8, TT], BF16)
        oT_hg = carve(80, [128, 8, TT], BF16)
        TB = 88
        qTa = carve(TB, [128, 2, 8, TT], BF16)
        kTd = carve(TB + 16, [128, 4, 768], BF16)
        vat = carve(TB + 22, [128, 6, 4, 2, 128], BF16)
        PT = [carve(TB + 34 + 4 * i, [128, 4, 512], BF16) for i in range(2)]
        scs = [carve(TB + 42 + 3 * i, [128, 3, 256], F32) for i in range(2)]
        rec = [carve(TB + 48 + i, [128, 256], F32) for i in range(2)]
        onesc = carve(TB + 50, [128, 4, 128], BF16)
        mg = [carve(TB + 2 * i, [128, TT], F32) for i in range(8)]
        zbuf = carve(48, [128, 4, D], F32)
        h2T = carve(96, [128, KC, TT], BF16)
        xsf2 = [carve(112, [128, D], F32), carve(120, [128, D], F32)]
        gpo = carve(128, [128, D], F32)
        uT = carve(32, [128, 64, TT], BF16)
        rtmp = [carve(112 + 2 * i, [128, TT], F32) for i in range(4)]
        z2 = carve(96, [128, 4, D], F32)
        junkb = carve(136, [128, D], BF16)

        ps = [pst(f"ps{i}", [128, 512]) for i in range(8)]
        psb = ps[7][:, :].bitcast(BF16)

        sems = {e: es.enter_context(nc.semaphore("s_" + e)) for e in ENGS}
        dsem = {}

        def dma(eng, out, in_, sem, reads=(), writes=()):
            if sem not in dsem:
                dsem[sem] = es.enter_context(nc.semaphore("d_" + sem))
            s = dsem[sem]
            return P.emit(eng, lambda e: e.dma_start(out=out, in_=in_).then_inc(s, 16),
                          reads=reads, writes=writes, dma_sem=sem)

        def dump(name, ap, keys):
            if not dbg:
                return
            t = nc.dram_tensor("dbg_" + name, list(ap.shape), F32, kind="ExternalOutput").ap()
            idx = tuple(slice(None) for _ in ap.shape)
            add(None, lambda slot: dma("pool", t[idx], ap, "dbg_" + name, reads=keys))

        def emit_m(eng, meth, kw, reads=(), writes=()):
            return P.emit(eng, lambda e: getattr(e, meth)(**kw), reads=reads, writes=writes)

        rr = {"ev": 0}

        def evac_eng():
            rr["ev"] ^= 1
            return "act" if rr["ev"] else "dve"

        def copy_op(eng, out, in_, reads, writes, scale=None):
            if eng == "act":
                if scale is None:
                    return emit_m("act", "activation", dict(out=out, in_=in_, func=AF.Copy), reads=reads, writes=writes)
                return emit_m("act", "activation", dict(out=out, in_=in_, func=AF.Copy, scale=scale), reads=reads, writes=writes)
            if scale is None:
                return P.emit(eng, lambda e: e.tensor_copy(out=out, in_=in_), reads=reads, writes=writes)
            return P.emit(eng, lambda e: e.tensor_scalar(out=out, in0=in_, scalar1=scale, scalar2=None, op0=ALU.mult), reads=reads, writes=writes)

        def act(out, in_, func, reads, writes, **kw):
            return emit_m("act", "activation", dict(out=out, in_=in_, func=func, **kw), reads=reads, writes=writes)

        def mm(out, lhsT, rhs, start, stop, reads, writes, **kw):
            return P.emit("pe", lambda e: e.matmul(out, lhsT=lhsT, rhs=rhs, start=start, stop=stop, **kw), reads=reads, writes=writes)

        cload = [
            (ident_f[:], identf[:, :]), (hgm[:], hgmask[:, :, :, :]), (rowm[:], rowmask[:, :]), (scm[:], scanmask[:, :]), (dtb[:], dtab[:, :, :]),
            (gpre_s[:], gpre[:, :, :]), (lb_s[:], lbl[:, :, :, :]), (hgain_s[:], hgain[:, :]), (esink[:], sink[:, :]),
            (pmask[:], premask[:, :]),
        ]
        for i, (o, s) in enumerate(cload):
            dma("sp", o, s, "const", writes=[("c", i)])
        CK = [("c", i) for i in range(len(cload))]

        late_casts = []

        def cast(dst, src, grp):
            if grp.startswith("g1"):
                dma("pool", dst, src, "cast_" + grp, writes=[("wb", grp)])
            else:
                late_casts.append(lambda: dma("pool", dst, src, "cast_" + grp, writes=[("wb", grp)]))

        def emit_casts(n):
            for _ in range(min(n, len(late_casts))):
                late_casts.pop(0)()

        def cast_cols(c0, c1, grp):
            for c in range(c0, c1, 512):
                cast(wb_in[:, c:c + 512], w_in[:, c:c + 512], grp)

        cast_cols(C_I, C_I + 1024, "g1")
        cast_cols(C_FF, C_FF + 2048, "g1")
        cast_cols(C_Q, C_Q + 1024, "g2")
        cast_cols(C_G, C_G + 1024, "g2")
        cast_cols(C_AQ, C_AQ + 1024, "g3")
        cast_cols(C_AK, C_AK + 512, "g1k")
        for g in range(4):
            for dup in range(2):
                cast(wb_kd[:, g * 128 + dup * 64: g * 128 + dup * 64 + 64], w_in[:, C_AK + g * 64: C_AK + g * 64 + 64], "g1k")
        cast_cols(C_GA, C_GA + 4096, "g4")
        for r in range(0, 1024, 512):
            cast(wb_p[r:r + 512, :], w_phg[r:r + 512, :], "g4")
            cast(wb_p[1024 + r:1024 + r + 512, :], w_patt[r:r + 512, :], "g4")
        for r in range(0, D, 512):
            cast(wb_out[r:r + 512, :], w_out[r:r + 512, :], "g5")
        for c in range(0, DFF, 512):
            cast(wb_ff1[:, c:c + 512], w_ff1[:, c:c + 512], "g6")
        for r in range(0, DFF, 512):
            cast(wb_ff2[r:r + 512, :], w_ff2[r:r + 512, :], "g7")

        copy_op("dve", ident_b[:], ident_f[:], CK, ["identb"])
        P.emit("dve", lambda e: e.memset(ones_b[:], 1.0), writes=["ones"])
        emit_m("dve", "tensor_tensor", dict(out=oml[:], in0=lb_s[:, 1], in1=lb_s[:, 0], op=ALU.subtract), reads=CK, writes=["oml"])
        act(oml[:], oml[:], AF.Exp, ["oml"], ["oml"])
        act(oml[:], oml[:], AF.Ln, ["oml"], ["oml"], bias=1.0)
        act(oml[:], oml[:], AF.Exp, ["oml"], ["oml"], scale=-1.0)
        emit_m("dve", "tensor_scalar", dict(out=oml[:], in0=oml[:], scalar1=-1.0, scalar2=1.0, op0=ALU.mult, op1=ALU.add), reads=["oml"], writes=["oml"])
        act(esink[:], esink[:], AF.Exp, CK, ["esink"])
        P.emit("dve", lambda e: e.memset(Sst[:], 0.0), writes=["S"])
        P.emit("pool", lambda e: e.memset(kmT[:], 0.0), writes=["kmT"])
        P.emit("pool", lambda e: e.memset(vm[:], 0.0), writes=["vm"])
        emit_m("dve", "tensor_scalar", dict(out=rowm8[:], in0=rowm[:], scalar1=0.125, scalar2=None, op0=ALU.mult), reads=CK, writes=["rowm8"])

        items = []

        def piece_in(c0, grp, kd=False):
            src = (wb_kd if kd else wb_in)[:, c0:c0 + 512].rearrange("(kc p) n -> p kc n", p=128)
            return {"src": src, "wkey": ("wb", grp), "nk": KC}

        def piece_rows(wb, r0, c0, grp, nk=KC):
            return {"src": wb[r0:r0 + nk * 128, c0:c0 + 512].rearrange("(kc p) n -> p kc n", p=128), "wkey": ("wb", grp), "nk": nk}

        def add(piece, fn):
            items.append((piece, fn))

        def add_barrier():
            items.append(("barrier", None))

        def set_c1(col):
            def fn(slot):
                if col is None:
                    copy_op("dve", c1t[:], oml[:], ["oml"], ["c1"])
                else:
                    for d in range(2):
                        emit_m("dve", "tensor_scalar", dict(out=c1t[:, d], in0=oml[:, d], scalar1=pmask[:, col + d:col + d + 1], scalar2=None, op0=ALU.mult), reads=["oml"] + CK, writes=["c1"])
                emit_m("dve", "tensor_scalar", dict(out=nc1t[:], in0=c1t[:], scalar1=-1.0, scalar2=None, op0=ALU.mult), reads=["c1"], writes=["c1"])
            add(None, fn)

        def prep(src_rows, nsub, col0, gsel, dstT, dkey, rows_key=None, xr=None, xs=None):
            xr = xr or xrow
            xsk = "xsf2" if xs is not None else None
            xs = xs or xsf
            xkey = (lambda b: ("xsf2", b)) if xsk else (lambda b: ("x1", 2 + b))

            def fn(slot):
                if rows_key is None:
                    dma("act", xr[0][:], src_rows[0:128, :], "xrow0", writes=[("x1", 0)])
                for s in range(nsub):
                    b = s % 2
                    if rows_key is None:
                        if s + 1 < nsub:
                            dma("act", xr[1 - b][:], src_rows[(s + 1) * 128:(s + 2) * 128, :], f"xrow{1 - b}", writes=[("x1", 1 - b)])
                        src = xr[b][:]
                        rk = [("x1", b)]
                    else:
                        src = src_rows[:, s, :]
                        rk = [(rows_key, s)]
                    sc = stat[:, 2 * b:2 * b + 1]
                    act(xs[b][:], src, AF.Square, rk, [xkey(b), ("st", b)], accum_out=sc)
                    act(sc, sc, AF.Ln, [("st", b)], [("st", b)], scale=1.0 / D, bias=EPS)
                    act(sc, sc, AF.Exp, [("st", b)], [("st", b)], scale=-0.5)
                    emit_m("dve", "tensor_scalar", dict(out=xs[b][:], in0=src, scalar1=sc, scalar2=None, op0=ALU.mult), reads=rk + [("st", b)], writes=[xkey(b)])
                    for q4 in range(4):
                        pb = ps[q4]
                        for j in range(4):
                            kc = q4 * 4 + j
                            emit_m("pe", "transpose", dict(out=pb[:, j * 128:(j + 1) * 128], in_=xs[b][:, kc * 128:(kc + 1) * 128], identity=ident_f[:]),
                                   reads=[xkey(b)] + CK, writes=[("ps", q4)])
                        for j in range(4):
                            kc = q4 * 4 + j
                            copy_op(evac_eng(), dstT[:, kc, col0 + s * 128: col0 + (s + 1) * 128], pb[:, j * 128:(j + 1) * 128],
                                    [("ps", q4)] + CK, [dkey], scale=gpre_s[:, gsel, kc:kc + 1])
            add(None, fn)

        cnt8 = {"b": 0}

        xTL = [carve(48, [128, KC, TT], BF16), carve(64, [128, KC, TT], BF16)]
        xrL = [carve(40, [128, D], F32), carve(80, [128, D], F32)]
        junkL = carve(24, [128, D], BF16)

        def prep_light(src_rows, dstT, dkey, nxt_rows=None):
            def issue(rows, s):
                dma("act", xrL[s % 2][:], rows[s * 128:(s + 1) * 128, :], f"xrL{s % 2}", writes=[("xrL", s % 2)])

            def front(s):
                b = s % 2
                src = xrL[b][:]
                rk = [("xrL", b)]
                sc = stat[:, 8 + b:9 + b]
                act(junkL[:], src, AF.Square, rk, ["junkL", ("stL", b)], accum_out=sc)
                act(sc, sc, AF.Ln, [("stL", b)], [("stL", b)], scale=1.0 / D, bias=EPS)
                act(sc, sc, AF.Exp, [("stL", b)], [("stL", b)], scale=-0.5)
                emit_m("dve", "tensor_scalar", dict(out=src, in0=src, scalar1=sc, scalar2=None, op0=ALU.mult), reads=rk + [("stL", b)], writes=rk)

            def back(s):
                b = s % 2
                rk = [("xrL", b)]
                for q4 in range(4):
                    bkk = (3, 6)[q4 % 2]
                    pb = ps[bkk]
                    for j in range(4):
                        kc = q4 * 4 + j
                        P.emit("pe", (lambda pb=pb, j=j, kc=kc, b=b: (lambda e: e.transpose(out=pb[:, j * 128:(j + 1) * 128], in_=xrL[b][:, kc * 128:(kc + 1) * 128], identity=ident_f[:])))(),
                               reads=rk + CK, writes=[("ps", bkk)])
                    for j in range(4):
                        kc = q4 * 4 + j
                        copy_op(evac_eng(), dstT[:, kc, s * 128:(s + 1) * 128], pb[:, j * 128:(j + 1) * 128], [("ps", bkk)] + CK, [dkey], scale=gpre_s[:, 0, kc:kc + 1])

            def p0():
                front(0)

            def p1():
                back(0)
                issue(src_rows, 2)
                front(1)

            def p2():
                back(1)
                issue(src_rows, 3)
                front(2)

            def p3():
                back(2)
                front(3)

            def p4():
                back(3)
                if nxt_rows is not None:
                    issue(nxt_rows, 0)
                    issue(nxt_rows, 1)
            return [p0, p1, p2, p3, p4], (lambda: (issue(src_rows, 0), issue(src_rows, 1)))

        def proj_fm(piece, xTv, ncols_tok, dst_fn, xkey, ntile=4, nb8=4):
            def fn(slot):
                for j in range(ntile):
                    cnt8["b"] = (cnt8["b"] + 1) % nb8
                    bk = cnt8["b"]
                    for kc in range(KC):
                        mm(ps[bk][:, :ncols_tok], wsl[slot][:, kc, j * 128:(j + 1) * 128], xTv(kc), kc == 0, kc == KC - 1,
                           [("w", slot), xkey], [("ps", bk)])
                    dst_fn(j, ps[bk], ("ps", bk))
            add(piece, fn)

        def proj_tm(piece, xTv_sub, nsub, dst_fn, xkey, ncol=512, nk=KC, msize=128):
            def fn(slot):
                for s in range(nsub):
                    bk = 4 + (s % 3)
                    for kc in range(nk):
                        mm(ps[bk][:msize, :ncol], xTv_sub(kc, s), wsl[slot][:, kc, :ncol], kc == 0, kc == nk - 1,
                           [("w", slot), xkey], [("ps", bk)])
                    dst_fn(s, ps[bk], ("ps", bk))
            add(piece, fn)

        vpar = [carve(0, [128, 4, 1024], BF16), carve(8, [128, 4, 1024], BF16)]
        kinvT = carve(16, [128, 8, TT], BF16)
        qdecT = carve(24, [128, 8, TT], BF16)
        ktok = carve(32, [128, 8, TT], BF16)
        gsb = carve(40, [128, 8, TT], BF16)
        o_sb = carve(88, [128, 8, TT], F32)
        NTS = 3
        Tset = [[carve(104 + 10 * j + 2 * i, [128, TT], F32) for i in range(5)] for j in range(NTS)]
        scb = [carve(134 + i, [128, 8, 64], BF16) for i in range(2)]
        onesf = carve(134, [128, TT], F32)
        Sbf = [carve(136 + 2 * i, [128, 8, 128], BF16) for i in range(2)]
        decs = carve(140, [128, 8, 8], F32)
        cnt = {"bank": 0, "ts": 0}

        def nbank(m=4):
            cnt["bank"] = (cnt["bank"] + 1) % m
            return cnt["bank"]

        def sigm_neg(dst, src, rk, wk, n, sgn=1.0):
            act(dst[:, :n], src, AF.Exp, rk, [wk], scale=sgn)
            act(dst[:, :n], dst[:, :n], AF.Ln, [wk], [wk], bias=1.0)
            act(dst[:, :n], dst[:, :n], AF.Exp, [wk], [wk], scale=-1.0)

        deferred = []

        def defer(fn):
            deferred.append(fn)

        def flush(keep=0):
            while len(deferred) > keep:
                deferred.pop(0)()

        def proj_head(piece, j, xv, ntok, consume, xkey="xT", nbk=4, pre=None):
            def fn(slot):
                bk = nbank(nbk)
                for kc in range(KC):
                    mm(ps[bk][:, :ntok], wsl[slot][:, kc, j * 128:(j + 1) * 128], xv(kc), kc == 0, kc == KC - 1, [("w", slot), xkey], [("ps", bk)])
                flush(2)
                if pre is not None:
                    pre()
                consume(ps[bk], ("ps", bk))
            add(piece, fn)

        def hg_v(xvs, nsub, msz, par, xkey="xT"):
            for half in range(2):
                def dst(s, pb, pk, half=half):
                    if par:
                        copy_op("act", vpar[0][:, s, half * 512:(half + 1) * 512], pb[:, :], [pk] + CK, ["vpar"], scale=rowm[:, 0:1])
                        copy_op("dve", vpar[1][:, s, half * 512:(half + 1) * 512], pb[:, :], [pk] + CK, ["vpar"], scale=rowm[:, 1:2])
                    else:
                        copy_op(evac_eng(), vpar[0][:msz, s, half * 512:(half + 1) * 512], pb[:msz, :], [pk], ["vpar"])
                proj_tm(piece_in(C_I + half * 512, "g1"), xvs, nsub, dst, xkey, msize=msz)

        def hg_pre(xcol0, ntok, dirs, xb=None, inter=()):
            nsub = (ntok + 127) // 128
            msz = min(128, ntok)
            xt_, xkey = (xT, "xT") if xb is None else xb
            nbk = 4 if xb is None else 3
            inter = list(inter)
            nhead = [0]
            xv = lambda kc: xt_[:, kc, xcol0:xcol0 + ntok]
            xvs = lambda kc, s: xt_[:, kc, xcol0 + s * 128: xcol0 + s * 128 + msz]
            add(None, lambda slot: P.emit("pool", lambda e: e.memset(onesf[:], 1.0), writes=["onesf"]))
            if inter:
                add(None, lambda slot, f=inter.pop(0): f())
            hg_v(xvs, nsub, msz, False, xkey=xkey)
            fcols = {0: C_FF, 1: C_FB}
            for hp in range(2):
                for dr in dirs:
                    pf = piece_in(fcols[dr] + hp * 512, "g1")
                    for j in range(4):
                        h = hp * 4 + j

                        def consume(pb, pk, h=h, dr=dr):
                            r = cnt["ts"] = (cnt["ts"] + 1) % NTS
                            T = Tset[r]
                            tk = ("T", r)
                            n = ntok
                            sigm_neg(T[0], pb[:, :n], [pk], tk, n)
                            act(T[1][:, :n], T[0][:, :n], AF.Ln, [tk, "c1"], [tk], scale=nc1t[:, dr, h:h + 1], bias=1.0)
                            emit_m("dve", "tensor_tensor_scan", dict(out=T[2][:, :n], data0=onesf[:, :n], data1=T[1][:, :n], initial=0.0, op0=ALU.mult, op1=ALU.add),
                                   reads=[tk, "onesf"], writes=[tk])
                            act(T[4][:, 0:1], T[2][:, n - 1:n], AF.Exp, [tk], [tk])
                            if dr == 0:
                                act(T[3][:, :n], T[2][:, :n], AF.Exp, [tk], [tk], scale=-1.0, bias=T[2][:, n - 1:n])
                            else:
                                emit_m("dve", "tensor_tensor", dict(out=T[2][:, :n], in0=T[2][:, :n], in1=T[1][:, :n], op=ALU.subtract), reads=[tk], writes=[tk])
                                act(T[3][:, :n], T[2][:, :n], AF.Exp, [tk], [tk])
                            emit_m("dve", "scalar_tensor_tensor", dict(out=kinvT[:, h, :n], in0=T[0][:, :n], scalar=c1t[:, dr, h:h + 1], in1=T[3][:, :n], op0=ALU.mult, op1=ALU.mult),
                                   reads=[tk, "c1"], writes=[("kinvT", h)])
                            def tail(h=h, dr=dr, T=T, tk=tk):
                                for s in range(nsub):
                                    P.emit("pe", (lambda s=s: (lambda e: e.transpose(out=psb[:msz, s * 128:(s + 1) * 128], in_=kinvT[:, h, s * 128:s * 128 + msz], identity=ident_b[:])))(),
                                           reads=[("kinvT", h), "identb"], writes=[("ps", 7)])
                                copy_op(evac_eng(), ktok[:msz, h, :nsub * 128], psb[:msz, :nsub * 128], [("ps", 7)], [("ktok", h)])
                                ib = 4 + (h // 4)
                                for s in range(nsub):
                                    mm(ps[ib][:, (h % 4) * 128:(h % 4 + 1) * 128], ktok[:msz, h, s * 128:(s + 1) * 128], vpar[0][:msz, s, h * 128:(h + 1) * 128], s == 0, s == nsub - 1,
                                       [("ktok", h), "vpar"], [("ps", ib)])
                                Sv = Sst[:, dr, h, :]
                                emit_m("dve", "scalar_tensor_tensor", dict(out=Sv, in0=Sv, scalar=T[4][:, 0:1], in1=ps[ib][:, (h % 4) * 128:(h % 4 + 1) * 128], op0=ALU.mult, op1=ALU.add),
                                       reads=[("S", dr, h), tk, ("ps", ib)], writes=[("S", dr, h)])
                            defer(tail)
                        pre_ = None
                        if inter and nhead[0] > 0 and nhead[0] % 4 == 0:
                            pre_ = inter.pop(0)
                        proj_head(pf, j, xv, ntok, consume, xkey=xkey, nbk=nbk, pre=pre_)
                        nhead[0] += 1
                        if xb is not None and nhead[0] in (2, 7, 12):
                            add(None, lambda slot: emit_casts(-(-(-(-63 // max(1, NPRE - 2))) // 3)))
            add(None, lambda slot: flush())
            for f in inter:
                add(None, lambda slot, f=f: f())

        def hg_main(mode, tok0, xcol0):
            dr = 1 if mode == "bwd" else 0
            ntok = TT
            xv = lambda kc: xT[:, kc, xcol0:xcol0 + ntok]
            xvs = lambda kc, s: xT[:, kc, xcol0 + s * 128: xcol0 + s * 128 + 128]
            if mode == "fwd":
                add(None, lambda slot: dma("pool", o_sb[:], ob_d[:, :, tok0:tok0 + TT].rearrange("h p t -> p h t"), "osb", reads=[("ob_d", tok0)], writes=["osb"]))
            hg_v(xvs, 4, 128, True)
            fcol = C_FB if dr else C_FF
            for hp in range(2):
                pf = piece_in(fcol + hp * 512, "g1")
                pq = piece_in(C_Q + hp * 512, "g2")
                pg = piece_in(C_G + hp * 512, "g2") if mode == "fwd" else None
                hold = {}

                def mk(h, hold=hold):
                        def cf(pb, pk, h=h, hold=hold):
                            r = cnt["ts"] = (cnt["ts"] + 1) % NTS
                            hold[h] = r
                            T = Tset[r]
                            tk = ("T", r)
                            sigm_neg(T[0], pb[:, :], [pk], tk, TT)
                            act(T[1][:], T[0][:], AF.Ln, [tk, "c1"], [tk], scale=nc1t[:, dr, h:h + 1], bias=1.0)
                            emit_m("dve", "tensor_tensor_scan", dict(out=T[2][:], data0=scm[:], data1=T[1][:], initial=0.0, op0=ALU.mult, op1=ALU.add), reads=[tk] + CK, writes=[tk])
                            act(decs[:, h, :], T[2][:, 63::64], AF.Exp, [tk], [("dec", h)])
                            if dr == 0:
                                act(T[3][:], T[2][:], AF.Exp, [tk], [tk])
                                act(T[4][:], T[2][:], AF.Exp, [tk], [tk], scale=-1.0)
                            else:
                                emit_m("dve", "tensor_tensor", dict(out=T[2][:], in0=T[2][:], in1=T[1][:], op=ALU.subtract), reads=[tk], writes=[tk])
                                act(T[4][:], T[2][:], AF.Exp, [tk], [tk])
                                act(T[3][:], T[2][:], AF.Exp, [tk], [tk], scale=-1.0)
                            emit_m("dve", "scalar_tensor_tensor", dict(out=kinvT[:, h, :], in0=T[0][:], scalar=c1t[:, dr, h:h + 1], in1=T[4][:], op0=ALU.mult, op1=ALU.mult),
                                   reads=[tk, "c1"], writes=[("kinvT", h)])
                            def tail(h=h):
                                for s in range(4):
                                    P.emit("pe", (lambda s=s: (lambda e: e.transpose(out=psb[:, s * 128:(s + 1) * 128], in_=kinvT[:, h, s * 128:(s + 1) * 128], identity=ident_b[:])))(),
                                           reads=[("kinvT", h), "identb"], writes=[("ps", 7)])
                                copy_op(evac_eng(), ktok[:, h, :], psb[:, :512], [("ps", 7)], [("ktok", h)])
                            defer(tail)
                        def cq(pb, pk, h=h, hold=hold):
                            T = Tset[hold[h]]
                            tk = ("T", hold[h])
                            sigm_neg(T[1], pb[:, :], [pk], tk, TT, sgn=-1.0)
                            emit_m("dve", "tensor_tensor", dict(out=T[1][:], in0=T[1][:], in1=pb[:, :], op=ALU.mult), reads=[tk, pk], writes=[tk])
                            emit_m("dve", "scalar_tensor_tensor", dict(out=qdecT[:, h, :], in0=T[1][:], scalar=float(128 ** -0.5), in1=T[3][:], op0=ALU.mult, op1=ALU.mult),
                                   reads=[tk], writes=[("qdecT", h)])

                        return cf, cq

                def mkg(h):
                    def cg(pb, pk, h=h):
                        r = cnt["ts"] = (cnt["ts"] + 1) % NTS
                        T = Tset[r]
                        tk = ("T", r)
                        sigm_neg(T[0], pb[:, :], [pk], tk, TT, sgn=-1.0)
                        emit_m("dve", "tensor_tensor", dict(out=gsb[:, h, :], in0=T[0][:], in1=pb[:, :], op=ALU.mult), reads=[tk, pk], writes=[("gs", h)])

                    return cg
                fns = {hp * 4 + j: mk(hp * 4 + j) for j in range(4)}
                for pr2 in range(2):
                    for j in (2 * pr2, 2 * pr2 + 1):
                        proj_head(pf, j, xv, ntok, fns[hp * 4 + j][0])
                    for j in (2 * pr2, 2 * pr2 + 1):
                        proj_head(pq, j, xv, ntok, fns[hp * 4 + j][1])
                if mode == "fwd":
                    for j in range(4):
                        proj_head(pg, j, xv, ntok, mkg(hp * 4 + j))

            add(None, lambda slot: flush())

            def scan(slot):
                order = list(range(8)) if dr == 0 else list(range(7, -1, -1))

                def sc_mask(i):
                    c = order[i]
                    pr, par = c // 2, c % 2
                    bk = i % 2
                    for h in range(8):
                        mm(ps[bk][:, h * 64:(h + 1) * 64], kinvT[:, h, pr * 128:(pr + 1) * 128], qdecT[:, h, c * 64:(c + 1) * 64], True, True,
                           [("kinvT", h), ("qdecT", h)], [("ps", bk)])
                    for h in range(8):
                        emit_m("dve", "tensor_tensor", dict(out=scb[bk][:, h, :], in0=ps[bk][:, h * 64:(h + 1) * 64], in1=hgm[:, dr, par, :], op=ALU.mult),
                               reads=[("ps", bk)] + CK, writes=[("scb", bk)])

                def sbf(i, h):
                    c = order[i]
                    Sv = Sst[:, dr, h, :]
                    if dr == 0:
                        sc_ = None if i == 0 else decs[:, h, order[i - 1]:order[i - 1] + 1]
                    else:
                        sc_ = decs[:, h, c:c + 1]
                    copy_op("act", Sbf[i % 2][:, h, :], Sv, [("S", dr, h), ("dec", h)], [("Sbf", i % 2, h)], scale=sc_)

                for h in range(8):
                    sbf(0, h)
                sc_mask(0)
                for i in range(8):
                    c = order[i]
                    pr, par = c // 2, c % 2
                    if i + 1 < 8:
                        sc_mask(i + 1)
                    ob = 2 + (i % 2)
                    for h in range(8):
                        oo = ps[ob][:, h * 64:(h + 1) * 64]
                        mm(oo, vpar[par][:, pr, h * 128:(h + 1) * 128], scb[i % 2][:, h, :], True, False, ["vpar", ("scb", i % 2)], [("ps", ob)])
                        mm(oo, Sbf[i % 2][:, h, :], qdecT[:, h, c * 64:(c + 1) * 64], False, True, [("Sbf", i % 2, h), ("qdecT", h)], [("ps", ob)])
                    osl = o_sb[:, :, c * 64:(c + 1) * 64]
                    pso = ps[ob][:, :].rearrange("p (h t) -> p h t", h=8)
                    if mode == "fwd":
                        emit_m("dve", "tensor_tensor", dict(out=osl, in0=osl, in1=pso, op=ALU.add), reads=[("ps", ob), "osb"], writes=["osb"])
                    else:
                        copy_op("act", osl, pso, [("ps", ob)], ["osb"])
                    for h in range(8):
                        ib = 4 + (h // 4)
                        mm(ps[ib][:, (h % 4) * 128:(h % 4 + 1) * 128], ktok[:, h, pr * 128:(pr + 1) * 128], vpar[par][:, pr, h * 128:(h + 1) * 128], True, True,
                           [("ktok", h), "vpar"], [("ps", ib)])
                    for h in range(8):
                        Sv = Sst[:, dr, h, :]
                        if dr == 0:
                            sc_ = 1.0 if i == 0 else decs[:, h, order[i - 1]:order[i - 1] + 1]
                        else:
                            sc_ = decs[:, h, c:c + 1]
                        emit_m("dve", "scalar_tensor_tensor", dict(out=Sv, in0=Sv, scalar=sc_, in1=ps[4 + (h // 4)][:, (h % 4) * 128:(h % 4 + 1) * 128], op0=ALU.mult, op1=ALU.add),
                               reads=[("S", dr, h), ("ps", 4 + (h // 4)), ("dec", h)], writes=[("S", dr, h)])
                    if i + 1 < 8:
                        for h in range(8):
                            sbf(i + 1, h)
                if dr == 0:
                    for h in range(8):
                        Sv = Sst[:, dr, h, :]
                        emit_m("dve", "tensor_scalar", dict(out=Sv, in0=Sv, scalar1=decs[:, h, 7:8], scalar2=None, op0=ALU.mult), reads=[("S", dr, h), ("dec", h)], writes=[("S", dr, h)])
            add(None, scan)

            if mode == "bwd":
                add(None, lambda slot: dma("pool", ob_d[:, :, tok0:tok0 + TT].rearrange("h p t -> p h t"), o_sb[:], "osb", reads=["osb"], writes=[("ob_d", tok0)]))
            else:
                def fin(slot):
                    for h in range(8):
                        r = cnt["ts"] = (cnt["ts"] + 1) % NTS
                        T = Tset[r]
                        tk = ("T", r)
                        sqb = scb[0] if h % 2 == 0 else scb[1]
                        sq = kinvT[:, h, :]
                        act(sq, o_sb[:, h, :], AF.Square, ["osb"], [("kinvT", h)])
                        mm(ps[6][:, :], ones_b[:], sq, True, True, ["ones", ("kinvT", h)], [("ps", 6)])
                        act(T[1][:], ps[6][:, :], AF.Ln, [("ps", 6)], [tk], scale=1.0 / 128, bias=EPS)
                        act(T[1][:], T[1][:], AF.Exp, [tk], [tk], scale=-0.5)
                        emit_m("dve", "scalar_tensor_tensor", dict(out=T[0][:], in0=o_sb[:, h, :], scalar=hgain_s[:, h:h + 1], in1=T[1][:], op0=ALU.mult, op1=ALU.mult), reads=[tk, "osb"] + CK, writes=[tk])
                        emit_m("dve", "tensor_tensor", dict(out=oT_hg[:, h, :], in0=T[0][:], in1=gsb[:, h, :], op=ALU.mult), reads=[tk, ("gs", h)], writes=["oT_hg"])
                add(None, fin)

        def hg_tile(mode, tok0, xcol0, ntok=TT):
            if mode == "meta":
                hg_pre(xcol0, ntok, [0])
            elif mode == "pre":
                hg_pre(xcol0, ntok, [0, 1])
            else:
                hg_main(mode, tok0, xcol0)

        def attn_kv(xcol0, ntok, kdst, vdst, vkey, kkey):
            def kd(j, pb, pk, n0, n):
                copy_op(evac_eng(), kdst[:, j, n0:n0 + n], pb[:, :n], [pk], [kkey])
            for n0 in range(0, ntok, 512):
                n = min(512, ntok - n0)
                proj_fm(piece_in(0, "g1k", kd=True), lambda kc, n0=n0, n=n: xT[:, kc, xcol0 + n0: xcol0 + n0 + n], n,
                        lambda j, pb, pk, n0=n0, n=n: kd(j, pb, pk, n0, n), "xT")
            nsub = (ntok + 127) // 128
            msz = min(128, ntok)

            def vd(s, pb, pk):
                src = pb[:msz, 256:512].rearrange("p (g d) -> p g d", g=4)
                if ntok == 16:
                    d0, d1 = vdst[:msz, :, 0, 0:64], vdst[:msz, :, 1, 64:128]
                else:
                    d0, d1 = vdst[:msz, s, :, 0, 0:64], vdst[:msz, s, :, 1, 64:128]
                copy_op("act", d0, src, [pk], [vkey])
                copy_op("dve", d1, src, [pk], [vkey])
            proj_tm(piece_in(C_AV - 256, "g1k"), lambda kc, s: xT[:, kc, xcol0 + s * 128: xcol0 + s * 128 + msz], nsub, vd, "xT", msize=msz)

        slopes = [float(2.0 ** (-8.0 * (h + 1) / 16.0)) for h in range(16)]

        def attn_tile(ti):
            for half in range(2):
                def qd(j, pb, pk, half=half):
                    copy_op("act", qTa[:, 0, half * 4 + j, :], pb[:, :], [pk, "rowm8"], ["qTa"], scale=rowm8[:, 0:1])
                    copy_op("dve", qTa[:, 1, half * 4 + j, :], pb[:, :], [pk, "rowm8"], ["qTa"], scale=rowm8[:, 1:2])
                proj_fm(piece_in(C_AQ + half * 512, "g3"), lambda kc: xT[:, kc, 128:128 + TT], TT, qd, "xT")
            add(None, lambda slot: P.emit("pool", lambda e: e.memset(vat[:], 0.0), writes=["vat"]))
            add(None, lambda slot: dma("pool", onesc[:], onesc_d[:, :, :], "onesc", writes=["onesc"]))
            attn_kv(0, 768, kTd, vat, "vat", "kTd")

            def blocks(slot):
                its = [(qb, g, hl) for qb in range(4) for g in range(4) for hl in range(2)]

                def A(n):
                    qb, g, hl = its[n]
                    p2 = (n % 2) * 2
                    rq = qTa[:, hl, 2 * g:2 * g + 2, qb * 128:(qb + 1) * 128]
                    for kb in range(3):
                        bank = p2 + (kb // 2)
                        co = (kb % 2) * 256
                        mm(ps[bank][:, co:co + 256].rearrange("p (a b) -> p a b", a=2), kTd[:, g, (qb + kb) * 128:(qb + kb + 1) * 128], rq, True, True, ["kTd", "qTa"], [("ps", bank)])
                    mm(ps[p2 + 1][:, 256:512].rearrange("p (a b) -> p a b", a=2), kmT[:, g, :], rq, True, True, ["kmT", "qTa"], [("ps", p2 + 1)])

                def B(n):
                    qb, g, hl = its[n]
                    p2 = (n % 2) * 2
                    first = (ti == 0 and qb == 0)
                    last = (ti == NT - 1 and qb == 3)
                    pi = (qb * 4 + g) % 2
                    pt = PT[pi]
                    sc = scs[n % 2]
                    sk = ("scs", n % 2)
                    for kb in range(3):
                        dsel = kb
                        if kb == 0 and first:
                            dsel = 3
                        if kb == 2 and last:
                            dsel = 4
                        bank = p2 + (kb // 2)
                        co = (kb % 2) * 256
                        for c in range(2):
                            head = 4 * g + 2 * c + hl
                            emit_m("dve", "scalar_tensor_tensor", dict(out=sc[:, kb, c * 128:(c + 1) * 128], in0=dtb[:, dsel, :], scalar=slopes[head],
                                                                        in1=ps[bank][:, co + c * 128: co + (c + 1) * 128], op0=ALU.mult, op1=ALU.add),
                                   reads=[("ps", bank)] + CK, writes=[sk])
                    act(pt[:, 0:3, hl * 256:(hl + 1) * 256], sc[:, :, :], AF.Exp, [sk], [("PT", pi, hl)])
                    act(pt[:, 3, hl * 256:(hl + 1) * 256], ps[p2 + 1][:, 256:512], AF.Exp, [("ps", p2 + 1)], [("PT", pi, hl)])

                def C(n):
                    qb, g, hl = its[n]
                    pi = (qb * 4 + g) % 2
                    pt = PT[pi]
                    ptk = ("PT", pi, hl)
                    nb = 4 + 2 * pi
                    for kb in range(4):
                        if kb < 3:
                            lv = vat[:, qb + kb, g, hl, :]
                        else:
                            lv = vm[:, g, hl, :]
                        mm(ps[nb][:, 0:256], lv, pt[:, kb, hl * 256:(hl + 1) * 256], hl == 0 and kb == 0, hl == 1 and kb == 3, ["vat", "vm", ptk], [("ps", nb)])
                    for kb in range(4):
                        lo = onesc[:, (0 if kb < 3 else 2) + hl, :]
                        mm(ps[nb + 1][:, 0:256], lo, pt[:, kb, hl * 256:(hl + 1) * 256], hl == 0 and kb == 0, hl == 1 and kb == 3, ["onesc", ptk], [("ps", nb + 1)])
                    if hl == 1:
                        rc = rec[pi]
                        rk = ("rec", pi)
                        for c in range(2):
                            ch = 2 * g + c
                            act(rc[:, c * 128:(c + 1) * 128], ps[nb + 1][:, c * 128:(c + 1) * 128], AF.Ln, [("ps", nb + 1), "esink"], [rk], bias=esink[:, ch:ch + 1])
                        act(rc[:], rc[:], AF.Exp, [rk], [rk], scale=-1.0)
                        for c in range(2):
                            ch = 2 * g + c
                            emit_m("dve", "tensor_tensor", dict(out=oT_att[:, ch, qb * 128:(qb + 1) * 128], in0=ps[nb][:, c * 128:(c + 1) * 128], in1=rc[:, c * 128:(c + 1) * 128], op=ALU.mult),
                                   reads=[("ps", nb), rk], writes=["oT_att"])

                N = len(its)
                A(0)
                B(0)
                for n in range(N):
                    if n + 1 < N:
                        A(n + 1)
                        B(n + 1)
                    C(n)
            add(None, blocks)

        def merge_tile():
            xv = lambda kc: xT[:, kc, 128:128 + TT]
            for ng in range(4):
                def fa(slot, ng=ng):
                    pass
                st = {}

                def ga_item(slot, ng=ng, st=st):
                    st["ga"] = slot
                pa_ = piece_in(C_GA + ng * 512, "g4")
                add(pa_, ga_item)

                def gb_item(slot, ng=ng, st=st):
                    st["gb"] = slot
                pb_ = piece_in(C_GB + ng * 512, "g4")
                add(pb_, gb_item)

                def p_item(slot, ng=ng, st=st):
                    sa, sbb, sp_ = st["ga"], st["gb"], slot
                    for j in range(4):
                        nch = ng * 4 + j
                        cj = slice(j * 128, (j + 1) * 128)
                        b0 = 4 * (j % 2)
                        for kc in range(KC):
                            mm(ps[b0][:, :], wsl[sa][:, kc, cj], xv(kc), kc == 0, kc == KC - 1, [("w", sa), "xT"], [("ps", b0)])
                        for kc in range(KC):
                            mm(ps[b0 + 1][:, :], wsl[sbb][:, kc, cj], xv(kc), kc == 0, kc == KC - 1, [("w", sbb), "xT"], [("ps", b0 + 1)])
                        for kc in range(8):
                            mm(ps[b0 + 2][:, :], wsl[sp_][:, kc, cj], oT_hg[:, kc, :], kc == 0, kc == 7, [("w", sp_), "oT_hg"], [("ps", b0 + 2)])
                        for kc in range(8):
                            mm(ps[b0 + 3][:, :], wsl[sp_][:, 8 + kc, cj], oT_att[:, kc, :], kc == 0, kc == 7, [("w", sp_), "oT_att"], [("ps", b0 + 3)])
                        m0 = mg[(j % 2) * 4: (j % 2) * 4 + 4]
                        mk = ("mg", j % 2)
                        for gi in range(2):
                            act(m0[gi][:], ps[b0 + gi][:, :], AF.Exp, [("ps", b0 + gi)], [mk], scale=-1.0)
                            act(m0[gi][:], m0[gi][:], AF.Ln, [mk], [mk], bias=1.0)
                            act(m0[gi][:], m0[gi][:], AF.Exp, [mk], [mk], scale=-1.0)
                        emit_m("dve", "tensor_tensor", dict(out=m0[2][:], in0=m0[0][:], in1=ps[b0 + 2][:, :], op=ALU.mult), reads=[mk, ("ps", b0 + 2)], writes=[mk])
                        emit_m("dve", "tensor_tensor", dict(out=m0[3][:], in0=m0[1][:], in1=ps[b0 + 3][:, :], op=ALU.mult), reads=[mk, ("ps", b0 + 3)], writes=[mk])
                        emit_m("pool", "tensor_tensor", dict(out=mergedT[:, nch, :], in0=m0[2][:], in1=m0[3][:], op=ALU.add), reads=[mk], writes=["mergedT"])
                add({"src": wb_p[:, ng * 512:(ng + 1) * 512].rearrange("(kc p) n -> p kc n", p=128), "wkey": ("wb", "g4"), "nk": KC}, p_item)
                add(pa_, lambda slot: None)
                add(pb_, lambda slot: None)

        def tm_proj_norm(srcT, skey, wb, grp, nkp, zb, zkey, gsel, resid_load, out_store, ti):
            add(None, lambda slot: dma("pool", gpo[:], gpost[gsel:gsel + 1, :].partition_broadcast(128), "gpo", writes=["gpo"]))
            for ng in range(4):
                for kp in range(nkp):
                    def item(slot, ng=ng, kp=kp):
                        for s in range(4):
                            bk = s + (4 * (ng % 2))
                            for kc in range(KC):
                                kk = kp * KC + kc
                                mm(ps[bk][:, :], srcT[:, kk, s * 128:(s + 1) * 128], wsl[slot][:, kc, :], kk == 0, kk == nkp * KC - 1, [("w", slot), skey], [("ps", bk)])
                            if kp == nkp - 1:
                                copy_op(evac_eng(), zb[:, s, ng * 512:(ng + 1) * 512], ps[bk][:, :], [("ps", bk)], [(zkey, s)])
                    add(piece_rows(wb, kp * KC * 128, ng * 512, grp), item)

            def fin(slot):
                for s in range(4):
                    b = s % 2
                    sc = stat[:, 4 + b:5 + b]
                    junk = junkb if zkey == "z2" else xsf2[b]
                    if resid_load:
                        dma("pool", x1[:, s, :], x_ext[128 + ti * TT + s * 128: 128 + ti * TT + (s + 1) * 128, :], f"x1l{s}", writes=[("x1", s)])
                    act(junk[:], zb[:, s, :], AF.Square, [(zkey, s)], [("junkb" if zkey == "z2" else ("xsf2", b)), ("st2", b)], accum_out=sc)
                    act(sc, sc, AF.Ln, [("st2", b)], [("st2", b)], scale=1.0 / D, bias=EPS)
                    act(sc, sc, AF.Exp, [("st2", b)], [("st2", b)], scale=-0.5)
                    emit_m("dve", "scalar_tensor_tensor", dict(out=zb[:, s, :], in0=zb[:, s, :], scalar=sc, in1=gpo[:], op0=ALU.mult, op1=ALU.mult),
                           reads=[(zkey, s), ("st2", b), "gpo"], writes=[(zkey, s)])
                    emit_m("pool", "tensor_tensor", dict(out=x1[:, s, :], in0=x1[:, s, :], in1=zb[:, s, :], op=ALU.add), reads=[(zkey, s), ("x1", s)], writes=[("x1", s)])
                    if out_store:
                        dma("pool", y[ti * TT + s * 128: ti * TT + (s + 1) * 128, :], x1[:, s, :], f"yst{s}", reads=[("x1", s)])
            add(None, fin)

        def ffn1_tile():
            for pc in range(16):
                def dst(j, pb, pk, pc=pc):
                    r = rtmp[j % 4]
                    rk = ("rtmp", j % 4)
                    act(r[:], pb[:, :], AF.Relu, [pk], [rk])
                    emit_m("pool", "tensor_tensor", dict(out=uT[:, pc * 4 + j, :], in0=r[:], in1=r[:], op=ALU.mult), reads=[rk], writes=["uT"])
                proj_fm(piece_rows(wb_ff1, 0, pc * 512, "g6"), lambda kc: h2T[:, kc, :], TT, dst, "h2T", nb8=8)

        prep(meta, 1, 128, 0, xT, "xT")
        add_barrier()
        dump("xT_meta", xT[:, :, 128:144], ["xT"])
        set_c1(None)
        attn_kv(128, 16, kmT, vm, "vm", "kmT")
        hg_tile("meta", 0, 128, ntok=16)
        add_barrier()
        dump("kmT", kmT[:], ["kmT"])
        dump("vm", vm[:], ["vm"])
        dump("oml", oml[:], ["oml"])
        dump("S_meta", Sst[:], [("S", d_, h_) for d_ in range(2) for h_ in range(8)])
        def pre_rows(t):
            return x_pre[t * TT:(t + 1) * TT, :] if t < NPRE else None

        if NPRE > 0:
            cl, pre_issue = prep_light(pre_rows(0), xTL[0], "xTL0", pre_rows(1))
            add(None, lambda slot: pre_issue())
            for f in cl:
                add(None, lambda slot, f=f: f())
        for pt_ in range(NPRE):
            nxt = prep_light(pre_rows(pt_ + 1), xTL[(pt_ + 1) % 2], f"xTL{(pt_ + 1) % 2}", pre_rows(pt_ + 2))[0] if pt_ + 1 < NPRE else []
            set_c1(pt_ * 2)
            hg_pre(0, TT, [0, 1], xb=(xTL[pt_ % 2], f"xTL{pt_ % 2}"), inter=nxt)
        add_barrier()
        add(None, lambda slot: emit_casts(1000))
        dump("S_pre", Sst[:], [("S", d_, h_) for d_ in range(2) for h_ in range(8)])
        set_c1(None)
        STOP = _os.environ.get("STOP", "")
        for ti in (range(NT - 1, -1, -1) if STOP != "pre" else []):
            prep(x_ext[128 + ti * TT: 128 + (ti + 1) * TT, :], 4, 128, 0, xT, "xT")
            add_barrier()
            hg_tile("bwd", ti * TT, 128)
            add_barrier()
        if dbg:
            add_barrier()
            add(None, lambda slot: dma("pool", o_sb[:], ob_d[:, :, 0:TT].rearrange("h p t -> p h t"), "osb", reads=[("ob_d", 0)], writes=["osb"]))
            for h_ in range(8):
                dump(f"ob{h_}", o_sb[:, h_, :], ["osb"])
            add_barrier()
        for ti in (range(NT) if STOP == "" else []):
            prep(x_ext[ti * TT: ti * TT + 768, :], 6, 0, 0, xT, "xT")
            add_barrier()
            attn_tile(ti)
            add_barrier()
            if ti == 0:
                dump("xT0", xT[:, :, :], ["xT"])
                dump("oT_att", oT_att[:], ["oT_att"])
            hg_tile("fwd", ti * TT, 128)
            add_barrier()
            if ti == 0:
                dump("oT_hg", oT_hg[:], ["oT_hg"])
            merge_tile()
            add_barrier()
            if ti == 0:
                dump("mergedT", mergedT[:], ["mergedT"])
            tm_proj_norm(mergedT, "mergedT", wb_out, "g5", 1, zbuf, "z", 0, True, False, ti)
            if ti == 0:
                dump("x1", x1[:], [("x1", s_) for s_ in range(4)])
            prep(x1, 4, 0, 1, h2T, "h2T", rows_key="x1", xs=xsf2)
            add_barrier()
            if ti == 0:
                dump("h2T", h2T[:], ["h2T"])
            ffn1_tile()
            add_barrier()
            tm_proj_norm(uT, "uT", wb_ff2, "g7", 4, z2, "z2", 1, False, True, ti)
            add_barrier()

        order = []
        last_use = {}
        for idx, (p, f) in enumerate(items):
            if isinstance(p, dict):
                if id(p) not in last_use:
                    order.append(p)
                last_use[id(p)] = idx
        pos = {id(p): k for k, p in enumerate(order)}
        state = {"next": 0}

        def issue_load(k):
            p = order[k]
            slot = k % NSLOT
            p["slot"] = slot
            dma("sp", wsl[slot][:, :p["nk"], :], p["src"], f"w{slot}", reads=[p["wkey"]], writes=[("w", slot)])

        for idx, (p, f) in enumerate(items):
            if p == "barrier":
                P.barrier()
                continue
            while state["next"] < len(order) and (state["next"] < NSLOT or last_use[id(order[state["next"] - NSLOT])] < idx):
                issue_load(state["next"])
                state["next"] += 1
            if isinstance(p, dict):
                assert pos[id(p)] < state["next"], "weight slot deadlock"
                f(p["slot"])
            else:
                f(None)

        P.replay(nc, sems, dsem, final_eng="sp")
    return nc


def const_tables():
    j = np.arange(128)[:, None].astype(np.float32)
    r = np.arange(128)[None, :].astype(np.float32)
    dl = np.where(j >= r, -(r + 128 - j), NEG)
    dc = -np.abs(j - r)
    dr = np.where(j <= r, -(j + 128 - r), NEG)
    edge = np.full((128, 128), NEG, np.float32)
    dtab = np.stack([dl, dc, dr, edge, edge], axis=1).astype(np.float32)
    s = np.arange(128)[:, None]
    t = np.arange(128)[None, :]
    same = (s // 64) == (t // 64)
    s3 = np.arange(128)[:, None, None, None]
    d3 = np.arange(2)[None, :, None, None]
    p3 = np.arange(2)[None, None, :, None]
    t3 = np.arange(64)[None, None, None, :]
    hgmask = ((s3 // 64 == p3) & np.where(d3 == 0, (s3 % 64) <= t3, (s3 % 64) >= t3)).astype(np.float32)
    scan = np.ones((128, TT), np.float32)
    scan[:, ::64] = 0.0
    return dtab, hgmask, scan, np.eye(128, dtype=np.float32)


def core_inputs(seq_x, start, ntok, is_first, is_last, npre, meta_tokens, shared):
    S = seq_x.shape[0]
    x_ext = np.zeros((ntok + 256, D), np.float32)
    lo = max(0, start - 128)
    hi = min(S, start + ntok + 128)
    x_ext[128 - (start - lo): 128 - (start - lo) + (hi - lo)] = seq_x[lo:hi]
    x_pre = np.zeros((max(npre, 1) * TT, D), np.float32)
    premask = np.zeros((128, max(npre, 1) * 2), np.float32)
    npf = start // TT
    nsf = (S - start - ntok) // TT
    assert npf + nsf <= npre
    for i in range(npf):
        x_pre[i * TT:(i + 1) * TT] = seq_x[i * TT:(i + 1) * TT]
        premask[:, 2 * i] = 1.0
    for i in range(nsf):
        t0 = S - (i + 1) * TT
        x_pre[(npf + i) * TT:(npf + i + 1) * TT] = seq_x[t0:t0 + TT]
        premask[:, 2 * (npf + i) + 1] = 1.0
    dtab, hgmask, scan, ident = shared["tables"]
    dtab = dtab.copy()
    if not is_first:
        dtab[:, 3] = dtab[:, 0]
    if not is_last:
        dtab[:, 4] = dtab[:, 2]
    m = dict(shared["weights"])
    rowmask = np.zeros((128, 2), np.float32)
    rowmask[:64, 0] = 1.0
    rowmask[64:, 1] = 1.0
    onesc = np.zeros((128, 4, 128), np.float32)
    onesc[:, 0, 0:64] = 1.0
    onesc[:, 1, 64:128] = 1.0
    onesc[:16, 2, 0:64] = 1.0
    onesc[:16, 3, 64:128] = 1.0
    m.update(onesc=onesc)
    m.update(x_ext=x_ext, x_pre=x_pre, premask=premask, dtab=dtab, hgmask=hgmask, scanmask=scan, identf=ident, rowmask=rowmask)
    return m


def shared_inputs(meta_tokens, w_in, w_proj_hg, w_proj_att, w_out, w_ff1, w_ff2, g_pre_mix, g_post_mix, g_pre_ff, g_post_ff,
                  lb_logits, hg_out_gain, attn_sink):
    f = lambda a: np.ascontiguousarray(np.asarray(a, dtype=np.float32))
    meta = np.zeros((128, D), np.float32)
    meta[:16] = f(meta_tokens)
    gpre = np.stack([f(g_pre_mix)[0].reshape(KC, 128).T, f(g_pre_ff)[0].reshape(KC, 128).T], axis=1)
    gpost = np.stack([f(g_post_mix)[0], f(g_post_ff)[0]], axis=0)
    lbl = f(lb_logits).reshape(2, 2, 8, 128).transpose(3, 0, 1, 2)
    hgain = f(hg_out_gain)[0].reshape(8, 128).T
    sk = f(attn_sink)[0]
    sink = np.zeros((128, 8), np.float32)
    sink[:64] = sk[0::2][None, :]
    sink[64:] = sk[1::2][None, :]
    w = dict(meta=meta, w_in=f(w_in)[0], w_phg=f(w_proj_hg)[0], w_patt=f(w_proj_att)[0], w_out=f(w_out)[0], w_ff1=f(w_ff1)[0],
             w_ff2=f(w_ff2)[0], gpre=f(gpre), gpost=f(gpost), lbl=f(lbl), hgain=f(hgain), sink=sink)
    return {"weights": w, "tables": const_tables()}


_CACHE = {}


def run_layout(seqs, core_plan, NT, NPRE, shared, full=False):
    key = (NT, NPRE)
    if key not in _CACHE:
        _CACHE[key] = build(NT, NPRE)
    nc = _CACHE[key]
    in_maps = []
    for (si, start) in core_plan:
        S = seqs[si].shape[0]
        in_maps.append(core_inputs(seqs[si], start, NT * TT, start == 0, start + NT * TT == S, NPRE, None, shared))
    res = run_bass_kernel_spmd(nc, in_maps, core_ids=list(range(len(core_plan))))
    if full:
        return res.results
    return [r["y"] for r in res.results]


def kernel(x_prompt, x_sample, meta_tokens, w_in, w_proj_hg, w_proj_att, w_out, w_ff1, w_ff2,
           g_pre_mix, g_post_mix, g_pre_ff, g_post_ff, lb_logits, hg_out_gain, attn_sink):
    x_prompt = np.asarray(x_prompt, dtype=np.float32)
    x_sample = np.asarray(x_sample, dtype=np.float32)
    shared = shared_inputs(meta_tokens, w_in, w_proj_hg, w_proj_att, w_out, w_ff1, w_ff2, g_pre_mix, g_post_mix,
                           g_pre_ff, g_post_ff, lb_logits, hg_out_gain, attn_sink)
    seqs = [x_prompt[b] for b in range(4)] + [x_sample[0]]
    plan = [(b, 0) for b in range(4)] + [(4, c * 4096) for c in range(4)]
    outs = run_layout(seqs, plan, 8, 24, shared)
    y_prompt = np.stack(outs[:4], axis=0)
    y_sample = np.concatenate(outs[4:], axis=0)[None]
    return (y_prompt, y_sample)
```

```python
import numpy as np
from contextlib import ExitStack
import concourse.bass as bass
import concourse.mybir as mybir
from concourse.bass_utils import run_bass_kernel_spmd

F32 = mybir.dt.float32
BF16 = mybir.dt.bfloat16
AF = mybir.ActivationFunctionType
ALU = mybir.AluOpType

D = 2048
KC = 16
NIN = 10752
DFF = 8192
TT = 512
EPS = 1e-6
C_Q, C_I, C_FF, C_FB, C_G, C_AQ, C_AK, C_AV, C_GA, C_GB = 0, 1024, 2048, 3072, 4096, 5120, 6144, 6400, 6656, 8704
NEG = -1.0e9
ENGS = ("pe", "act", "dve", "pool", "sp")


class Op:
    __slots__ = ("eng", "fn", "waits", "idx", "marked", "dma_sem", "dma_val", "val")

    def __init__(self, eng, fn):
        self.eng = eng
        self.fn = fn
        self.waits = []
        self.idx = None
        self.marked = False
        self.dma_sem = None
        self.dma_val = None
        self.val = None


class Prog:
    def __init__(self, same_engine_sync=("act", "dve", "pool")):
        self.ops = {e: [] for e in ENGS}
        self.last_writes = {}
        self.readers = {}
        self.known = {e: {} for e in ENGS}
        self.dma_counts = {}
        self.same_engine_sync = set(same_engine_sync)
        self.pending = {e: [] for e in ENGS}

    @staticmethod
    def _stream(op):
        return ("dma", op.dma_sem) if op.dma_sem is not None else ("eng", op.eng)

    def _need(self, op, dep):
        if dep is None or dep is op:
            return
        st = self._stream(dep)
        if dep.dma_sem is not None:
            v = dep.dma_val
        else:
            if dep.eng == op.eng and dep.eng not in self.same_engine_sync:
                return
            v = dep.idx
        k = self.known[op.eng]
        if k.get(st, -1) >= v:
            return
        k[st] = v
        op.waits = [w for w in op.waits if self._stream(w) != st]
        op.waits.append(dep)

    def emit(self, eng, fn, reads=(), writes=(), dma_sem=None, ndma=1):
        op = Op(eng, fn)
        op.idx = len(self.ops[eng])
        if dma_sem is not None:
            c = self.dma_counts.get(dma_sem, 0) + ndma
            self.dma_counts[dma_sem] = c
            op.dma_sem = dma_sem
            op.dma_val = 16 * c
        for d in self.pending[eng]:
            self._need(op, d)
        self.pending[eng] = []
        for r in reads:
            for d in self.last_writes.get(r, {}).values():
                self._need(op, d)
        for w in writes:
            for d in self.last_writes.get(w, {}).values():
                self._need(op, d)
            for rd in self.readers.get(w, ()):
                self._need(op, rd)
        for r in reads:
            self.readers.setdefault(r, []).append(op)
        for w in writes:
            self.last_writes.setdefault(w, {})[self._stream(op)] = op
            self.readers[w] = []
        self.ops[eng].append(op)
        return op

    def barrier(self, engs=("pe", "act", "dve", "pool")):
        lasts = [self.ops[e][-1] for e in engs if self.ops[e]]
        for e in engs:
            self.pending[e] = list(lasts)

    def replay(self, nc, sems, dma_sems, final_eng="sp"):
        for e in ENGS:
            for op in self.ops[e]:
                for d in op.waits:
                    if d.dma_sem is None:
                        d.marked = True
        for e in ENGS:
            c = 0
            for op in self.ops[e]:
                if op.dma_sem is None and op.marked:
                    c += 1
                    op.val = c
        handles = {"pe": "tensor", "act": "scalar", "dve": "vector", "pool": "gpsimd", "sp": "sync"}
        finals = [(dma_sems[n], 16 * c) for n, c in self.dma_counts.items()]
        with nc.Block() as block:
            for e in ENGS:
                ops = self.ops[e]

                def body(engine, ops=ops, e=e):
                    for op in ops:
                        for d in op.waits:
                            if d.dma_sem is not None:
                                engine.wait_ge(dma_sems[d.dma_sem], d.dma_val)
                            else:
                                engine.wait_ge(sems[d.eng], d.val)
                        ins = op.fn(engine)
                        if op.dma_sem is None and op.marked:
                            ins.then_inc(sems[e], 1)
                    if e == final_eng:
                        for s, v in finals:
                            engine.wait_ge(s, v)
                getattr(block, handles[e])(body)


def build(NT, NPRE, dbg=False):
    NTOK = NT * TT
    nc = bass.Bass("TRN2", target_bir_lowering=False)
    es = ExitStack()
    import os as _os
    P = Prog(same_engine_sync=tuple(x for x in _os.environ.get("SES", "act,dve,pool").split(",") if x))
    dma_sem_names = []

    def din(name, shape, dt=F32):
        return nc.dram_tensor(name, list(shape), dt, kind="ExternalInput").ap()

    x_ext = din("x_ext", [NTOK + 256, D])
    x_pre = din("x_pre", [max(NPRE, 1) * TT, D])
    meta = din("meta", [128, D])
    w_in = din("w_in", [D, NIN])
    w_phg = din("w_phg", [1024, D])
    w_patt = din("w_patt", [1024, D])
    w_out = din("w_out", [D, D])
    w_ff1 = din("w_ff1", [D, DFF])
    w_ff2 = din("w_ff2", [DFF, D])
    gpre = din("gpre", [128, 2, KC])
    gpost = din("gpost", [2, D])
    lbl = din("lbl", [128, 2, 2, 8])
    hgain = din("hgain", [128, 8])
    sink = din("sink", [128, 8])
    premask = din("premask", [128, max(NPRE, 1) * 2])
    dtab = din("dtab", [128, 5, 128])
    hgmask = din("hgmask", [128, 2, 2, 64])
    rowmask = din("rowmask", [128, 2])
    scanmask = din("scanmask", [128, TT])
    identf = din("identf", [128, 128])
    onesc_d = din("onesc", [128, 4, 128])
    y = nc.dram_tensor("y", [NTOK, D], F32, kind="ExternalOutput").ap()
    wb_in = nc.dram_tensor("wb_in", [D, NIN], BF16, kind="Internal").ap()
    wb_kd = nc.dram_tensor("wb_kd", [D, 512], BF16, kind="Internal").ap()
    wb_p = nc.dram_tensor("wb_p", [2048, D], BF16, kind="Internal").ap()
    wb_out = nc.dram_tensor("wb_out", [D, D], BF16, kind="Internal").ap()
    wb_ff1 = nc.dram_tensor("wb_ff1", [D, DFF], BF16, kind="Internal").ap()
    wb_ff2 = nc.dram_tensor("wb_ff2", [DFF, D], BF16, kind="Internal").ap()
    ob_d = nc.dram_tensor("ob_d", [8, 128, NTOK], F32, kind="Internal").ap()
    dbg_out = {}

    with es:
        def sb(name, shape, dt):
            return es.enter_context(nc.sbuf_tensor(name, list(shape), dt))

        def pst(name, shape, dt=F32):
            return es.enter_context(nc.psum_tensor(name, list(shape), dt))

        NSLOT = 3
        wsl = [sb(f"wsl{i}", [128, KC, 512], BF16) for i in range(NSLOT)]
        Sst = sb("Sst", [128, 2, 8, 128], F32)
        ident_f = sb("ident_f", [128, 128], F32)
        ident_b = sb("ident_b", [128, 128], BF16)
        ones_b = sb("ones_b", [128, 128], BF16)
        hgm = sb("hgm", [128, 2, 2, 64], F32)
        rowm = sb("rowm", [128, 2], F32)
        scm = sb("scm", [128, TT], F32)
        dtb = sb("dtb", [128, 5, 128], F32)
        gpre_s = sb("gpre_s", [128, 2, KC], F32)
        lb_s = sb("lb_s", [128, 2, 2, 8], F32)
        oml = sb("oml", [128, 2, 8], F32)
        c1t = sb("c1t", [128, 2, 8], F32)
        nc1t = sb("nc1t", [128, 2, 8], F32)
        hgain_s = sb("hgain_s", [128, 8], F32)
        esink = sb("esink", [128, 8], F32)
        pmask = sb("pmask", [128, max(NPRE, 1) * 2], F32)
        stat = sb("stat", [128, 16], F32)
        kmT = sb("kmT", [128, 4, 128], BF16)
        vm = sb("vm", [128, 4, 2, 128], BF16)
        rowm8 = sb("rowm8", [128, 2], F32)
        arena = sb("arena", [128, 144128], mybir.dt.uint8)

        def carve(off_kb, shape, dt):
            nbytes = int(np.prod(shape[1:])) * (2 if dt == BF16 else 4)
            off = int(off_kb * 1024)
            assert off + nbytes <= 144128, (off_kb, shape)
            v = arena[:, off:off + nbytes].bitcast(dt)
            if len(shape) == 2:
                return v
            names = " ".join(f"a{i}" for i in range(len(shape) - 1))
            kw = {f"a{i}": shape[i + 1] for i in range(len(shape) - 1)}
            return v.rearrange(f"p ({names}) -> p {names}", **kw)

        x1 = carve(0, [128, 4, D], F32)
        xrow = [carve(0, [128, D], F32), carve(8, [128, D], F32)]
        xsf = [carve(16, [128, D], F32), carve(24, [128, D], F32)]
        mergedT = carve(32, [128, KC, TT], BF16)
        xT = carve(48, [128, KC, 768], BF16)
        oT_att = carve(72, [128, 8, TT], BF16)
        oT_hg = carve(80, [128, 8, TT], BF16)
        TB = 88
        qTa = carve(TB, [128, 2, 8, TT], BF16)
        kTd = carve(TB + 16, [128, 4, 768], BF16)
        vat = carve(TB + 22, [128, 6, 4, 2, 128], BF16)
        PT = [carve(TB + 34 + 4 * i, [128, 4, 512], BF16) for i in range(2)]
        scs = [carve(TB + 42 + 3 * i, [128, 3, 256], F32) for i in range(2)]
        rec = [carve(TB + 48 + i, [128, 256], F32) for i in range(2)]
        onesc = carve(TB + 50, [128, 4, 128], BF16)
        mg = [carve(TB + 2 * i, [128, TT], F32) for i in range(8)]
        zbuf = carve(48, [128, 4, D], F32)
        h2T = carve(96, [128, KC, TT], BF16)
        xsf2 = [carve(112, [128, D], F32), carve(120, [128, D], F32)]
        gpo = carve(128, [128, D], F32)
        uT = carve(32, [128, 64, TT], BF16)
        rtmp = [carve(112 + 2 * i, [128, TT], F32) for i in range(4)]
        z2 = carve(96, [128, 4, D], F32)
        junkb = carve(136, [128, D], BF16)

        ps = [pst(f"ps{i}", [128, 512]) for i in range(8)]
        psb = ps[7][:, :].bitcast(BF16)

        sems = {e: es.enter_context(nc.semaphore("s_" + e)) for e in ENGS}
        dsem = {}

        def dma(eng, out, in_, sem, reads=(), writes=()):
            if sem not in dsem:
                dsem[sem] = es.enter_context(nc.semaphore("d_" + sem))
            s = dsem[sem]
            return P.emit(eng, lambda e: e.dma_start(out=out, in_=in_).then_inc(s, 16),
                          reads=reads, writes=writes, dma_sem=sem)

        def dump(name, ap, keys):
            if not dbg:
                return
            t = nc.dram_tensor("dbg_" + name, list(ap.shape), F32, kind="ExternalOutput").ap()
            idx = tuple(slice(None) for _ in ap.shape)
            add(None, lambda slot: dma("pool", t[idx], ap, "dbg_" + name, reads=keys))

        def emit_m(eng, meth, kw, reads=(), writes=()):
            return P.emit(eng, lambda e: getattr(e, meth)(**kw), reads=reads, writes=writes)

        rr = {"ev": 0}

        def evac_eng():
            rr["ev"] ^= 1
            return "act" if rr["ev"] else "dve"

        def copy_op(eng, out, in_, reads, writes, scale=None):
            if eng == "act":
                if scale is None:
                    return emit_m("act", "activation", dict(out=out, in_=in_, func=AF.Copy), reads=reads, writes=writes)
                return emit_m("act", "activation", dict(out=out, in_=in_, func=AF.Copy, scale=scale), reads=reads, writes=writes)
            if scale is None:
                return P.emit(eng, lambda e: e.tensor_copy(out=out, in_=in_), reads=reads, writes=writes)
            return P.emit(eng, lambda e: e.tensor_scalar(out=out, in0=in_, scalar1=scale, scalar2=None, op0=ALU.mult), reads=reads, writes=writes)

        def act(out, in_, func, reads, writes, **kw):
            return emit_m("act", "activation", dict(out=out, in_=in_, func=func, **kw), reads=reads, writes=writes)

        def mm(out, lhsT, rhs, start, stop, reads, writes, **kw):
            return P.emit("pe", lambda e: e.matmul(out, lhsT=lhsT, rhs=rhs, start=start, stop=stop, **kw), reads=reads, writes=writes)

        cload = [
            (ident_f[:], identf[:, :]), (hgm[:], hgmask[:, :, :, :]), (rowm[:], rowmask[:, :]), (scm[:], scanmask[:, :]), (dtb[:], dtab[:, :, :]),
            (gpre_s[:], gpre[:, :, :]), (lb_s[:], lbl[:, :, :, :]), (hgain_s[:], hgain[:, :]), (esink[:], sink[:, :]),
            (pmask[:], premask[:, :]),
        ]
        for i, (o, s) in enumerate(cload):
            dma("sp", o, s, "const", writes=[("c", i)])
        CK = [("c", i) for i in range(len(cload))]

        late_casts = []

        def cast(dst, src, grp):
            if grp.startswith("g1"):
                dma("pool", dst, src, "cast_" + grp, writes=[("wb", grp)])
            else:
                late_casts.append(lambda: dma("pool", dst, src, "cast_" + grp, writes=[("wb", grp)]))

        def emit_casts(n):
            for _ in range(min(n, len(late_casts))):
                late_casts.pop(0)()

        def cast_cols(c0, c1, grp):
            for c in range(c0, c1, 512):
                cast(wb_in[:, c:c + 512], w_in[:, c:c + 512], grp)

        cast_cols(C_I, C_I + 1024, "g1")
        cast_cols(C_FF, C_FF + 2048, "g1")
        cast_cols(C_Q, C_Q + 1024, "g2")
        cast_cols(C_G, C_G + 1024, "g2")
        cast_cols(C_AQ, C_AQ + 1024, "g3")
        cast_cols(C_AK, C_AK + 512, "g1k")
        for g in range(4):
            for dup in range(2):
                cast(wb_kd[:, g * 128 + dup * 64: g * 128 + dup * 64 + 64], w_in[:, C_AK + g * 64: C_AK + g * 64 + 64], "g1k")
        cast_cols(C_GA, C_GA + 4096, "g4")
        for r in range(0, 1024, 512):
            cast(wb_p[r:r + 512, :], w_phg[r:r + 512, :], "g4")
            cast(wb_p[1024 + r:1024 + r + 512, :], w_patt[r:r + 512, :], "g4")
        for r in range(0, D, 512):
            cast(wb_out[r:r + 512, :], w_out[r:r + 512, :], "g5")
        for c in range(0, DFF, 512):
            cast(wb_ff1[:, c:c + 512], w_ff1[:, c:c + 512], "g6")
        for r in range(0, DFF, 512):
            cast(wb_ff2[r:r + 512, :], w_ff2[r:r + 512, :], "g7")

        copy_op("dve", ident_b[:], ident_f[:], CK, ["identb"])
        P.emit("dve", lambda e: e.memset(ones_b[:], 1.0), writes=["ones"])
        emit_m("dve", "tensor_tensor", dict(out=oml[:], in0=lb_s[:, 1], in1=lb_s[:, 0], op=ALU.subtract), reads=CK, writes=["oml"])
        act(oml[:], oml[:], AF.Exp, ["oml"], ["oml"])
        act(oml[:], oml[:], AF.Ln, ["oml"], ["oml"], bias=1.0)
        act(oml[:], oml[:], AF.Exp, ["oml"], ["oml"], scale=-1.0)
        emit_m("dve", "tensor_scalar", dict(out=oml[:], in0=oml[:], scalar1=-1.0, scalar2=1.0, op0=ALU.mult, op1=ALU.add), reads=["oml"], writes=["oml"])
        act(esink[:], esink[:], AF.Exp, CK, ["esink"])
        P.emit("dve", lambda e: e.memset(Sst[:], 0.0), writes=["S"])
        P.emit("pool", lambda e: e.memset(kmT[:], 0.0), writes=["kmT"])
        P.emit("pool", lambda e: e.memset(vm[:], 0.0), writes=["vm"])
        emit_m("dve", "tensor_scalar", dict(out=rowm8[:], in0=rowm[:], scalar1=0.125, scalar2=None, op0=ALU.mult), reads=CK, writes=["rowm8"])

        items = []

        def piece_in(c0, grp, kd=False):
            src = (wb_kd if kd else wb_in)[:, c0:c0 + 512].rearrange("(kc p) n -> p kc n", p=128)
            return {"src": src, "wkey": ("wb", grp), "nk": KC}

        def piece_rows(wb, r0, c0, grp, nk=KC):
            return {"src": wb[r0:r0 + nk * 128, c0:c0 + 512].rearrange("(kc p) n -> p kc n", p=128), "wkey": ("wb", grp), "nk": nk}

        def add(piece, fn):
            items.append((piece, fn))

        def add_barrier():
            items.append(("barrier", None))

        def set_c1(col):
            def fn(slot):
                if col is None:
                    copy_op("dve", c1t[:], oml[:], ["oml"], ["c1"])
                else:
                    for d in range(2):
                        emit_m("dve", "tensor_scalar", dict(out=c1t[:, d], in0=oml[:, d], scalar1=pmask[:, col + d:col + d + 1], scalar2=None, op0=ALU.mult), reads=["oml"] + CK, writes=["c1"])
                emit_m("dve", "tensor_scalar", dict(out=nc1t[:], in0=c1t[:], scalar1=-1.0, scalar2=None, op0=ALU.mult), reads=["c1"], writes=["c1"])
            add(None, fn)

        def prep(src_rows, nsub, col0, gsel, dstT, dkey, rows_key=None, xr=None, xs=None):
            xr = xr or xrow
            xsk = "xsf2" if xs is not None else None
            xs = xs or xsf
            xkey = (lambda b: ("xsf2", b)) if xsk else (lambda b: ("x1", 2 + b))

            def fn(slot):
                if rows_key is None:
                    dma("act", xr[0][:], src_rows[0:128, :], "xrow0", writes=[("x1", 0)])
                for s in range(nsub):
                    b = s % 2
                    if rows_key is None:
                        if s + 1 < nsub:
                            dma("act", xr[1 - b][:], src_rows[(s + 1) * 128:(s + 2) * 128, :], f"xrow{1 - b}", writes=[("x1", 1 - b)])
                        src = xr[b][:]
                        rk = [("x1", b)]
                    else:
                        src = src_rows[:, s, :]
                        rk = [(rows_key, s)]
                    sc = stat[:, 2 * b:2 * b + 1]
                    act(xs[b][:], src, AF.Square, rk, [xkey(b), ("st", b)], accum_out=sc)
                    act(sc, sc, AF.Ln, [("st", b)], [("st", b)], scale=1.0 / D, bias=EPS)
                    act(sc, sc, AF.Exp, [("st", b)], [("st", b)], scale=-0.5)
                    emit_m("dve", "tensor_scalar", dict(out=xs[b][:], in0=src, scalar1=sc, scalar2=None, op0=ALU.mult), reads=rk + [("st", b)], writes=[xkey(b)])
                    for q4 in range(4):
                        pb = ps[q4]
                        for j in range(4):
                            kc = q4 * 4 + j
                            emit_m("pe", "transpose", dict(out=pb[:, j * 128:(j + 1) * 128], in_=xs[b][:, kc * 128:(kc + 1) * 128], identity=ident_f[:]),
                                   reads=[xkey(b)] + CK, writes=[("ps", q4)])
                        for j in range(4):
                            kc = q4 * 4 + j
                            copy_op(evac_eng(), dstT[:, kc, col0 + s * 128: col0 + (s + 1) * 128], pb[:, j * 128:(j + 1) * 128],
                                    [("ps", q4)] + CK, [dkey], scale=gpre_s[:, gsel, kc:kc + 1])
            add(None, fn)

        cnt8 = {"b": 0}

        xTL = [carve(48, [128, KC, TT], BF16), carve(64, [128, KC, TT], BF16)]
        xrL = [carve(40, [128, D], F32), carve(80, [128, D], F32)]
        junkL = carve(24, [128, D], BF16)

        def prep_light(src_rows, dstT, dkey, nxt_rows=None):
            def issue(rows, s):
                dma("act", xrL[s % 2][:], rows[s * 128:(s + 1) * 128, :], f"xrL{s % 2}", writes=[("xrL", s % 2)])

            def front(s):
                b = s % 2
                src = xrL[b][:]
                rk = [("xrL", b)]
                sc = stat[:, 8 + b:9 + b]
                act(junkL[:], src, AF.Square, rk, ["junkL", ("stL", b)], accum_out=sc)
                act(sc, sc, AF.Ln, [("stL", b)], [("stL", b)], scale=1.0 / D, bias=EPS)
                act(sc, sc, AF.Exp, [("stL", b)], [("stL", b)], scale=-0.5)
                emit_m("dve", "tensor_scalar", dict(out=src, in0=src, scalar1=sc, scalar2=None, op0=ALU.mult), reads=rk + [("stL", b)], writes=rk)

            def back(s):
                b = s % 2
                rk = [("xrL", b)]
                for q4 in range(4):
                    bkk = (3, 6)[q4 % 2]
                    pb = ps[bkk]
                    for j in range(4):
                        kc = q4 * 4 + j
                        P.emit("pe", (lambda pb=pb, j=j, kc=kc, b=b: (lambda e: e.transpose(out=pb[:, j * 128:(j + 1) * 128], in_=xrL[b][:, kc * 128:(kc + 1) * 128], identity=ident_f[:])))(),
                               reads=rk + CK, writes=[("ps", bkk)])
                    for j in range(4):
                        kc = q4 * 4 + j
                        copy_op(evac_eng(), dstT[:, kc, s * 128:(s + 1) * 128], pb[:, j * 128:(j + 1) * 128], [("ps", bkk)] + CK, [dkey], scale=gpre_s[:, 0, kc:kc + 1])

            def p0():
                front(0)

            def p1():
                back(0)
                issue(src_rows, 2)
                front(1)

            def p2():
                back(1)
                issue(src_rows, 3)
                front(2)

            def p3():
                back(2)
                front(3)

            def p4():
                back(3)
                if nxt_rows is not None:
                    issue(nxt_rows, 0)
                    issue(nxt_rows, 1)
            return [p0, p1, p2, p3, p4], (lambda: (issue(src_rows, 0), issue(src_rows, 1)))

        def proj_fm(piece, xTv, ncols_tok, dst_fn, xkey, ntile=4, nb8=4):
            def fn(slot):
                for j in range(ntile):
                    cnt8["b"] = (cnt8["b"] + 1) % nb8
                    bk = cnt8["b"]
                    for kc in range(KC):
                        mm(ps[bk][:, :ncols_tok], wsl[slot][:, kc, j * 128:(j + 1) * 128], xTv(kc), kc == 0, kc == KC - 1,
                           [("w", slot), xkey], [("ps", bk)])
                    dst_fn(j, ps[bk], ("ps", bk))
            add(piece, fn)

        def proj_tm(piece, xTv_sub, nsub, dst_fn, xkey, ncol=512, nk=KC, msize=128):
            def fn(slot):
                for s in range(nsub):
                    bk = 4 + (s % 3)
                    for kc in range(nk):
                        mm(ps[bk][:msize, :ncol], xTv_sub(kc, s), wsl[slot][:, kc, :ncol], kc == 0, kc == nk - 1,
                           [("w", slot), xkey], [("ps", bk)])
                    dst_fn(s, ps[bk], ("ps", bk))
            add(piece, fn)

        vpar = [carve(0, [128, 4, 1024], BF16), carve(8, [128, 4, 1024], BF16)]
        kinvT = carve(16, [128, 8, TT], BF16)
        qdecT = carve(24, [128, 8, TT], BF16)
        ktok = carve(32, [128, 8, TT], BF16)
        gsb = carve(40, [128, 8, TT], BF16)
        o_sb = carve(88, [128, 8, TT], F32)
        NTS = 3
        Tset = [[carve(104 + 10 * j + 2 * i, [128, TT], F32) for i in range(5)] for j in range(NTS)]
        scb = [carve(134 + i, [128, 8, 64], BF16) for i in range(2)]
        onesf = carve(134, [128, TT], F32)
        Sbf = [carve(136 + 2 * i, [128, 8, 128], BF16) for i in range(2)]
        decs = carve(140, [128, 8, 8], F32)
        cnt = {"bank": 0, "ts": 0}

        def nbank(m=4):
            cnt["bank"] = (cnt["bank"] + 1) % m
            return cnt["bank"]

        def sigm_neg(dst, src, rk, wk, n, sgn=1.0):
            act(dst[:, :n], src, AF.Exp, rk, [wk], scale=sgn)
            act(dst[:, :n], dst[:, :n], AF.Ln, [wk], [wk], bias=1.0)
            act(dst[:, :n], dst[:, :n], AF.Exp, [wk], [wk], scale=-1.0)

        deferred = []

        def defer(fn):
            deferred.append(fn)

        def flush(keep=0):
            while len(deferred) > keep:
                deferred.pop(0)()

        def proj_head(piece, j, xv, ntok, consume, xkey="xT", nbk=4, pre=None):
            def fn(slot):
                bk = nbank(nbk)
                for kc in range(KC):
                    mm(ps[bk][:, :ntok], wsl[slot][:, kc, j * 128:(j + 1) * 128], xv(kc), kc == 0, kc == KC - 1, [("w", slot), xkey], [("ps", bk)])
                flush(2)
                if pre is not None:
                    pre()
                consume(ps[bk], ("ps", bk))
            add(piece, fn)

        def hg_v(xvs, nsub, msz, par, xkey="xT"):
            for half in range(2):
                def dst(s, pb, pk, half=half):
                    if par:
                        copy_op("act", vpar[0][:, s, half * 512:(half + 1) * 512], pb[:, :], [pk] + CK, ["vpar"], scale=rowm[:, 0:1])
                        copy_op("dve", vpar[1][:, s, half * 512:(half + 1) * 512], pb[:, :], [pk] + CK, ["vpar"], scale=rowm[:, 1:2])
                    else:
                        copy_op(evac_eng(), vpar[0][:msz, s, half * 512:(half + 1) * 512], pb[:msz, :], [pk], ["vpar"])
                proj_tm(piece_in(C_I + half * 512, "g1"), xvs, nsub, dst, xkey, msize=msz)

        def hg_pre(xcol0, ntok, dirs, xb=None, inter=()):
            nsub = (ntok + 127) // 128
            msz = min(128, ntok)
            xt_, xkey = (xT, "xT") if xb is None else xb
            nbk = 4 if xb is None else 3
            inter = list(inter)
            nhead = [0]
            pend = []
            xv = lambda kc: xt_[:, kc, xcol0:xcol0 + ntok]
            xvs = lambda kc, s: xt_[:, kc, xcol0 + s * 128: xcol0 + s * 128 + msz]
            add(None, lambda slot: P.emit("pool", lambda e: e.memset(onesf[:], 1.0), writes=["onesf"]))
            if inter:
                add(None, lambda slot, f=inter.pop(0): f())
            hg_v(xvs, nsub, msz, False, xkey=xkey)
            fcols = {0: C_FF, 1: C_FB}
            for hp in range(2):
                for dr in dirs:
                    pf = piece_in(fcols[dr] + hp * 512, "g1")
                    for j in range(4):
                        h = hp * 4 + j

                        def consume(pb, pk, h=h, dr=dr):
                            r = cnt["ts"] = (cnt["ts"] + 1) % NTS
                            T = Tset[r]
                            tk = ("T", r)
                            n = ntok
                            prev = pend.pop(0) if pend else None
                            if prev:
                                prev[0]()
                            sigm_neg(T[0], pb[:, :n], [pk], tk, n)
                            act(T[1][:, :n], T[0][:, :n], AF.Ln, [tk, "c1"], [tk], scale=nc1t[:, dr, h:h + 1], bias=1.0)
                            emit_m("dve", "tensor_tensor_scan", dict(out=T[2][:, :n], data0=onesf[:, :n], data1=T[1][:, :n], initial=0.0, op0=ALU.mult, op1=ALU.add),
                                   reads=[tk, "onesf"], writes=[tk])

                            def part2a(T=T, tk=tk, dr=dr):
                                if dr == 1:
                                    emit_m("dve", "tensor_copy", dict(out=T[4][:, 1:2], in_=T[2][:, n - 1:n]), reads=[tk], writes=[tk])
                                    emit_m("dve", "tensor_tensor", dict(out=T[2][:, :n], in0=T[2][:, :n], in1=T[1][:, :n], op=ALU.subtract), reads=[tk], writes=[tk])

                            def part2b(T=T, tk=tk, dr=dr, h=h):
                                if dr == 0:
                                    act(T[4][:, 0:1], T[2][:, n - 1:n], AF.Exp, [tk], [tk])
                                    act(T[3][:, :n], T[2][:, :n], AF.Exp, [tk], [tk], scale=-1.0, bias=T[2][:, n - 1:n])
                                else:
                                    act(T[4][:, 0:1], T[4][:, 1:2], AF.Exp, [tk], [tk])
                                    act(T[3][:, :n], T[2][:, :n], AF.Exp, [tk], [tk])
                                emit_m("dve", "scalar_tensor_tensor", dict(out=kinvT[:, h, :n], in0=T[0][:, :n], scalar=c1t[:, dr, h:h + 1], in1=T[3][:, :n], op0=ALU.mult, op1=ALU.mult),
                                       reads=[tk, "c1"], writes=[("kinvT", h)])
                                defer(mktail(h, dr, T, tk))
                            pend.append((part2a, part2b))
                            if prev:
                                prev[1]()

                        def mktail(h, dr, T, tk):
                            def tail(h=h, dr=dr, T=T, tk=tk):
                                for s in range(nsub):
                                    P.emit("pe", (lambda s=s: (lambda e: e.transpose(out=psb[:msz, s * 128:(s + 1) * 128], in_=kinvT[:, h, s * 128:s * 128 + msz], identity=ident_b[:])))(),
                                           reads=[("kinvT", h), "identb"], writes=[("ps", 7)])
                                copy_op(evac_eng(), ktok[:msz, h, :nsub * 128], psb[:msz, :nsub * 128], [("ps", 7)], [("ktok", h)])
                                ib = 4 + (h // 4)
                                for s in range(nsub):
                                    mm(ps[ib][:, (h % 4) * 128:(h % 4 + 1) * 128], ktok[:msz, h, s * 128:(s + 1) * 128], vpar[0][:msz, s, h * 128:(h + 1) * 128], s == 0, s == nsub - 1,
                                       [("ktok", h), "vpar"], [("ps", ib)])
                                Sv = Sst[:, dr, h, :]
                                emit_m("dve", "scalar_tensor_tensor", dict(out=Sv, in0=Sv, scalar=T[4][:, 0:1], in1=ps[ib][:, (h % 4) * 128:(h % 4 + 1) * 128], op0=ALU.mult, op1=ALU.add),
                                       reads=[("S", dr, h), tk, ("ps", ib)], writes=[("S", dr, h)])
                            return tail
                        pre_ = None
                        if inter and nhead[0] > 0 and nhead[0] % 4 == 0:
                            pre_ = inter.pop(0)
                        proj_head(pf, j, xv, ntok, consume, xkey=xkey, nbk=nbk, pre=pre_)
                        nhead[0] += 1
                        if xb is not None and nhead[0] in (2, 7, 12):
                            add(None, lambda slot: emit_casts(-(-(-(-63 // max(1, NPRE - 2))) // 3)))

            def drain(slot):
                while pend:
                    a, b = pend.pop(0)
                    a()
                    b()
                flush()
            add(None, drain)
            for f in inter:
                add(None, lambda slot, f=f: f())

        def hg_main(mode, tok0, xcol0):
            dr = 1 if mode == "bwd" else 0
            ntok = TT
            xv = lambda kc: xT[:, kc, xcol0:xcol0 + ntok]
            xvs = lambda kc, s: xT[:, kc, xcol0 + s * 128: xcol0 + s * 128 + 128]
            if mode == "fwd":
                add(None, lambda slot: dma("pool", o_sb[:], ob_d[:, :, tok0:tok0 + TT].rearrange("h p t -> p h t"), "osb", reads=[("ob_d", tok0)], writes=["osb"]))
            hg_v(xvs, 4, 128, True)
            fcol = C_FB if dr else C_FF
            for hp in range(2):
                pf = piece_in(fcol + hp * 512, "g1")
                pq = piece_in(C_Q + hp * 512, "g2")
                pg = piece_in(C_G + hp * 512, "g2") if mode == "fwd" else None
                hold = {}

                def mk(h, hold=hold):
                        def cf(pb, pk, h=h, hold=hold):
                            r = cnt["ts"] = (cnt["ts"] + 1) % NTS
                            hold[h] = r
                            T = Tset[r]
                            tk = ("T", r)
                            sigm_neg(T[0], pb[:, :], [pk], tk, TT)
                            act(T[1][:], T[0][:], AF.Ln, [tk, "c1"], [tk], scale=nc1t[:, dr, h:h + 1], bias=1.0)
                            emit_m("dve", "tensor_tensor_scan", dict(out=T[2][:], data0=scm[:], data1=T[1][:], initial=0.0, op0=ALU.mult, op1=ALU.add), reads=[tk] + CK, writes=[tk])
                            act(decs[:, h, :], T[2][:, 63::64], AF.Exp, [tk], [("dec", h)])
                            if dr == 0:
                                act(T[3][:], T[2][:], AF.Exp, [tk], [tk])
                                act(T[4][:], T[2][:], AF.Exp, [tk], [tk], scale=-1.0)
                            else:
                                emit_m("dve", "tensor_tensor", dict(out=T[2][:], in0=T[2][:], in1=T[1][:], op=ALU.subtract), reads=[tk], writes=[tk])
                                act(T[4][:], T[2][:], AF.Exp, [tk], [tk])
                                act(T[3][:], T[2][:], AF.Exp, [tk], [tk], scale=-1.0)
                            emit_m("dve", "scalar_tensor_tensor", dict(out=kinvT[:, h, :], in0=T[0][:], scalar=c1t[:, dr, h:h + 1], in1=T[4][:], op0=ALU.mult, op1=ALU.mult),
                                   reads=[tk, "c1"], writes=[("kinvT", h)])
                            def tail(h=h):
                                for s in range(4):
                                    P.emit("pe", (lambda s=s: (lambda e: e.transpose(out=psb[:, s * 128:(s + 1) * 128], in_=kinvT[:, h, s * 128:(s + 1) * 128], identity=ident_b[:])))(),
                                           reads=[("kinvT", h), "identb"], writes=[("ps", 7)])
                                copy_op(evac_eng(), ktok[:, h, :], psb[:, :512], [("ps", 7)], [("ktok", h)])
                            defer(tail)
                        def cq(pb, pk, h=h, hold=hold):
                            T = Tset[hold[h]]
                            tk = ("T", hold[h])
                            sigm_neg(T[1], pb[:, :], [pk], tk, TT, sgn=-1.0)
                            emit_m("dve", "tensor_tensor", dict(out=T[1][:], in0=T[1][:], in1=pb[:, :], op=ALU.mult), reads=[tk, pk], writes=[tk])
                            emit_m("dve", "scalar_tensor_tensor", dict(out=qdecT[:, h, :], in0=T[1][:], scalar=float(128 ** -0.5), in1=T[3][:], op0=ALU.mult, op1=ALU.mult),
                                   reads=[tk], writes=[("qdecT", h)])

                        return cf, cq

                def mkg(h):
                    def cg(pb, pk, h=h):
                        r = cnt["ts"] = (cnt["ts"] + 1) % NTS
                        T = Tset[r]
                        tk = ("T", r)
                        sigm_neg(T[0], pb[:, :], [pk], tk, TT, sgn=-1.0)
                        emit_m("dve", "tensor_tensor", dict(out=gsb[:, h, :], in0=T[0][:], in1=pb[:, :], op=ALU.mult), reads=[tk, pk], writes=[("gs", h)])

                    return cg
                fns = {hp * 4 + j: mk(hp * 4 + j) for j in range(4)}
                for pr2 in range(2):
                    for j in (2 * pr2, 2 * pr2 + 1):
                        proj_head(pf, j, xv, ntok, fns[hp * 4 + j][0])
                    for j in (2 * pr2, 2 * pr2 + 1):
                        proj_head(pq, j, xv, ntok, fns[hp * 4 + j][1])
                if mode == "fwd":
                    for j in range(4):
                        proj_head(pg, j, xv, ntok, mkg(hp * 4 + j))

            add(None, lambda slot: flush())

            def scan(slot):
                order = list(range(8)) if dr == 0 else list(range(7, -1, -1))

                def sc_mask(i):
                    c = order[i]
                    pr, par = c // 2, c % 2
                    bk = i % 2
                    for h in range(8):
                        mm(ps[bk][:, h * 64:(h + 1) * 64], kinvT[:, h, pr * 128:(pr + 1) * 128], qdecT[:, h, c * 64:(c + 1) * 64], True, True,
                           [("kinvT", h), ("qdecT", h)], [("ps", bk)])
                    for h in range(8):
                        emit_m("dve", "tensor_tensor", dict(out=scb[bk][:, h, :], in0=ps[bk][:, h * 64:(h + 1) * 64], in1=hgm[:, dr, par, :], op=ALU.mult),
                               reads=[("ps", bk)] + CK, writes=[("scb", bk)])

                def sbf(i, h):
                    c = order[i]
                    Sv = Sst[:, dr, h, :]
                    if dr == 0:
                        sc_ = None if i == 0 else decs[:, h, order[i - 1]:order[i - 1] + 1]
                    else:
                        sc_ = decs[:, h, c:c + 1]
                    copy_op("act", Sbf[i % 2][:, h, :], Sv, [("S", dr, h), ("dec", h)], [("Sbf", i % 2, h)], scale=sc_)

                for h in range(8):
                    sbf(0, h)
                sc_mask(0)
                for i in range(8):
                    c = order[i]
                    pr, par = c // 2, c % 2
                    if i + 1 < 8:
                        sc_mask(i + 1)
                    ob = 2 + (i % 2)
                    for h in range(8):
                        oo = ps[ob][:, h * 64:(h + 1) * 64]
                        mm(oo, vpar[par][:, pr, h * 128:(h + 1) * 128], scb[i % 2][:, h, :], True, False, ["vpar", ("scb", i % 2)], [("ps", ob)])
                        mm(oo, Sbf[i % 2][:, h, :], qdecT[:, h, c * 64:(c + 1) * 64], False, True, [("Sbf", i % 2, h), ("qdecT", h)], [("ps", ob)])
                    osl = o_sb[:, :, c * 64:(c + 1) * 64]
                    pso = ps[ob][:, :].rearrange("p (h t) -> p h t", h=8)
                    if mode == "fwd":
                        emit_m("dve", "tensor_tensor", dict(out=osl, in0=osl, in1=pso, op=ALU.add), reads=[("ps", ob), "osb"], writes=["osb"])
                    else:
                        copy_op("act", osl, pso, [("ps", ob)], ["osb"])
                    for h in range(8):
                        ib = 4 + (h // 4)
                        mm(ps[ib][:, (h % 4) * 128:(h % 4 + 1) * 128], ktok[:, h, pr * 128:(pr + 1) * 128], vpar[par][:, pr, h * 128:(h + 1) * 128], True, True,
                           [("ktok", h), "vpar"], [("ps", ib)])
                    for h in range(8):
                        Sv = Sst[:, dr, h, :]
                        if dr == 0:
                            sc_ = 1.0 if i == 0 else decs[:, h, order[i - 1]:order[i - 1] + 1]
                        else:
                            sc_ = decs[:, h, c:c + 1]
                        emit_m("dve", "scalar_tensor_tensor", dict(out=Sv, in0=Sv, scalar=sc_, in1=ps[4 + (h // 4)][:, (h % 4) * 128:(h % 4 + 1) * 128], op0=ALU.mult, op1=ALU.add),
                               reads=[("S", dr, h), ("ps", 4 + (h // 4)), ("dec", h)], writes=[("S", dr, h)])
                    if i + 1 < 8:
                        for h in range(8):
                            sbf(i + 1, h)
                if dr == 0:
                    for h in range(8):
                        Sv = Sst[:, dr, h, :]
                        emit_m("dve", "tensor_scalar", dict(out=Sv, in0=Sv, scalar1=decs[:, h, 7:8], scalar2=None, op0=ALU.mult), reads=[("S", dr, h), ("dec", h)], writes=[("S", dr, h)])
            add(None, scan)

            if mode == "bwd":
                add(None, lambda slot: dma("pool", ob_d[:, :, tok0:tok0 + TT].rearrange("h p t -> p h t"), o_sb[:], "osb", reads=["osb"], writes=[("ob_d", tok0)]))
            else:
                def fin(slot):
                    for h in range(8):
                        r = cnt["ts"] = (cnt["ts"] + 1) % NTS
                        T = Tset[r]
                        tk = ("T", r)
                        sqb = scb[0] if h % 2 == 0 else scb[1]
                        sq = kinvT[:, h, :]
                        act(sq, o_sb[:, h, :], AF.Square, ["osb"], [("kinvT", h)])
                        mm(ps[6][:, :], ones_b[:], sq, True, True, ["ones", ("kinvT", h)], [("ps", 6)])
                        act(T[1][:], ps[6][:, :], AF.Ln, [("ps", 6)], [tk], scale=1.0 / 128, bias=EPS)
                        act(T[1][:], T[1][:], AF.Exp, [tk], [tk], scale=-0.5)
                        emit_m("dve", "scalar_tensor_tensor", dict(out=T[0][:], in0=o_sb[:, h, :], scalar=hgain_s[:, h:h + 1], in1=T[1][:], op0=ALU.mult, op1=ALU.mult), reads=[tk, "osb"] + CK, writes=[tk])
                        emit_m("dve", "tensor_tensor", dict(out=oT_hg[:, h, :], in0=T[0][:], in1=gsb[:, h, :], op=ALU.mult), reads=[tk, ("gs", h)], writes=["oT_hg"])
                add(None, fin)

        def hg_tile(mode, tok0, xcol0, ntok=TT):
            if mode == "meta":
                hg_pre(xcol0, ntok, [0])
            elif mode == "pre":
                hg_pre(xcol0, ntok, [0, 1])
            else:
                hg_main(mode, tok0, xcol0)

        def attn_kv(xcol0, ntok, kdst, vdst, vkey, kkey):
            def kd(j, pb, pk, n0, n):
                copy_op(evac_eng(), kdst[:, j, n0:n0 + n], pb[:, :n], [pk], [kkey])
            for n0 in range(0, ntok, 512):
                n = min(512, ntok - n0)
                proj_fm(piece_in(0, "g1k", kd=True), lambda kc, n0=n0, n=n: xT[:, kc, xcol0 + n0: xcol0 + n0 + n], n,
                        lambda j, pb, pk, n0=n0, n=n: kd(j, pb, pk, n0, n), "xT")
            nsub = (ntok + 127) // 128
            msz = min(128, ntok)

            def vd(s, pb, pk):
                src = pb[:msz, 256:512].rearrange("p (g d) -> p g d", g=4)
                if ntok == 16:
                    d0, d1 = vdst[:msz, :, 0, 0:64], vdst[:msz, :, 1, 64:128]
                else:
                    d0, d1 = vdst[:msz, s, :, 0, 0:64], vdst[:msz, s, :, 1, 64:128]
                copy_op("act", d0, src, [pk], [vkey])
                copy_op("dve", d1, src, [pk], [vkey])
            proj_tm(piece_in(C_AV - 256, "g1k"), lambda kc, s: xT[:, kc, xcol0 + s * 128: xcol0 + s * 128 + msz], nsub, vd, "xT", msize=msz)

        slopes = [float(2.0 ** (-8.0 * (h + 1) / 16.0)) for h in range(16)]

        def attn_tile(ti):
            for half in range(2):
                def qd(j, pb, pk, half=half):
                    copy_op("act", qTa[:, 0, half * 4 + j, :], pb[:, :], [pk, "rowm8"], ["qTa"], scale=rowm8[:, 0:1])
                    copy_op("dve", qTa[:, 1, half * 4 + j, :], pb[:, :], [pk, "rowm8"], ["qTa"], scale=rowm8[:, 1:2])
                proj_fm(piece_in(C_AQ + half * 512, "g3"), lambda kc: xT[:, kc, 128:128 + TT], TT, qd, "xT")
            add(None, lambda slot: P.emit("pool", lambda e: e.memset(vat[:], 0.0), writes=["vat"]))
            add(None, lambda slot: dma("pool", onesc[:], onesc_d[:, :, :], "onesc", writes=["onesc"]))
            attn_kv(0, 768, kTd, vat, "vat", "kTd")

            def blocks(slot):
                its = [(qb, g, hl) for qb in range(4) for g in range(4) for hl in range(2)]

                def A(n):
                    qb, g, hl = its[n]
                    p2 = (n % 2) * 2
                    rq = qTa[:, hl, 2 * g:2 * g + 2, qb * 128:(qb + 1) * 128]
                    for kb in range(3):
                        bank = p2 + (kb // 2)
                        co = (kb % 2) * 256
                        mm(ps[bank][:, co:co + 256].rearrange("p (a b) -> p a b", a=2), kTd[:, g, (qb + kb) * 128:(qb + kb + 1) * 128], rq, True, True, ["kTd", "qTa"], [("ps", bank)])
                    mm(ps[p2 + 1][:, 256:512].rearrange("p (a b) -> p a b", a=2), kmT[:, g, :], rq, True, True, ["kmT", "qTa"], [("ps", p2 + 1)])

                def B(n):
                    qb, g, hl = its[n]
                    p2 = (n % 2) * 2
                    first = (ti == 0 and qb == 0)
                    last = (ti == NT - 1 and qb == 3)
                    pi = (qb * 4 + g) % 2
                    pt = PT[pi]
                    sc = scs[n % 2]
                    sk = ("scs", n % 2)
                    for kb in range(3):
                        dsel = kb
                        if kb == 0 and first:
                            dsel = 3
                        if kb == 2 and last:
                            dsel = 4
                        bank = p2 + (kb // 2)
                        co = (kb % 2) * 256
                        for c in range(2):
                            head = 4 * g + 2 * c + hl
                            emit_m("dve", "scalar_tensor_tensor", dict(out=sc[:, kb, c * 128:(c + 1) * 128], in0=dtb[:, dsel, :], scalar=slopes[head],
                                                                        in1=ps[bank][:, co + c * 128: co + (c + 1) * 128], op0=ALU.mult, op1=ALU.add),
                                   reads=[("ps", bank)] + CK, writes=[sk])
                    act(pt[:, 0:3, hl * 256:(hl + 1) * 256], sc[:, :, :], AF.Exp, [sk], [("PT", pi, hl)])
                    act(pt[:, 3, hl * 256:(hl + 1) * 256], ps[p2 + 1][:, 256:512], AF.Exp, [("ps", p2 + 1)], [("PT", pi, hl)])

                def C(n):
                    qb, g, hl = its[n]
                    pi = (qb * 4 + g) % 2
                    pt = PT[pi]
                    ptk = ("PT", pi, hl)
                    nb = 4 + 2 * pi
                    for kb in range(4):
                        if kb < 3:
                            lv = vat[:, qb + kb, g, hl, :]
                        else:
                            lv = vm[:, g, hl, :]
                        mm(ps[nb][:, 0:256], lv, pt[:, kb, hl * 256:(hl + 1) * 256], hl == 0 and kb == 0, hl == 1 and kb == 3, ["vat", "vm", ptk], [("ps", nb)])
                    for kb in range(4):
                        lo = onesc[:, (0 if kb < 3 else 2) + hl, :]
                        mm(ps[nb + 1][:, 0:256], lo, pt[:, kb, hl * 256:(hl + 1) * 256], hl == 0 and kb == 0, hl == 1 and kb == 3, ["onesc", ptk], [("ps", nb + 1)])
                    if hl == 1:
                        rc = rec[pi]
                        rk = ("rec", pi)
                        for c in range(2):
                            ch = 2 * g + c
                            act(rc[:, c * 128:(c + 1) * 128], ps[nb + 1][:, c * 128:(c + 1) * 128], AF.Ln, [("ps", nb + 1), "esink"], [rk], bias=esink[:, ch:ch + 1])
                        act(rc[:], rc[:], AF.Exp, [rk], [rk], scale=-1.0)
                        for c in range(2):
                            ch = 2 * g + c
                            emit_m("dve", "tensor_tensor", dict(out=oT_att[:, ch, qb * 128:(qb + 1) * 128], in0=ps[nb][:, c * 128:(c + 1) * 128], in1=rc[:, c * 128:(c + 1) * 128], op=ALU.mult),
                                   reads=[("ps", nb), rk], writes=["oT_att"])

                N = len(its)
                A(0)
                B(0)
                for n in range(N):
                    if n + 1 < N:
                        A(n + 1)
                        B(n + 1)
                    C(n)
            add(None, blocks)

        def merge_tile():
            xv = lambda kc: xT[:, kc, 128:128 + TT]
            for ng in range(4):
                def fa(slot, ng=ng):
                    pass
                st = {}

                def ga_item(slot, ng=ng, st=st):
                    st["ga"] = slot
                pa_ = piece_in(C_GA + ng * 512, "g4")
                add(pa_, ga_item)

                def gb_item(slot, ng=ng, st=st):
                    st["gb"] = slot
                pb_ = piece_in(C_GB + ng * 512, "g4")
                add(pb_, gb_item)

                def p_item(slot, ng=ng, st=st):
                    sa, sbb, sp_ = st["ga"], st["gb"], slot
                    for j in range(4):
                        nch = ng * 4 + j
                        cj = slice(j * 128, (j + 1) * 128)
                        b0 = 4 * (j % 2)
                        for kc in range(KC):
                            mm(ps[b0][:, :], wsl[sa][:, kc, cj], xv(kc), kc == 0, kc == KC - 1, [("w", sa), "xT"], [("ps", b0)])
                        for kc in range(KC):
                            mm(ps[b0 + 1][:, :], wsl[sbb][:, kc, cj], xv(kc), kc == 0, kc == KC - 1, [("w", sbb), "xT"], [("ps", b0 + 1)])
                        for kc in range(8):
                            mm(ps[b0 + 2][:, :], wsl[sp_][:, kc, cj], oT_hg[:, kc, :], kc == 0, kc == 7, [("w", sp_), "oT_hg"], [("ps", b0 + 2)])
                        for kc in range(8):
                            mm(ps[b0 + 3][:, :], wsl[sp_][:, 8 + kc, cj], oT_att[:, kc, :], kc == 0, kc == 7, [("w", sp_), "oT_att"], [("ps", b0 + 3)])
                        m0 = mg[(j % 2) * 4: (j % 2) * 4 + 4]
                        mk = ("mg", j % 2)
                        for gi in range(2):
                            act(m0[gi][:], ps[b0 + gi][:, :], AF.Exp, [("ps", b0 + gi)], [mk], scale=-1.0)
                            act(m0[gi][:], m0[gi][:], AF.Ln, [mk], [mk], bias=1.0)
                            act(m0[gi][:], m0[gi][:], AF.Exp, [mk], [mk], scale=-1.0)
                        emit_m("dve", "tensor_tensor", dict(out=m0[2][:], in0=m0[0][:], in1=ps[b0 + 2][:, :], op=ALU.mult), reads=[mk, ("ps", b0 + 2)], writes=[mk])
                        emit_m("dve", "tensor_tensor", dict(out=m0[3][:], in0=m0[1][:], in1=ps[b0 + 3][:, :], op=ALU.mult), reads=[mk, ("ps", b0 + 3)], writes=[mk])
                        emit_m("pool", "tensor_tensor", dict(out=mergedT[:, nch, :], in0=m0[2][:], in1=m0[3][:], op=ALU.add), reads=[mk], writes=["mergedT"])
                add({"src": wb_p[:, ng * 512:(ng + 1) * 512].rearrange("(kc p) n -> p kc n", p=128), "wkey": ("wb", "g4"), "nk": KC}, p_item)
                add(pa_, lambda slot: None)
                add(pb_, lambda slot: None)

        def tm_proj_norm(srcT, skey, wb, grp, nkp, zb, zkey, gsel, resid_load, out_store, ti):
            add(None, lambda slot: dma("pool", gpo[:], gpost[gsel:gsel + 1, :].partition_broadcast(128), "gpo", writes=["gpo"]))
            for ng in range(4):
                for kp in range(nkp):
                    def item(slot, ng=ng, kp=kp):
                        for s in range(4):
                            bk = s + (4 * (ng % 2))
                            for kc in range(KC):
                                kk = kp * KC + kc
                                mm(ps[bk][:, :], srcT[:, kk, s * 128:(s + 1) * 128], wsl[slot][:, kc, :], kk == 0, kk == nkp * KC - 1, [("w", slot), skey], [("ps", bk)])
                            if kp == nkp - 1:
                                copy_op(evac_eng(), zb[:, s, ng * 512:(ng + 1) * 512], ps[bk][:, :], [("ps", bk)], [(zkey, s)])
                    add(piece_rows(wb, kp * KC * 128, ng * 512, grp), item)

            def fin(slot):
                for s in range(4):
                    b = s % 2
                    sc = stat[:, 4 + b:5 + b]
                    junk = junkb if zkey == "z2" else xsf2[b]
                    if resid_load:
                        dma("pool", x1[:, s, :], x_ext[128 + ti * TT + s * 128: 128 + ti * TT + (s + 1) * 128, :], f"x1l{s}", writes=[("x1", s)])
                    act(junk[:], zb[:, s, :], AF.Square, [(zkey, s)], [("junkb" if zkey == "z2" else ("xsf2", b)), ("st2", b)], accum_out=sc)
                    act(sc, sc, AF.Ln, [("st2", b)], [("st2", b)], scale=1.0 / D, bias=EPS)
                    act(sc, sc, AF.Exp, [("st2", b)], [("st2", b)], scale=-0.5)
                    emit_m("dve", "scalar_tensor_tensor", dict(out=zb[:, s, :], in0=zb[:, s, :], scalar=sc, in1=gpo[:], op0=ALU.mult, op1=ALU.mult),
                           reads=[(zkey, s), ("st2", b), "gpo"], writes=[(zkey, s)])
                    emit_m("pool", "tensor_tensor", dict(out=x1[:, s, :], in0=x1[:, s, :], in1=zb[:, s, :], op=ALU.add), reads=[(zkey, s), ("x1", s)], writes=[("x1", s)])
                    if out_store:
                        dma("pool", y[ti * TT + s * 128: ti * TT + (s + 1) * 128, :], x1[:, s, :], f"yst{s}", reads=[("x1", s)])
            add(None, fin)

        def ffn1_tile():
            for pc in range(16):
                def dst(j, pb, pk, pc=pc):
                    r = rtmp[j % 4]
                    rk = ("rtmp", j % 4)
                    act(r[:], pb[:, :], AF.Relu, [pk], [rk])
                    emit_m("pool", "tensor_tensor", dict(out=uT[:, pc * 4 + j, :], in0=r[:], in1=r[:], op=ALU.mult), reads=[rk], writes=["uT"])
                proj_fm(piece_rows(wb_ff1, 0, pc * 512, "g6"), lambda kc: h2T[:, kc, :], TT, dst, "h2T", nb8=8)

        prep(meta, 1, 128, 0, xT, "xT")
        add_barrier()
        dump("xT_meta", xT[:, :, 128:144], ["xT"])
        set_c1(None)
        attn_kv(128, 16, kmT, vm, "vm", "kmT")
        hg_tile("meta", 0, 128, ntok=16)
        add_barrier()
        dump("kmT", kmT[:], ["kmT"])
        dump("vm", vm[:], ["vm"])
        dump("oml", oml[:], ["oml"])
        dump("S_meta", Sst[:], [("S", d_, h_) for d_ in range(2) for h_ in range(8)])
        def pre_rows(t):
            return x_pre[t * TT:(t + 1) * TT, :] if t < NPRE else None

        if NPRE > 0:
            cl, pre_issue = prep_light(pre_rows(0), xTL[0], "xTL0", pre_rows(1))
            add(None, lambda slot: pre_issue())
            for f in cl:
                add(None, lambda slot, f=f: f())
        for pt_ in range(NPRE):
            nxt = prep_light(pre_rows(pt_ + 1), xTL[(pt_ + 1) % 2], f"xTL{(pt_ + 1) % 2}", pre_rows(pt_ + 2))[0] if pt_ + 1 < NPRE else []
            set_c1(pt_ * 2)
            hg_pre(0, TT, [0, 1], xb=(xTL[pt_ % 2], f"xTL{pt_ % 2}"), inter=nxt)
        add_barrier()
        add(None, lambda slot: emit_casts(1000))
        dump("S_pre", Sst[:], [("S", d_, h_) for d_ in range(2) for h_ in range(8)])
        set_c1(None)
        STOP = _os.environ.get("STOP", "")
        for ti in (range(NT - 1, -1, -1) if STOP != "pre" else []):
            prep(x_ext[128 + ti * TT: 128 + (ti + 1) * TT, :], 4, 128, 0, xT, "xT")
            add_barrier()
            hg_tile("bwd", ti * TT, 128)
            add_barrier()
        if dbg:
            add_barrier()
            add(None, lambda slot: dma("pool", o_sb[:], ob_d[:, :, 0:TT].rearrange("h p t -> p h t"), "osb", reads=[("ob_d", 0)], writes=["osb"]))
            for h_ in range(8):
                dump(f"ob{h_}", o_sb[:, h_, :], ["osb"])
            add_barrier()
        for ti in (range(NT) if STOP == "" else []):
            prep(x_ext[ti * TT: ti * TT + 768, :], 6, 0, 0, xT, "xT")
            add_barrier()
            attn_tile(ti)
            add_barrier()
            if ti == 0:
                dump("xT0", xT[:, :, :], ["xT"])
                dump("oT_att", oT_att[:], ["oT_att"])
            hg_tile("fwd", ti * TT, 128)
            add_barrier()
            if ti == 0:
                dump("oT_hg", oT_hg[:], ["oT_hg"])
            merge_tile()
            add_barrier()
            if ti == 0:
                dump("mergedT", mergedT[:], ["mergedT"])
            tm_proj_norm(mergedT, "mergedT", wb_out, "g5", 1, zbuf, "z", 0, True, False, ti)
            if ti == 0:
                dump("x1", x1[:], [("x1", s_) for s_ in range(4)])
            prep(x1, 4, 0, 1, h2T, "h2T", rows_key="x1", xs=xsf2)
            add_barrier()
            if ti == 0:
                dump("h2T", h2T[:], ["h2T"])
            ffn1_tile()
            add_barrier()
            tm_proj_norm(uT, "uT", wb_ff2, "g7", 4, z2, "z2", 1, False, True, ti)
            add_barrier()

        order = []
        last_use = {}
        for idx, (p, f) in enumerate(items):
            if isinstance(p, dict):
                if id(p) not in last_use:
                    order.append(p)
                last_use[id(p)] = idx
        pos = {id(p): k for k, p in enumerate(order)}
        state = {"next": 0}

        def issue_load(k):
            p = order[k]
            slot = k % NSLOT
            p["slot"] = slot
            dma("sp", wsl[slot][:, :p["nk"], :], p["src"], f"w{slot}", reads=[p["wkey"]], writes=[("w", slot)])

        for idx, (p, f) in enumerate(items):
            if p == "barrier":
                P.barrier()
                continue
            while state["next"] < len(order) and (state["next"] < NSLOT or last_use[id(order[state["next"] - NSLOT])] < idx):
                issue_load(state["next"])
                state["next"] += 1
            if isinstance(p, dict):
                assert pos[id(p)] < state["next"], "weight slot deadlock"
                f(p["slot"])
            else:
                f(None)

        P.replay(nc, sems, dsem, final_eng="sp")
    return nc


def const_tables():
    j = np.arange(128)[:, None].astype(np.float32)
    r = np.arange(128)[None, :].astype(np.float32)
    dl = np.where(j >= r, -(r + 128 - j), NEG)
    dc = -np.abs(j - r)
    dr = np.where(j <= r, -(j + 128 - r), NEG)
    edge = np.full((128, 128), NEG, np.float32)
    dtab = np.stack([dl, dc, dr, edge, edge], axis=1).astype(np.float32)
    s = np.arange(128)[:, None]
    t = np.arange(128)[None, :]
    same = (s // 64) == (t // 64)
    s3 = np.arange(128)[:, None, None, None]
    d3 = np.arange(2)[None, :, None, None]
    p3 = np.arange(2)[None, None, :, None]
    t3 = np.arange(64)[None, None, None, :]
    hgmask = ((s3 // 64 == p3) & np.where(d3 == 0, (s3 % 64) <= t3, (s3 % 64) >= t3)).astype(np.float32)
    scan = np.ones((128, TT), np.float32)
    scan[:, ::64] = 0.0
    return dtab, hgmask, scan, np.eye(128, dtype=np.float32)


def core_inputs(seq_x, start, ntok, is_first, is_last, npre, meta_tokens, shared):
    S = seq_x.shape[0]
    x_ext = np.zeros((ntok + 256, D), np.float32)
    lo = max(0, start - 128)
    hi = min(S, start + ntok + 128)
    x_ext[128 - (start - lo): 128 - (start - lo) + (hi - lo)] = seq_x[lo:hi]
    x_pre = np.zeros((max(npre, 1) * TT, D), np.float32)
    premask = np.zeros((128, max(npre, 1) * 2), np.float32)
    npf = start // TT
    nsf = (S - start - ntok) // TT
    assert npf + nsf <= npre
    for i in range(npf):
        x_pre[i * TT:(i + 1) * TT] = seq_x[i * TT:(i + 1) * TT]
        premask[:, 2 * i] = 1.0
    for i in range(nsf):
        t0 = S - (i + 1) * TT
        x_pre[(npf + i) * TT:(npf + i + 1) * TT] = seq_x[t0:t0 + TT]
        premask[:, 2 * (npf + i) + 1] = 1.0
    dtab, hgmask, scan, ident = shared["tables"]
    dtab = dtab.copy()
    if not is_first:
        dtab[:, 3] = dtab[:, 0]
    if not is_last:
        dtab[:, 4] = dtab[:, 2]
    m = dict(shared["weights"])
    rowmask = np.zeros((128, 2), np.float32)
    rowmask[:64, 0] = 1.0
    rowmask[64:, 1] = 1.0
    onesc = np.zeros((128, 4, 128), np.float32)
    onesc[:, 0, 0:64] = 1.0
    onesc[:, 1, 64:128] = 1.0
    onesc[:16, 2, 0:64] = 1.0
    onesc[:16, 3, 64:128] = 1.0
    m.update(onesc=onesc)
    m.update(x_ext=x_ext, x_pre=x_pre, premask=premask, dtab=dtab, hgmask=hgmask, scanmask=scan, identf=ident, rowmask=rowmask)
    return m


def shared_inputs(meta_tokens, w_in, w_proj_hg, w_proj_att, w_out, w_ff1, w_ff2, g_pre_mix, g_post_mix, g_pre_ff, g_post_ff,
                  lb_logits, hg_out_gain, attn_sink):
    f = lambda a: np.ascontiguousarray(np.asarray(a, dtype=np.float32))
    meta = np.zeros((128, D), np.float32)
    meta[:16] = f(meta_tokens)
    gpre = np.stack([f(g_pre_mix)[0].reshape(KC, 128).T, f(g_pre_ff)[0].reshape(KC, 128).T], axis=1)
    gpost = np.stack([f(g_post_mix)[0], f(g_post_ff)[0]], axis=0)
    lbl = f(lb_logits).reshape(2, 2, 8, 128).transpose(3, 0, 1, 2)
    hgain = f(hg_out_gain)[0].reshape(8, 128).T
    sk = f(attn_sink)[0]
    sink = np.zeros((128, 8), np.float32)
    sink[:64] = sk[0::2][None, :]
    sink[64:] = sk[1::2][None, :]
    w = dict(meta=meta, w_in=f(w_in)[0], w_phg=f(w_proj_hg)[0], w_patt=f(w_proj_att)[0], w_out=f(w_out)[0], w_ff1=f(w_ff1)[0],
             w_ff2=f(w_ff2)[0], gpre=f(gpre), gpost=f(gpost), lbl=f(lbl), hgain=f(hgain), sink=sink)
    return {"weights": w, "tables": const_tables()}


_CACHE = {}


def run_layout(seqs, core_plan, NT, NPRE, shared, full=False):
    key = (NT, NPRE)
    if key not in _CACHE:
        _CACHE[key] = build(NT, NPRE)
    nc = _CACHE[key]
    in_maps = []
    for (si, start) in core_plan:
        S = seqs[si].shape[0]
        in_maps.append(core_inputs(seqs[si], start, NT * TT, start == 0, start + NT * TT == S, NPRE, None, shared))
    res = run_bass_kernel_spmd(nc, in_maps, core_ids=list(range(len(core_plan))))
    if full:
        return res.results
    return [r["y"] for r in res.results]


def kernel(x_prompt, x_sample, meta_tokens, w_in, w_proj_hg, w_proj_att, w_out, w_ff1, w_ff2,
           g_pre_mix, g_post_mix, g_pre_ff, g_post_ff, lb_logits, hg_out_gain, attn_sink):
    x_prompt = np.asarray(x_prompt, dtype=np.float32)
    x_sample = np.asarray(x_sample, dtype=np.float32)
    shared = shared_inputs(meta_tokens, w_in, w_proj_hg, w_proj_att, w_out, w_ff1, w_ff2, g_pre_mix, g_post_mix,
                           g_pre_ff, g_post_ff, lb_logits, hg_out_gain, attn_sink)
    seqs = [x_prompt[b] for b in range(4)] + [x_sample[0]]
    plan = [(b, 0) for b in range(4)] + [(4, c * 4096) for c in range(4)]
    outs = run_layout(seqs, plan, 8, 24, shared)
    y_prompt = np.stack(outs[:4], axis=0)
    y_sample = np.concatenate(outs[4:], axis=0)[None]
    return (y_prompt, y_sample)
```

```python
import numpy as np
from contextlib import ExitStack
import concourse.bass as bass
import concourse.mybir as mybir
from concourse.bass_utils import run_bass_kernel_spmd

F32 = mybir.dt.float32
BF16 = mybir.dt.bfloat16
AF = mybir.ActivationFunctionType
ALU = mybir.AluOpType

D = 2048
KC = 16
NIN = 10752
DFF = 8192
TT = 512
EPS = 1e-6
C_Q, C_I, C_FF, C_FB, C_G, C_AQ, C_AK, C_AV, C_GA, C_GB = 0, 1024, 2048, 3072, 4096, 5120, 6144, 6400, 6656, 8704
NEG = -1.0e9
ENGS = ("pe", "act", "dve", "pool", "sp")


class Op:
    __slots__ = ("eng", "fn", "waits", "idx", "marked", "dma_sem", "dma_val", "val")

    def __init__(self, eng, fn):
        self.eng = eng
        self.fn = fn
        self.waits = []
        self.idx = None
        self.marked = False
        self.dma_sem = None
        self.dma_val = None
        self.val = None


class Prog:
    def __init__(self, same_engine_sync=("act", "dve", "pool")):
        self.ops = {e: [] for e in ENGS}
        self.last_writes = {}
        self.readers = {}
        self.known = {e: {} for e in ENGS}
        self.dma_counts = {}
        self.same_engine_sync = set(same_engine_sync)
        self.pending = {e: [] for e in ENGS}

    @staticmethod
    def _stream(op):
        return ("dma", op.dma_sem) if op.dma_sem is not None else ("eng", op.eng)

    def _need(self, op, dep):
        if dep is None or dep is op:
            return
        st = self._stream(dep)
        if dep.dma_sem is not None:
            v = dep.dma_val
        else:
            if dep.eng == op.eng and dep.eng not in self.same_engine_sync:
                return
            v = dep.idx
        k = self.known[op.eng]
        if k.get(st, -1) >= v:
            return
        k[st] = v
        op.waits = [w for w in op.waits if self._stream(w) != st]
        op.waits.append(dep)

    def emit(self, eng, fn, reads=(), writes=(), dma_sem=None, ndma=1):
        op = Op(eng, fn)
        op.idx = len(self.ops[eng])
        if dma_sem is not None:
            c = self.dma_counts.get(dma_sem, 0) + ndma
            self.dma_counts[dma_sem] = c
            op.dma_sem = dma_sem
            op.dma_val = 16 * c
        for d in self.pending[eng]:
            self._need(op, d)
        self.pending[eng] = []
        for r in reads:
            for d in self.last_writes.get(r, {}).values():
                self._need(op, d)
        for w in writes:
            for d in self.last_writes.get(w, {}).values():
                self._need(op, d)
            for rd in self.readers.get(w, ()):
                self._need(op, rd)
        for r in reads:
            self.readers.setdefault(r, []).append(op)
        for w in writes:
            self.last_writes.setdefault(w, {})[self._stream(op)] = op
            self.readers[w] = []
        self.ops[eng].append(op)
        return op

    def barrier(self, engs=("pe", "act", "dve", "pool")):
        lasts = [self.ops[e][-1] for e in engs if self.ops[e]]
        for e in engs:
            self.pending[e] = list(lasts)

    def replay(self, nc, sems, dma_sems, final_eng="sp"):
        for e in ENGS:
            for op in self.ops[e]:
                for d in op.waits:
                    if d.dma_sem is None:
                        d.marked = True
        for e in ENGS:
            c = 0
            for op in self.ops[e]:
                if op.dma_sem is None and op.marked:
                    c += 1
                    op.val = c
        handles = {"pe": "tensor", "act": "scalar", "dve": "vector", "pool": "gpsimd", "sp": "sync"}
        finals = [(dma_sems[n], 16 * c) for n, c in self.dma_counts.items()]
        with nc.Block() as block:
            for e in ENGS:
                ops = self.ops[e]

                def body(engine, ops=ops, e=e):
                    for op in ops:
                        for d in op.waits:
                            if d.dma_sem is not None:
                                engine.wait_ge(dma_sems[d.dma_sem], d.dma_val)
                            else:
                                engine.wait_ge(sems[d.eng], d.val)
                        ins = op.fn(engine)
                        if op.dma_sem is None and op.marked:
                            ins.then_inc(sems[e], 1)
                    if e == final_eng:
                        for s, v in finals:
                            engine.wait_ge(s, v)
                getattr(block, handles[e])(body)


def build(NT, NPRE, dbg=False):
    NTOK = NT * TT
    nc = bass.Bass("TRN2", target_bir_lowering=False)
    es = ExitStack()
    import os as _os
    P = Prog(same_engine_sync=tuple(x for x in _os.environ.get("SES", "act,dve,pool").split(",") if x))
    dma_sem_names = []

    def din(name, shape, dt=F32):
        return nc.dram_tensor(name, list(shape), dt, kind="ExternalInput").ap()

    x_ext = din("x_ext", [NTOK + 256, D])
    x_pre = din("x_pre", [max(NPRE, 1) * TT, D])
    meta = din("meta", [128, D])
    w_in = din("w_in", [D, NIN])
    w_phg = din("w_phg", [1024, D])
    w_patt = din("w_patt", [1024, D])
    w_out = din("w_out", [D, D])
    w_ff1 = din("w_ff1", [D, DFF])
    w_ff2 = din("w_ff2", [DFF, D])
    gpre = din("gpre", [128, 2, KC])
    gpost = din("gpost", [2, D])
    lbl = din("lbl", [128, 2, 2, 8])
    hgain = din("hgain", [128, 8])
    sink = din("sink", [128, 8])
    premask = din("premask", [128, max(NPRE, 1) * 2])
    dtab = din("dtab", [128, 5, 128])
    hgmask = din("hgmask", [128, 2, 2, 64])
    rowmask = din("rowmask", [128, 2])
    scanmask = din("scanmask", [128, TT])
    identf = din("identf", [128, 128])
    onesc_d = din("onesc", [128, 4, 128])
    y = nc.dram_tensor("y", [NTOK, D], F32, kind="ExternalOutput").ap()
    wb_in = nc.dram_tensor("wb_in", [D, NIN], BF16, kind="Internal").ap()
    wb_kd = nc.dram_tensor("wb_kd", [D, 512], BF16, kind="Internal").ap()
    wb_p = nc.dram_tensor("wb_p", [2048, D], BF16, kind="Internal").ap()
    wb_out = nc.dram_tensor("wb_out", [D, D], BF16, kind="Internal").ap()
    wb_ff1 = nc.dram_tensor("wb_ff1", [D, DFF], BF16, kind="Internal").ap()
    wb_ff2 = nc.dram_tensor("wb_ff2", [DFF, D], BF16, kind="Internal").ap()
    ob_d = nc.dram_tensor("ob_d", [8, 128, NTOK], F32, kind="Internal").ap()
    dbg_out = {}

    with es:
        def sb(name, shape, dt):
            return es.enter_context(nc.sbuf_tensor(name, list(shape), dt))

        def pst(name, shape, dt=F32):
            return es.enter_context(nc.psum_tensor(name, list(shape), dt))

        NSLOT = 3
        wsl = [sb(f"wsl{i}", [128, KC, 512], BF16) for i in range(NSLOT)]
        Sst = sb("Sst", [128, 2, 8, 128], F32)
        ident_f = sb("ident_f", [128, 128], F32)
        ident_b = sb("ident_b", [128, 128], BF16)
        ones_b = sb("ones_b", [128, 128], BF16)
        hgm = sb("hgm", [128, 2, 2, 64], F32)
        rowm = sb("rowm", [128, 2], F32)
        scm = sb("scm", [128, TT], F32)
        dtb = sb("dtb", [128, 5, 128], F32)
        gpre_s = sb("gpre_s", [128, 2, KC], F32)
        lb_s = sb("lb_s", [128, 2, 2, 8], F32)
        oml = sb("oml", [128, 2, 8], F32)
        c1t = sb("c1t", [128, 2, 8], F32)
        nc1t = sb("nc1t", [128, 2, 8], F32)
        hgain_s = sb("hgain_s", [128, 8], F32)
        esink = sb("esink", [128, 8], F32)
        pmask = sb("pmask", [128, max(NPRE, 1) * 2], F32)
        stat = sb("stat", [128, 16], F32)
        kmT = sb("kmT", [128, 4, 128], BF16)
        vm = sb("vm", [128, 4, 2, 128], BF16)
        rowm8 = sb("rowm8", [128, 2], F32)
        arena = sb("arena", [128, 144128], mybir.dt.uint8)

        def carve(off_kb, shape, dt):
            nbytes = int(np.prod(shape[1:])) * (2 if dt == BF16 else 4)
            off = int(off_kb * 1024)
            assert off + nbytes <= 144128, (off_kb, shape)
            v = arena[:, off:off + nbytes].bitcast(dt)
            if len(shape) == 2:
                return v
            names = " ".join(f"a{i}" for i in range(len(shape) - 1))
            kw = {f"a{i}": shape[i + 1] for i in range(len(shape) - 1)}
            return v.rearrange(f"p ({names}) -> p {names}", **kw)

        x1 = carve(0, [128, 4, D], F32)
        xrow = [carve(0, [128, D], F32), carve(8, [128, D], F32)]
        xsf = [carve(16, [128, D], F32), carve(24, [128, D], F32)]
        mergedT = carve(32, [128, KC, TT], BF16)
        xT = carve(48, [128, KC, 768], BF16)
        oT_att = carve(72, [128, 8, TT], BF16)
        oT_hg = carve(80, [128, 8, TT], BF16)
        TB = 88
        qTa = carve(TB, [128, 2, 8, TT], BF16)
        kTd = carve(TB + 16, [128, 4, 768], BF16)
        vat = carve(TB + 22, [128, 6, 4, 2, 128], BF16)
        PT = [carve(TB + 34 + 4 * i, [128, 4, 512], BF16) for i in range(2)]
        scs = [carve(TB + 42 + 3 * i, [128, 3, 256], F32) for i in range(2)]
        rec = [carve(TB + 48 + i, [128, 256], F32) for i in range(2)]
        onesc = carve(TB + 50, [128, 4, 128], BF16)
        mg = [carve(TB + 2 * i, [128, TT], F32) for i in range(8)]
        zbuf = carve(48, [128, 4, D], F32)
        h2T = carve(96, [128, KC, TT], BF16)
        xsf2 = [carve(112, [128, D], F32), carve(120, [128, D], F32)]
        gpo = carve(128, [128, D], F32)
        uT = carve(32, [128, 64, TT], BF16)
        rtmp = [carve(112 + 2 * i, [128, TT], F32) for i in range(4)]
        z2 = carve(96, [128, 4, D], F32)
        junkb = carve(136, [128, D], BF16)

        ps = [pst(f"ps{i}", [128, 512]) for i in range(8)]
        psb = ps[7][:, :].bitcast(BF16)

        sems = {e: es.enter_context(nc.semaphore("s_" + e)) for e in ENGS}
        dsem = {}

        def dma(eng, out, in_, sem, reads=(), writes=()):
            if sem not in dsem:
                dsem[sem] = es.enter_context(nc.semaphore("d_" + sem))
            s = dsem[sem]
            return P.emit(eng, lambda e: e.dma_start(out=out, in_=in_).then_inc(s, 16),
                          reads=reads, writes=writes, dma_sem=sem)

        def dump(name, ap, keys):
            if not dbg:
                return
            t = nc.dram_tensor("dbg_" + name, list(ap.shape), F32, kind="ExternalOutput").ap()
            idx = tuple(slice(None) for _ in ap.shape)
            add(None, lambda slot: dma("pool", t[idx], ap, "dbg_" + name, reads=keys))

        def emit_m(eng, meth, kw, reads=(), writes=()):
            return P.emit(eng, lambda e: getattr(e, meth)(**kw), reads=reads, writes=writes)

        rr = {"ev": 0}

        def evac_eng():
            rr["ev"] ^= 1
            return "act" if rr["ev"] else "dve"

        def copy_op(eng, out, in_, reads, writes, scale=None):
            if eng == "act":
                if scale is None:
                    return emit_m("act", "activation", dict(out=out, in_=in_, func=AF.Copy), reads=reads, writes=writes)
                return emit_m("act", "activation", dict(out=out, in_=in_, func=AF.Copy, scale=scale), reads=reads, writes=writes)
            if scale is None:
                return P.emit(eng, lambda e: e.tensor_copy(out=out, in_=in_), reads=reads, writes=writes)
            return P.emit(eng, lambda e: e.tensor_scalar(out=out, in0=in_, scalar1=scale, scalar2=None, op0=ALU.mult), reads=reads, writes=writes)

        def act(out, in_, func, reads, writes, **kw):
            return emit_m("act", "activation", dict(out=out, in_=in_, func=func, **kw), reads=reads, writes=writes)

        def mm(out, lhsT, rhs, start, stop, reads, writes, **kw):
            return P.emit("pe", lambda e: e.matmul(out, lhsT=lhsT, rhs=rhs, start=start, stop=stop, **kw), reads=reads, writes=writes)

        cload = [
            (ident_f[:], identf[:, :]), (hgm[:], hgmask[:, :, :, :]), (rowm[:], rowmask[:, :]), (scm[:], scanmask[:, :]), (dtb[:], dtab[:, :, :]),
            (gpre_s[:], gpre[:, :, :]), (lb_s[:], lbl[:, :, :, :]), (hgain_s[:], hgain[:, :]), (esink[:], sink[:, :]),
            (pmask[:], premask[:, :]),
        ]
        for i, (o, s) in enumerate(cload):
            dma("sp", o, s, "const", writes=[("c", i)])
        CK = [("c", i) for i in range(len(cload))]

        late_casts = []

        def cast(dst, src, grp):
            if grp.startswith("g1"):
                dma("pool", dst, src, "cast_" + grp, writes=[("wb", grp)])
            else:
                late_casts.append(lambda: dma("pool", dst, src, "cast_" + grp, writes=[("wb", grp)]))

        def emit_casts(n):
            for _ in range(min(n, len(late_casts))):
                late_casts.pop(0)()

        def cast_cols(c0, c1, grp):
            for c in range(c0, c1, 512):
                cast(wb_in[:, c:c + 512], w_in[:, c:c + 512], grp)

        cast_cols(C_I, C_I + 1024, "g1")
        cast_cols(C_FF, C_FF + 2048, "g1")
        cast_cols(C_Q, C_Q + 1024, "g2")
        cast_cols(C_G, C_G + 1024, "g2")
        cast_cols(C_AQ, C_AQ + 1024, "g3")
        cast_cols(C_AK, C_AK + 512, "g1k")
        for g in range(4):
            for dup in range(2):
                cast(wb_kd[:, g * 128 + dup * 64: g * 128 + dup * 64 + 64], w_in[:, C_AK + g * 64: C_AK + g * 64 + 64], "g1k")
        cast_cols(C_GA, C_GA + 4096, "g4")
        for r in range(0, 1024, 512):
            cast(wb_p[r:r + 512, :], w_phg[r:r + 512, :], "g4")
            cast(wb_p[1024 + r:1024 + r + 512, :], w_patt[r:r + 512, :], "g4")
        for r in range(0, D, 512):
            cast(wb_out[r:r + 512, :], w_out[r:r + 512, :], "g5")
        for c in range(0, DFF, 512):
            cast(wb_ff1[:, c:c + 512], w_ff1[:, c:c + 512], "g6")
        for r in range(0, DFF, 512):
            cast(wb_ff2[r:r + 512, :], w_ff2[r:r + 512, :], "g7")

        copy_op("dve", ident_b[:], ident_f[:], CK, ["identb"])
        P.emit("dve", lambda e: e.memset(ones_b[:], 1.0), writes=["ones"])
        emit_m("dve", "tensor_tensor", dict(out=oml[:], in0=lb_s[:, 1], in1=lb_s[:, 0], op=ALU.subtract), reads=CK, writes=["oml"])
        act(oml[:], oml[:], AF.Exp, ["oml"], ["oml"])
        act(oml[:], oml[:], AF.Ln, ["oml"], ["oml"], bias=1.0)
        act(oml[:], oml[:], AF.Exp, ["oml"], ["oml"], scale=-1.0)
        emit_m("dve", "tensor_scalar", dict(out=oml[:], in0=oml[:], scalar1=-1.0, scalar2=1.0, op0=ALU.mult, op1=ALU.add), reads=["oml"], writes=["oml"])
        act(esink[:], esink[:], AF.Exp, CK, ["esink"])
        P.emit("dve", lambda e: e.memset(Sst[:], 0.0), writes=["S"])
        P.emit("pool", lambda e: e.memset(kmT[:], 0.0), writes=["kmT"])
        P.emit("pool", lambda e: e.memset(vm[:], 0.0), writes=["vm"])
        emit_m("dve", "tensor_scalar", dict(out=rowm8[:], in0=rowm[:], scalar1=0.125, scalar2=None, op0=ALU.mult), reads=CK, writes=["rowm8"])

        items = []

        def piece_in(c0, grp, kd=False):
            src = (wb_kd if kd else wb_in)[:, c0:c0 + 512].rearrange("(kc p) n -> p kc n", p=128)
            return {"src": src, "wkey": ("wb", grp), "nk": KC}

        def piece_rows(wb, r0, c0, grp, nk=KC):
            return {"src": wb[r0:r0 + nk * 128, c0:c0 + 512].rearrange("(kc p) n -> p kc n", p=128), "wkey": ("wb", grp), "nk": nk}

        def add(piece, fn):
            items.append((piece, fn))

        def add_barrier():
            items.append(("barrier", None))

        def set_c1(col):
            def fn(slot):
                if col is None:
                    copy_op("dve", c1t[:], oml[:], ["oml"], ["c1"])
                else:
                    for d in range(2):
                        emit_m("dve", "tensor_scalar", dict(out=c1t[:, d], in0=oml[:, d], scalar1=pmask[:, col + d:col + d + 1], scalar2=None, op0=ALU.mult), reads=["oml"] + CK, writes=["c1"])
                emit_m("dve", "tensor_scalar", dict(out=nc1t[:], in0=c1t[:], scalar1=-1.0, scalar2=None, op0=ALU.mult), reads=["c1"], writes=["c1"])
            add(None, fn)

        def prep(src_rows, nsub, col0, gsel, dstT, dkey, rows_key=None, xr=None, xs=None):
            xr = xr or xrow
            xsk = "xsf2" if xs is not None else None
            xs = xs or xsf
            xkey = (lambda b: ("xsf2", b)) if xsk else (lambda b: ("x1", 2 + b))

            def fn(slot):
                def issue(s):
                    dma("act", xr[s % 2][:], src_rows[s * 128:(s + 1) * 128, :], f"xrow{s % 2}", writes=[("x1", s % 2)])

                def srcs(s):
                    b = s % 2
                    if rows_key is None:
                        return xr[b][:], [("x1", b)]
                    return src_rows[:, s, :], [(rows_key, s)]

                def front(s):
                    b = s % 2
                    src, rk = srcs(s)
                    sc = stat[:, 2 * b:2 * b + 1]
                    act(xs[b][:], src, AF.Square, rk, [xkey(b), ("st", b)], accum_out=sc)
                    act(sc, sc, AF.Ln, [("st", b)], [("st", b)], scale=1.0 / D, bias=EPS)
                    act(sc, sc, AF.Exp, [("st", b)], [("st", b)], scale=-0.5)
                    emit_m("dve", "tensor_scalar", dict(out=xs[b][:], in0=src, scalar1=sc, scalar2=None, op0=ALU.mult), reads=rk + [("st", b)], writes=[xkey(b)])

                def back(s):
                    b = s % 2
                    for q4 in range(4):
                        pb = ps[q4]
                        pk_ = ("ps", q4)
                        for j in range(4):
                            kc = q4 * 4 + j
                            emit_m("pe", "transpose", dict(out=pb[:, j * 128:(j + 1) * 128], in_=xs[b][:, kc * 128:(kc + 1) * 128], identity=ident_f[:]),
                                   reads=[xkey(b)] + CK, writes=[pk_])
                        for j in range(4):
                            kc = q4 * 4 + j
                            copy_op(evac_eng(), dstT[:, kc, col0 + s * 128: col0 + (s + 1) * 128], pb[:, j * 128:(j + 1) * 128],
                                    [pk_] + CK, [dkey], scale=gpre_s[:, gsel, kc:kc + 1])

                if rows_key is None:
                    issue(0)
                    if nsub > 1:
                        issue(1)
                front(0)
                for s in range(nsub):
                    if s + 1 < nsub:
                        front(s + 1)
                    if rows_key is None and s + 2 < nsub:
                        issue(s + 2)
                    back(s)
            add(None, fn)

        cnt8 = {"b": 0}

        xTL = [carve(48, [128, KC, TT], BF16), carve(64, [128, KC, TT], BF16)]
        xrL = [carve(40, [128, D], F32), carve(80, [128, D], F32)]
        junkL = carve(24, [128, D], BF16)

        def prep_light(src_rows, dstT, dkey, nxt_rows=None):
            def issue(rows, s):
                dma("act", xrL[s % 2][:], rows[s * 128:(s + 1) * 128, :], f"xrL{s % 2}", writes=[("xrL", s % 2)])

            def front(s):
                b = s % 2
                src = xrL[b][:]
                rk = [("xrL", b)]
                sc = stat[:, 8 + b:9 + b]
                act(junkL[:], src, AF.Square, rk, ["junkL", ("stL", b)], accum_out=sc)
                act(sc, sc, AF.Ln, [("stL", b)], [("stL", b)], scale=1.0 / D, bias=EPS)
                act(sc, sc, AF.Exp, [("stL", b)], [("stL", b)], scale=-0.5)
                emit_m("dve", "tensor_scalar", dict(out=src, in0=src, scalar1=sc, scalar2=None, op0=ALU.mult), reads=rk + [("stL", b)], writes=rk)

            def back(s):
                b = s % 2
                rk = [("xrL", b)]
                for q4 in range(4):
                    bkk = (3, 6)[q4 % 2]
                    pb = ps[bkk]
                    for j in range(4):
                        kc = q4 * 4 + j
                        P.emit("pe", (lambda pb=pb, j=j, kc=kc, b=b: (lambda e: e.transpose(out=pb[:, j * 128:(j + 1) * 128], in_=xrL[b][:, kc * 128:(kc + 1) * 128], identity=ident_f[:])))(),
                               reads=rk + CK, writes=[("ps", bkk)])
                    for j in range(4):
                        kc = q4 * 4 + j
                        copy_op(evac_eng(), dstT[:, kc, s * 128:(s + 1) * 128], pb[:, j * 128:(j + 1) * 128], [("ps", bkk)] + CK, [dkey], scale=gpre_s[:, 0, kc:kc + 1])

            def p0():
                front(0)

            def p1():
                back(0)
                issue(src_rows, 2)
                front(1)

            def p2():
                back(1)
                issue(src_rows, 3)
                front(2)

            def p3():
                back(2)
                front(3)

            def p4():
                back(3)
                if nxt_rows is not None:
                    issue(nxt_rows, 0)
                    issue(nxt_rows, 1)
            return [p0, p1, p2, p3, p4], (lambda: (issue(src_rows, 0), issue(src_rows, 1)))

        def proj_fm(piece, xTv, ncols_tok, dst_fn, xkey, ntile=4, nb8=4):
            def fn(slot):
                for j in range(ntile):
                    cnt8["b"] = (cnt8["b"] + 1) % nb8
                    bk = cnt8["b"]
                    for kc in range(KC):
                        mm(ps[bk][:, :ncols_tok], wsl[slot][:, kc, j * 128:(j + 1) * 128], xTv(kc), kc == 0, kc == KC - 1,
                           [("w", slot), xkey], [("ps", bk)])
                    dst_fn(j, ps[bk], ("ps", bk))
            add(piece, fn)

        def proj_tm(piece, xTv_sub, nsub, dst_fn, xkey, ncol=512, nk=KC, msize=128):
            def fn(slot):
                for s in range(nsub):
                    bk = 4 + (s % 3)
                    for kc in range(nk):
                        mm(ps[bk][:msize, :ncol], xTv_sub(kc, s), wsl[slot][:, kc, :ncol], kc == 0, kc == nk - 1,
                           [("w", slot), xkey], [("ps", bk)])
                    dst_fn(s, ps[bk], ("ps", bk))
            add(piece, fn)

        vpar = [carve(0, [128, 4, 1024], BF16), carve(8, [128, 4, 1024], BF16)]
        kinvT = carve(16, [128, 8, TT], BF16)
        qdecT = carve(24, [128, 8, TT], BF16)
        ktok = carve(32, [128, 8, TT], BF16)
        gsb = carve(40, [128, 8, TT], BF16)
        o_sb = carve(88, [128, 8, TT], F32)
        NTS = 3
        Tset = [[carve(104 + 10 * j + 2 * i, [128, TT], F32) for i in range(5)] for j in range(NTS)]
        scb = [carve(134 + i, [128, 8, 64], BF16) for i in range(2)]
        onesf = carve(134, [128, TT], F32)
        Sbf = [carve(136 + 2 * i, [128, 8, 128], BF16) for i in range(2)]
        decs = carve(140, [128, 8, 8], F32)
        cnt = {"bank": 0, "ts": 0}

        def nbank(m=4):
            cnt["bank"] = (cnt["bank"] + 1) % m
            return cnt["bank"]

        def sigm_neg(dst, src, rk, wk, n, sgn=1.0):
            act(dst[:, :n], src, AF.Exp, rk, [wk], scale=sgn)
            act(dst[:, :n], dst[:, :n], AF.Ln, [wk], [wk], bias=1.0)
            act(dst[:, :n], dst[:, :n], AF.Exp, [wk], [wk], scale=-1.0)

        deferred = []

        def defer(fn):
            deferred.append(fn)

        def flush(keep=0):
            while len(deferred) > keep:
                deferred.pop(0)()

        def proj_head(piece, j, xv, ntok, consume, xkey="xT", nbk=4, pre=None):
            def fn(slot):
                bk = nbank(nbk)
                for kc in range(KC):
                    mm(ps[bk][:, :ntok], wsl[slot][:, kc, j * 128:(j + 1) * 128], xv(kc), kc == 0, kc == KC - 1, [("w", slot), xkey], [("ps", bk)])
                flush(2)
                if pre is not None:
                    pre()
                consume(ps[bk], ("ps", bk))
            add(piece, fn)

        def hg_v(xvs, nsub, msz, par, xkey="xT"):
            for half in range(2):
                def dst(s, pb, pk, half=half):
                    if par:
                        copy_op("act", vpar[0][:, s, half * 512:(half + 1) * 512], pb[:, :], [pk] + CK, ["vpar"], scale=rowm[:, 0:1])
                        copy_op("dve", vpar[1][:, s, half * 512:(half + 1) * 512], pb[:, :], [pk] + CK, ["vpar"], scale=rowm[:, 1:2])
                    else:
                        copy_op(evac_eng(), vpar[0][:msz, s, half * 512:(half + 1) * 512], pb[:msz, :], [pk], ["vpar"])
                proj_tm(piece_in(C_I + half * 512, "g1"), xvs, nsub, dst, xkey, msize=msz)

        def hg_pre(xcol0, ntok, dirs, xb=None, inter=()):
            nsub = (ntok + 127) // 128
            msz = min(128, ntok)
            xt_, xkey = (xT, "xT") if xb is None else xb
            nbk = 4 if xb is None else 3
            inter = list(inter)
            nhead = [0]
            pend = []
            xv = lambda kc: xt_[:, kc, xcol0:xcol0 + ntok]
            xvs = lambda kc, s: xt_[:, kc, xcol0 + s * 128: xcol0 + s * 128 + msz]
            add(None, lambda slot: P.emit("pool", lambda e: e.memset(onesf[:], 1.0), writes=["onesf"]))
            if inter:
                add(None, lambda slot, f=inter.pop(0): f())
            hg_v(xvs, nsub, msz, False, xkey=xkey)
            fcols = {0: C_FF, 1: C_FB}
            for hp in range(2):
                for dr in dirs:
                    pf = piece_in(fcols[dr] + hp * 512, "g1")
                    for j in range(4):
                        h = hp * 4 + j

                        def consume(pb, pk, h=h, dr=dr):
                            r = cnt["ts"] = (cnt["ts"] + 1) % NTS
                            T = Tset[r]
                            tk = ("T", r)
                            n = ntok
                            prev = pend.pop(0) if pend else None
                            if prev:
                                prev[0]()
                            sigm_neg(T[0], pb[:, :n], [pk], tk, n)
                            act(T[1][:, :n], T[0][:, :n], AF.Ln, [tk, "c1"], [tk], scale=nc1t[:, dr, h:h + 1], bias=1.0)
                            emit_m("dve", "tensor_tensor_scan", dict(out=T[2][:, :n], data0=onesf[:, :n], data1=T[1][:, :n], initial=0.0, op0=ALU.mult, op1=ALU.add),
                                   reads=[tk, "onesf"], writes=[tk])

                            def part2a(T=T, tk=tk, dr=dr):
                                if dr == 1:
                                    emit_m("dve", "tensor_copy", dict(out=T[4][:, 1:2], in_=T[2][:, n - 1:n]), reads=[tk], writes=[tk])
                                    emit_m("dve", "tensor_tensor", dict(out=T[2][:, :n], in0=T[2][:, :n], in1=T[1][:, :n], op=ALU.subtract), reads=[tk], writes=[tk])

                            def part2b(T=T, tk=tk, dr=dr, h=h):
                                if dr == 0:
                                    act(T[4][:, 0:1], T[2][:, n - 1:n], AF.Exp, [tk], [tk])
                                    act(T[3][:, :n], T[2][:, :n], AF.Exp, [tk], [tk], scale=-1.0, bias=T[2][:, n - 1:n])
                                else:
                                    act(T[4][:, 0:1], T[4][:, 1:2], AF.Exp, [tk], [tk])
                                    act(T[3][:, :n], T[2][:, :n], AF.Exp, [tk], [tk])
                                emit_m("dve", "scalar_tensor_tensor", dict(out=kinvT[:, h, :n], in0=T[0][:, :n], scalar=c1t[:, dr, h:h + 1], in1=T[3][:, :n], op0=ALU.mult, op1=ALU.mult),
                                       reads=[tk, "c1"], writes=[("kinvT", h)])
                                defer(mktail(h, dr, T, tk))
                            pend.append((part2a, part2b))
                            if prev:
                                prev[1]()

                        def mktail(h, dr, T, tk):
                            def tail(h=h, dr=dr, T=T, tk=tk):
                                for s in range(nsub):
                                    P.emit("pe", (lambda s=s: (lambda e: e.transpose(out=psb[:msz, s * 128:(s + 1) * 128], in_=kinvT[:, h, s * 128:s * 128 + msz], identity=ident_b[:])))(),
                                           reads=[("kinvT", h), "identb"], writes=[("ps", 7)])
                                copy_op(evac_eng(), ktok[:msz, h, :nsub * 128], psb[:msz, :nsub * 128], [("ps", 7)], [("ktok", h)])
                                ib = 4 + (h // 4)
                                for s in range(nsub):
                                    mm(ps[ib][:, (h % 4) * 128:(h % 4 + 1) * 128], ktok[:msz, h, s * 128:(s + 1) * 128], vpar[0][:msz, s, h * 128:(h + 1) * 128], s == 0, s == nsub - 1,
                                       [("ktok", h), "vpar"], [("ps", ib)])
                                Sv = Sst[:, dr, h, :]
                                emit_m("dve", "scalar_tensor_tensor", dict(out=Sv, in0=Sv, scalar=T[4][:, 0:1], in1=ps[ib][:, (h % 4) * 128:(h % 4 + 1) * 128], op0=ALU.mult, op1=ALU.add),
                                       reads=[("S", dr, h), tk, ("ps", ib)], writes=[("S", dr, h)])
                            return tail
                        pre_ = None
                        if inter and nhead[0] > 0 and nhead[0] % 4 == 0:
                            pre_ = inter.pop(0)
                        proj_head(pf, j, xv, ntok, consume, xkey=xkey, nbk=nbk, pre=pre_)
                        nhead[0] += 1
                        if xb is not None and nhead[0] in (2, 7, 12):
                            add(None, lambda slot: emit_casts(-(-(-(-63 // max(1, NPRE - 2))) // 3)))

            def drain(slot):
                while pend:
                    a, b = pend.pop(0)
                    a()
                    b()
                flush()
            add(None, drain)
            for f in inter:
                add(None, lambda slot, f=f: f())

        def hg_main(mode, tok0, xcol0):
            dr = 1 if mode == "bwd" else 0
            ntok = TT
            xv = lambda kc: xT[:, kc, xcol0:xcol0 + ntok]
            xvs = lambda kc, s: xT[:, kc, xcol0 + s * 128: xcol0 + s * 128 + 128]
            if mode == "fwd":
                add(None, lambda slot: dma("pool", o_sb[:], ob_d[:, :, tok0:tok0 + TT].rearrange("h p t -> p h t"), "osb", reads=[("ob_d", tok0)], writes=["osb"]))
            hg_v(xvs, 4, 128, True)
            fcol = C_FB if dr else C_FF
            for hp in range(2):
                pf = piece_in(fcol + hp * 512, "g1")
                pq = piece_in(C_Q + hp * 512, "g2")
                pg = piece_in(C_G + hp * 512, "g2") if mode == "fwd" else None
                hold = {}

                def mk(h, hold=hold):
                        def cf(pb, pk, h=h, hold=hold):
                            r = cnt["ts"] = (cnt["ts"] + 1) % NTS
                            hold[h] = r
                            T = Tset[r]
                            tk = ("T", r)
                            sigm_neg(T[0], pb[:, :], [pk], tk, TT)
                            act(T[1][:], T[0][:], AF.Ln, [tk, "c1"], [tk], scale=nc1t[:, dr, h:h + 1], bias=1.0)
                            emit_m("dve", "tensor_tensor_scan", dict(out=T[2][:], data0=scm[:], data1=T[1][:], initial=0.0, op0=ALU.mult, op1=ALU.add), reads=[tk] + CK, writes=[tk])
                            act(decs[:, h, :], T[2][:, 63::64], AF.Exp, [tk], [("dec", h)])
                            if dr == 0:
                                act(T[3][:], T[2][:], AF.Exp, [tk], [tk])
                                act(T[4][:], T[2][:], AF.Exp, [tk], [tk], scale=-1.0)
                            else:
                                emit_m("dve", "tensor_tensor", dict(out=T[2][:], in0=T[2][:], in1=T[1][:], op=ALU.subtract), reads=[tk], writes=[tk])
                                act(T[4][:], T[2][:], AF.Exp, [tk], [tk])
                                act(T[3][:], T[2][:], AF.Exp, [tk], [tk], scale=-1.0)
                            emit_m("dve", "scalar_tensor_tensor", dict(out=kinvT[:, h, :], in0=T[0][:], scalar=c1t[:, dr, h:h + 1], in1=T[4][:], op0=ALU.mult, op1=ALU.mult),
                                   reads=[tk, "c1"], writes=[("kinvT", h)])
                            def tail(h=h):
                                for s in range(4):
                                    P.emit("pe", (lambda s=s: (lambda e: e.transpose(out=psb[:, s * 128:(s + 1) * 128], in_=kinvT[:, h, s * 128:(s + 1) * 128], identity=ident_b[:])))(),
                                           reads=[("kinvT", h), "identb"], writes=[("ps", 7)])
                                copy_op(evac_eng(), ktok[:, h, :], psb[:, :512], [("ps", 7)], [("ktok", h)])
                            defer(tail)
                        def cq(pb, pk, h=h, hold=hold):
                            T = Tset[hold[h]]
                            tk = ("T", hold[h])
                            sigm_neg(T[1], pb[:, :], [pk], tk, TT, sgn=-1.0)
                            emit_m("dve", "tensor_tensor", dict(out=T[1][:], in0=T[1][:], in1=pb[:, :], op=ALU.mult), reads=[tk, pk], writes=[tk])
                            emit_m("dve", "scalar_tensor_tensor", dict(out=qdecT[:, h, :], in0=T[1][:], scalar=float(128 ** -0.5), in1=T[3][:], op0=ALU.mult, op1=ALU.mult),
                                   reads=[tk], writes=[("qdecT", h)])

                        return cf, cq

                def mkg(h):
                    def cg(pb, pk, h=h):
                        r = cnt["ts"] = (cnt["ts"] + 1) % NTS
                        T = Tset[r]
                        tk = ("T", r)
                        sigm_neg(T[0], pb[:, :], [pk], tk, TT, sgn=-1.0)
                        emit_m("dve", "tensor_tensor", dict(out=gsb[:, h, :], in0=T[0][:], in1=pb[:, :], op=ALU.mult), reads=[tk, pk], writes=[("gs", h)])

                    return cg
                fns = {hp * 4 + j: mk(hp * 4 + j) for j in range(4)}
                for pr2 in range(2):
                    for j in (2 * pr2, 2 * pr2 + 1):
                        proj_head(pf, j, xv, ntok, fns[hp * 4 + j][0])
                    for j in (2 * pr2, 2 * pr2 + 1):
                        proj_head(pq, j, xv, ntok, fns[hp * 4 + j][1])
                if mode == "fwd":
                    for j in range(4):
                        proj_head(pg, j, xv, ntok, mkg(hp * 4 + j))

            add(None, lambda slot: flush())

            def scan(slot):
                order = list(range(8)) if dr == 0 else list(range(7, -1, -1))

                def sc_mask(i):
                    c = order[i]
                    pr, par = c // 2, c % 2
                    bk = i % 2
                    for h in range(8):
                        mm(ps[bk][:, h * 64:(h + 1) * 64], kinvT[:, h, pr * 128:(pr + 1) * 128], qdecT[:, h, c * 64:(c + 1) * 64], True, True,
                           [("kinvT", h), ("qdecT", h)], [("ps", bk)])
                    for h in range(8):
                        emit_m("dve", "tensor_tensor", dict(out=scb[bk][:, h, :], in0=ps[bk][:, h * 64:(h + 1) * 64], in1=hgm[:, dr, par, :], op=ALU.mult),
                               reads=[("ps", bk)] + CK, writes=[("scb", bk)])

                def sbf(i, h):
                    c = order[i]
                    Sv = Sst[:, dr, h, :]
                    if dr == 0:
                        sc_ = None if i == 0 else decs[:, h, order[i - 1]:order[i - 1] + 1]
                    else:
                        sc_ = decs[:, h, c:c + 1]
                    copy_op("act", Sbf[i % 2][:, h, :], Sv, [("S", dr, h), ("dec", h)], [("Sbf", i % 2, h)], scale=sc_)

                for h in range(8):
                    sbf(0, h)
                sc_mask(0)
                for i in range(8):
                    c = order[i]
                    pr, par = c // 2, c % 2
                    if i + 1 < 8:
                        sc_mask(i + 1)
                    ob = 2 + (i % 2)
                    for h in range(8):
                        oo = ps[ob][:, h * 64:(h + 1) * 64]
                        mm(oo, vpar[par][:, pr, h * 128:(h + 1) * 128], scb[i % 2][:, h, :], True, False, ["vpar", ("scb", i % 2)], [("ps", ob)])
                        mm(oo, Sbf[i % 2][:, h, :], qdecT[:, h, c * 64:(c + 1) * 64], False, True, [("Sbf", i % 2, h), ("qdecT", h)], [("ps", ob)])
                    osl = o_sb[:, :, c * 64:(c + 1) * 64]
                    pso = ps[ob][:, :].rearrange("p (h t) -> p h t", h=8)
                    if mode == "fwd":
                        emit_m("dve", "tensor_tensor", dict(out=osl, in0=osl, in1=pso, op=ALU.add), reads=[("ps", ob), "osb"], writes=["osb"])
                    else:
                        copy_op("act", osl, pso, [("ps", ob)], ["osb"])
                    for h in range(8):
                        ib = 4 + (h // 4)
                        mm(ps[ib][:, (h % 4) * 128:(h % 4 + 1) * 128], ktok[:, h, pr * 128:(pr + 1) * 128], vpar[par][:, pr, h * 128:(h + 1) * 128], True, True,
                           [("ktok", h), "vpar"], [("ps", ib)])
                    for h in range(8):
                        Sv = Sst[:, dr, h, :]
                        if dr == 0:
                            sc_ = 1.0 if i == 0 else decs[:, h, order[i - 1]:order[i - 1] + 1]
                        else:
                            sc_ = decs[:, h, c:c + 1]
                        emit_m("dve", "scalar_tensor_tensor", dict(out=Sv, in0=Sv, scalar=sc_, in1=ps[4 + (h // 4)][:, (h % 4) * 128:(h % 4 + 1) * 128], op0=ALU.mult, op1=ALU.add),
                               reads=[("S", dr, h), ("ps", 4 + (h // 4)), ("dec", h)], writes=[("S", dr, h)])
                    if i + 1 < 8:
                        for h in range(8):
                            sbf(i + 1, h)
                if dr == 0:
                    for h in range(8):
                        Sv = Sst[:, dr, h, :]
                        emit_m("dve", "tensor_scalar", dict(out=Sv, in0=Sv, scalar1=decs[:, h, 7:8], scalar2=None, op0=ALU.mult), reads=[("S", dr, h), ("dec", h)], writes=[("S", dr, h)])
            add(None, scan)

            if mode == "bwd":
                add(None, lambda slot: dma("pool", ob_d[:, :, tok0:tok0 + TT].rearrange("h p t -> p h t"), o_sb[:], "osb", reads=["osb"], writes=[("ob_d", tok0)]))
            else:
                def fin(slot):
                    for h in range(8):
                        r = cnt["ts"] = (cnt["ts"] + 1) % NTS
                        T = Tset[r]
                        tk = ("T", r)
                        sqb = scb[0] if h % 2 == 0 else scb[1]
                        sq = kinvT[:, h, :]
                        act(sq, o_sb[:, h, :], AF.Square, ["osb"], [("kinvT", h)])
                        mm(ps[6][:, :], ones_b[:], sq, True, True, ["ones", ("kinvT", h)], [("ps", 6)])
                        act(T[1][:], ps[6][:, :], AF.Ln, [("ps", 6)], [tk], scale=1.0 / 128, bias=EPS)
                        act(T[1][:], T[1][:], AF.Exp, [tk], [tk], scale=-0.5)
                        emit_m("dve", "scalar_tensor_tensor", dict(out=T[0][:], in0=o_sb[:, h, :], scalar=hgain_s[:, h:h + 1], in1=T[1][:], op0=ALU.mult, op1=ALU.mult), reads=[tk, "osb"] + CK, writes=[tk])
                        emit_m("dve", "tensor_tensor", dict(out=oT_hg[:, h, :], in0=T[0][:], in1=gsb[:, h, :], op=ALU.mult), reads=[tk, ("gs", h)], writes=["oT_hg"])
                add(None, fin)

        def hg_tile(mode, tok0, xcol0, ntok=TT):
            if mode == "meta":
                hg_pre(xcol0, ntok, [0])
            elif mode == "pre":
                hg_pre(xcol0, ntok, [0, 1])
            else:
                hg_main(mode, tok0, xcol0)

        def attn_kv(xcol0, ntok, kdst, vdst, vkey, kkey):
            def kd(j, pb, pk, n0, n):
                copy_op(evac_eng(), kdst[:, j, n0:n0 + n], pb[:, :n], [pk], [kkey])
            for n0 in range(0, ntok, 512):
                n = min(512, ntok - n0)
                proj_fm(piece_in(0, "g1k", kd=True), lambda kc, n0=n0, n=n: xT[:, kc, xcol0 + n0: xcol0 + n0 + n], n,
                        lambda j, pb, pk, n0=n0, n=n: kd(j, pb, pk, n0, n), "xT")
            nsub = (ntok + 127) // 128
            msz = min(128, ntok)

            def vd(s, pb, pk):
                src = pb[:msz, 256:512].rearrange("p (g d) -> p g d", g=4)
                if ntok == 16:
                    d0, d1 = vdst[:msz, :, 0, 0:64], vdst[:msz, :, 1, 64:128]
                else:
                    d0, d1 = vdst[:msz, s, :, 0, 0:64], vdst[:msz, s, :, 1, 64:128]
                copy_op("act", d0, src, [pk], [vkey])
                copy_op("dve", d1, src, [pk], [vkey])
            proj_tm(piece_in(C_AV - 256, "g1k"), lambda kc, s: xT[:, kc, xcol0 + s * 128: xcol0 + s * 128 + msz], nsub, vd, "xT", msize=msz)

        slopes = [float(2.0 ** (-8.0 * (h + 1) / 16.0)) for h in range(16)]

        def attn_tile(ti):
            for half in range(2):
                def qd(j, pb, pk, half=half):
                    copy_op("act", qTa[:, 0, half * 4 + j, :], pb[:, :], [pk, "rowm8"], ["qTa"], scale=rowm8[:, 0:1])
                    copy_op("dve", qTa[:, 1, half * 4 + j, :], pb[:, :], [pk, "rowm8"], ["qTa"], scale=rowm8[:, 1:2])
                proj_fm(piece_in(C_AQ + half * 512, "g3"), lambda kc: xT[:, kc, 128:128 + TT], TT, qd, "xT")
            add(None, lambda slot: P.emit("pool", lambda e: e.memset(vat[:], 0.0), writes=["vat"]))
            add(None, lambda slot: dma("pool", onesc[:], onesc_d[:, :, :], "onesc", writes=["onesc"]))
            attn_kv(0, 768, kTd, vat, "vat", "kTd")

            def blocks(slot):
                its = [(qb, g, hl) for qb in range(4) for g in range(4) for hl in range(2)]

                def A(n):
                    qb, g, hl = its[n]
                    p2 = (n % 2) * 2
                    rq = qTa[:, hl, 2 * g:2 * g + 2, qb * 128:(qb + 1) * 128]
                    for kb in range(3):
                        bank = p2 + (kb // 2)
                        co = (kb % 2) * 256
                        mm(ps[bank][:, co:co + 256].rearrange("p (a b) -> p a b", a=2), kTd[:, g, (qb + kb) * 128:(qb + kb + 1) * 128], rq, True, True, ["kTd", "qTa"], [("ps", bank)])
                    mm(ps[p2 + 1][:, 256:512].rearrange("p (a b) -> p a b", a=2), kmT[:, g, :], rq, True, True, ["kmT", "qTa"], [("ps", p2 + 1)])

                def B(n):
                    qb, g, hl = its[n]
                    p2 = (n % 2) * 2
                    first = (ti == 0 and qb == 0)
                    last = (ti == NT - 1 and qb == 3)
                    pi = (qb * 4 + g) % 2
                    pt = PT[pi]
                    sc = scs[n % 2]
                    sk = ("scs", n % 2)
                    for kb in range(3):
                        dsel = kb
                        if kb == 0 and first:
                            dsel = 3
                        if kb == 2 and last:
                            dsel = 4
                        bank = p2 + (kb // 2)
                        co = (kb % 2) * 256
                        for c in range(2):
                            head = 4 * g + 2 * c + hl
                            emit_m("dve", "scalar_tensor_tensor", dict(out=sc[:, kb, c * 128:(c + 1) * 128], in0=dtb[:, dsel, :], scalar=slopes[head],
                                                                        in1=ps[bank][:, co + c * 128: co + (c + 1) * 128], op0=ALU.mult, op1=ALU.add),
                                   reads=[("ps", bank)] + CK, writes=[sk])
                    act(pt[:, 0:3, hl * 256:(hl + 1) * 256], sc[:, :, :], AF.Exp, [sk], [("PT", pi, hl)])
                    act(pt[:, 3, hl * 256:(hl + 1) * 256], ps[p2 + 1][:, 256:512], AF.Exp, [("ps", p2 + 1)], [("PT", pi, hl)])

                def C(n):
                    qb, g, hl = its[n]
                    pi = (qb * 4 + g) % 2
                    pt = PT[pi]
                    ptk = ("PT", pi, hl)
                    nb = 4 + 2 * pi
                    for kb in range(4):
                        if kb < 3:
                            lv = vat[:, qb + kb, g, hl, :]
                        else:
                            lv = vm[:, g, hl, :]
                        mm(ps[nb][:, 0:256], lv, pt[:, kb, hl * 256:(hl + 1) * 256], hl == 0 and kb == 0, hl == 1 and kb == 3, ["vat", "vm", ptk], [("ps", nb)])
                    for kb in range(4):
                        lo = onesc[:, (0 if kb < 3 else 2) + hl, :]
                        mm(ps[nb + 1][:, 0:256], lo, pt[:, kb, hl * 256:(hl + 1) * 256], hl == 0 and kb == 0, hl == 1 and kb == 3, ["onesc", ptk], [("ps", nb + 1)])
                    if hl == 1:
                        rc = rec[pi]
                        rk = ("rec", pi)
                        for c in range(2):
                            ch = 2 * g + c
                            act(rc[:, c * 128:(c + 1) * 128], ps[nb + 1][:, c * 128:(c + 1) * 128], AF.Ln, [("ps", nb + 1), "esink"], [rk], bias=esink[:, ch:ch + 1])
                        act(rc[:], rc[:], AF.Exp, [rk], [rk], scale=-1.0)
                        for c in range(2):
                            ch = 2 * g + c
                            emit_m("dve", "tensor_tensor", dict(out=oT_att[:, ch, qb * 128:(qb + 1) * 128], in0=ps[nb][:, c * 128:(c + 1) * 128], in1=rc[:, c * 128:(c + 1) * 128], op=ALU.mult),
                                   reads=[("ps", nb), rk], writes=["oT_att"])

                N = len(its)
                A(0)
                B(0)
                for n in range(N):
                    if n + 1 < N:
                        A(n + 1)
                        B(n + 1)
                    C(n)
            add(None, blocks)

        def merge_tile():
            xv = lambda kc: xT[:, kc, 128:128 + TT]
            for ng in range(4):
                def fa(slot, ng=ng):
                    pass
                st = {}

                def ga_item(slot, ng=ng, st=st):
                    st["ga"] = slot
                pa_ = piece_in(C_GA + ng * 512, "g4")
                add(pa_, ga_item)

                def gb_item(slot, ng=ng, st=st):
                    st["gb"] = slot
                pb_ = piece_in(C_GB + ng * 512, "g4")
                add(pb_, gb_item)

                def p_item(slot, ng=ng, st=st):
                    sa, sbb, sp_ = st["ga"], st["gb"], slot
                    for j in range(4):
                        nch = ng * 4 + j
                        cj = slice(j * 128, (j + 1) * 128)
                        b0 = 4 * (j % 2)
                        for kc in range(KC):
                            mm(ps[b0][:, :], wsl[sa][:, kc, cj], xv(kc), kc == 0, kc == KC - 1, [("w", sa), "xT"], [("ps", b0)])
                        for kc in range(KC):
                            mm(ps[b0 + 1][:, :], wsl[sbb][:, kc, cj], xv(kc), kc == 0, kc == KC - 1, [("w", sbb), "xT"], [("ps", b0 + 1)])
                        for kc in range(8):
                            mm(ps[b0 + 2][:, :], wsl[sp_][:, kc, cj], oT_hg[:, kc, :], kc == 0, kc == 7, [("w", sp_), "oT_hg"], [("ps", b0 + 2)])
                        for kc in range(8):
                            mm(ps[b0 + 3][:, :], wsl[sp_][:, 8 + kc, cj], oT_att[:, kc, :], kc == 0, kc == 7, [("w", sp_), "oT_att"], [("ps", b0 + 3)])
                        m0 = mg[(j % 2) * 4: (j % 2) * 4 + 4]
                        mk = ("mg", j % 2)
                        for gi in range(2):
                            act(m0[gi][:], ps[b0 + gi][:, :], AF.Exp, [("ps", b0 + gi)], [mk], scale=-1.0)
                            act(m0[gi][:], m0[gi][:], AF.Ln, [mk], [mk], bias=1.0)
                            act(m0[gi][:], m0[gi][:], AF.Exp, [mk], [mk], scale=-1.0)
                        emit_m("dve", "tensor_tensor", dict(out=m0[2][:], in0=m0[0][:], in1=ps[b0 + 2][:, :], op=ALU.mult), reads=[mk, ("ps", b0 + 2)], writes=[mk])
                        emit_m("dve", "tensor_tensor", dict(out=m0[3][:], in0=m0[1][:], in1=ps[b0 + 3][:, :], op=ALU.mult), reads=[mk, ("ps", b0 + 3)], writes=[mk])
                        emit_m("pool", "tensor_tensor", dict(out=mergedT[:, nch, :], in0=m0[2][:], in1=m0[3][:], op=ALU.add), reads=[mk], writes=["mergedT"])
                add({"src": wb_p[:, ng * 512:(ng + 1) * 512].rearrange("(kc p) n -> p kc n", p=128), "wkey": ("wb", "g4"), "nk": KC}, p_item)
                add(pa_, lambda slot: None)
                add(pb_, lambda slot: None)

        def tm_proj_norm(srcT, skey, wb, grp, nkp, zb, zkey, gsel, resid_load, out_store, ti):
            add(None, lambda slot: dma("pool", gpo[:], gpost[gsel:gsel + 1, :].partition_broadcast(128), "gpo", writes=["gpo"]))
            for ng in range(4):
                for kp in range(nkp):
                    def item(slot, ng=ng, kp=kp):
                        for s in range(4):
                            bk = s + (4 * (ng % 2))
                            for kc in range(KC):
                                kk = kp * KC + kc
                                mm(ps[bk][:, :], srcT[:, kk, s * 128:(s + 1) * 128], wsl[slot][:, kc, :], kk == 0, kk == nkp * KC - 1, [("w", slot), skey], [("ps", bk)])
                            if kp == nkp - 1:
                                copy_op(evac_eng(), zb[:, s, ng * 512:(ng + 1) * 512], ps[bk][:, :], [("ps", bk)], [(zkey, s)])
                    add(piece_rows(wb, kp * KC * 128, ng * 512, grp), item)

            def fin(slot):
                for s in range(4):
                    b = s % 2
                    sc = stat[:, 4 + b:5 + b]
                    junk = junkb if zkey == "z2" else xsf2[b]
                    if resid_load:
                        dma("pool", x1[:, s, :], x_ext[128 + ti * TT + s * 128: 128 + ti * TT + (s + 1) * 128, :], f"x1l{s}", writes=[("x1", s)])
                    act(junk[:], zb[:, s, :], AF.Square, [(zkey, s)], [("junkb" if zkey == "z2" else ("xsf2", b)), ("st2", b)], accum_out=sc)
                    act(sc, sc, AF.Ln, [("st2", b)], [("st2", b)], scale=1.0 / D, bias=EPS)
                    act(sc, sc, AF.Exp, [("st2", b)], [("st2", b)], scale=-0.5)
                    emit_m("dve", "scalar_tensor_tensor", dict(out=zb[:, s, :], in0=zb[:, s, :], scalar=sc, in1=gpo[:], op0=ALU.mult, op1=ALU.mult),
                           reads=[(zkey, s), ("st2", b), "gpo"], writes=[(zkey, s)])
                    emit_m("pool", "tensor_tensor", dict(out=x1[:, s, :], in0=x1[:, s, :], in1=zb[:, s, :], op=ALU.add), reads=[(zkey, s), ("x1", s)], writes=[("x1", s)])
                    if out_store:
                        dma("pool", y[ti * TT + s * 128: ti * TT + (s + 1) * 128, :], x1[:, s, :], f"yst{s}", reads=[("x1", s)])
            add(None, fin)

        def ffn1_tile():
            for pc in range(16):
                def dst(j, pb, pk, pc=pc):
                    r = rtmp[j % 4]
                    rk = ("rtmp", j % 4)
                    act(r[:], pb[:, :], AF.Relu, [pk], [rk])
                    emit_m("pool", "tensor_tensor", dict(out=uT[:, pc * 4 + j, :], in0=r[:], in1=r[:], op=ALU.mult), reads=[rk], writes=["uT"])
                proj_fm(piece_rows(wb_ff1, 0, pc * 512, "g6"), lambda kc: h2T[:, kc, :], TT, dst, "h2T", nb8=8)

        prep(meta, 1, 128, 0, xT, "xT")
        add_barrier()
        dump("xT_meta", xT[:, :, 128:144], ["xT"])
        set_c1(None)
        attn_kv(128, 16, kmT, vm, "vm", "kmT")
        hg_tile("meta", 0, 128, ntok=16)
        add_barrier()
        dump("kmT", kmT[:], ["kmT"])
        dump("vm", vm[:], ["vm"])
        dump("oml", oml[:], ["oml"])
        dump("S_meta", Sst[:], [("S", d_, h_) for d_ in range(2) for h_ in range(8)])
        def pre_rows(t):
            return x_pre[t * TT:(t + 1) * TT, :] if t < NPRE else None

        if NPRE > 0:
            cl, pre_issue = prep_light(pre_rows(0), xTL[0], "xTL0", pre_rows(1))
            add(None, lambda slot: pre_issue())
            for f in cl:
                add(None, lambda slot, f=f: f())
        for pt_ in range(NPRE):
            nxt = prep_light(pre_rows(pt_ + 1), xTL[(pt_ + 1) % 2], f"xTL{(pt_ + 1) % 2}", pre_rows(pt_ + 2))[0] if pt_ + 1 < NPRE else []
            set_c1(pt_ * 2)
            hg_pre(0, TT, [0, 1], xb=(xTL[pt_ % 2], f"xTL{pt_ % 2}"), inter=nxt)
        add_barrier()
        add(None, lambda slot: emit_casts(1000))
        dump("S_pre", Sst[:], [("S", d_, h_) for d_ in range(2) for h_ in range(8)])
        set_c1(None)
        STOP = _os.environ.get("STOP", "")
        for ti in (range(NT - 1, -1, -1) if STOP != "pre" else []):
            prep(x_ext[128 + ti * TT: 128 + (ti + 1) * TT, :], 4, 128, 0, xT, "xT")
            add_barrier()
            hg_tile("bwd", ti * TT, 128)
            add_barrier()
        if dbg:
            add_barrier()
            add(None, lambda slot: dma("pool", o_sb[:], ob_d[:, :, 0:TT].rearrange("h p t -> p h t"), "osb", reads=[("ob_d", 0)], writes=["osb"]))
            for h_ in range(8):
                dump(f"ob{h_}", o_sb[:, h_, :], ["osb"])
            add_barrier()
        for ti in (range(NT) if STOP == "" else []):
            prep(x_ext[ti * TT: ti * TT + 768, :], 6, 0, 0, xT, "xT")
            add_barrier()
            attn_tile(ti)
            add_barrier()
            if ti == 0:
                dump("xT0", xT[:, :, :], ["xT"])
                dump("oT_att", oT_att[:], ["oT_att"])
            hg_tile("fwd", ti * TT, 128)
            add_barrier()
            if ti == 0:
                dump("oT_hg", oT_hg[:], ["oT_hg"])
            merge_tile()
            add_barrier()
            if ti == 0:
                dump("mergedT", mergedT[:], ["mergedT"])
            tm_proj_norm(mergedT, "mergedT", wb_out, "g5", 1, zbuf, "z", 0, True, False, ti)
            if ti == 0:
                dump("x1", x1[:], [("x1", s_) for s_ in range(4)])
            prep(x1, 4, 0, 1, h2T, "h2T", rows_key="x1", xs=xsf2)
            add_barrier()
            if ti == 0:
                dump("h2T", h2T[:], ["h2T"])
            ffn1_tile()
            add_barrier()
            tm_proj_norm(uT, "uT", wb_ff2, "g7", 4, z2, "z2", 1, False, True, ti)
            add_barrier()

        order = []
        last_use = {}
        for idx, (p, f) in enumerate(items):
            if isinstance(p, dict):
                if id(p) not in last_use:
                    order.append(p)
                last_use[id(p)] = idx
        pos = {id(p): k for k, p in enumerate(order)}
        state = {"next": 0}

        def issue_load(k):
            p = order[k]
            slot = k % NSLOT
            p["slot"] = slot
            dma("sp", wsl[slot][:, :p["nk"], :], p["src"], f"w{slot}", reads=[p["wkey"]], writes=[("w", slot)])

        for idx, (p, f) in enumerate(items):
            if p == "barrier":
                P.barrier()
                continue
            while state["next"] < len(order) and (state["next"] < NSLOT or last_use[id(order[state["next"] - NSLOT])] < idx):
                issue_load(state["next"])
                state["next"] += 1
            if isinstance(p, dict):
                assert pos[id(p)] < state["next"], "weight slot deadlock"
                f(p["slot"])
            else:
                f(None)

        P.replay(nc, sems, dsem, final_eng="sp")
    return nc


def const_tables():
    j = np.arange(128)[:, None].astype(np.float32)
    r = np.arange(128)[None, :].astype(np.float32)
    dl = np.where(j >= r, -(r + 128 - j), NEG)
    dc = -np.abs(j - r)
    dr = np.where(j <= r, -(j + 128 - r), NEG)
    edge = np.full((128, 128), NEG, np.float32)
    dtab = np.stack([dl, dc, dr, edge, edge], axis=1).astype(np.float32)
    s = np.arange(128)[:, None]
    t = np.arange(128)[None, :]
    same = (s // 64) == (t // 64)
    s3 = np.arange(128)[:, None, None, None]
    d3 = np.arange(2)[None, :, None, None]
    p3 = np.arange(2)[None, None, :, None]
    t3 = np.arange(64)[None, None, None, :]
    hgmask = ((s3 // 64 == p3) & np.where(d3 == 0, (s3 % 64) <= t3, (s3 % 64) >= t3)).astype(np.float32)
    scan = np.ones((128, TT), np.float32)
    scan[:, ::64] = 0.0
    return dtab, hgmask, scan, np.eye(128, dtype=np.float32)


def core_inputs(seq_x, start, ntok, is_first, is_last, npre, meta_tokens, shared):
    S = seq_x.shape[0]
    x_ext = np.zeros((ntok + 256, D), np.float32)
    lo = max(0, start - 128)
    hi = min(S, start + ntok + 128)
    x_ext[128 - (start - lo): 128 - (start - lo) + (hi - lo)] = seq_x[lo:hi]
    x_pre = np.zeros((max(npre, 1) * TT, D), np.float32)
    premask = np.zeros((128, max(npre, 1) * 2), np.float32)
    npf = start // TT
    nsf = (S - start - ntok) // TT
    assert npf + nsf <= npre
    for i in range(npf):
        x_pre[i * TT:(i + 1) * TT] = seq_x[i * TT:(i + 1) * TT]
        premask[:, 2 * i] = 1.0
    for i in range(nsf):
        t0 = S - (i + 1) * TT
        x_pre[(npf + i) * TT:(npf + i + 1) * TT] = seq_x[t0:t0 + TT]
        premask[:, 2 * (npf + i) + 1] = 1.0
    dtab, hgmask, scan, ident = shared["tables"]
    dtab = dtab.copy()
    if not is_first:
        dtab[:, 3] = dtab[:, 0]
    if not is_last:
        dtab[:, 4] = dtab[:, 2]
    m = dict(shared["weights"])
    rowmask = np.zeros((128, 2), np.float32)
    rowmask[:64, 0] = 1.0
    rowmask[64:, 1] = 1.0
    onesc = np.zeros((128, 4, 128), np.float32)
    onesc[:, 0, 0:64] = 1.0
    onesc[:, 1, 64:128] = 1.0
    onesc[:16, 2, 0:64] = 1.0
    onesc[:16, 3, 64:128] = 1.0
    m.update(onesc=onesc)
    m.update(x_ext=x_ext, x_pre=x_pre, premask=premask, dtab=dtab, hgmask=hgmask, scanmask=scan, identf=ident, rowmask=rowmask)
    return m


def shared_inputs(meta_tokens, w_in, w_proj_hg, w_proj_att, w_out, w_ff1, w_ff2, g_pre_mix, g_post_mix, g_pre_ff, g_post_ff,
                  lb_logits, hg_out_gain, attn_sink):
    f = lambda a: np.ascontiguousarray(np.asarray(a, dtype=np.float32))
    meta = np.zeros((128, D), np.float32)
    meta[:16] = f(meta_tokens)
    gpre = np.stack([f(g_pre_mix)[0].reshape(KC, 128).T, f(g_pre_ff)[0].reshape(KC, 128).T], axis=1)
    gpost = np.stack([f(g_post_mix)[0], f(g_post_ff)[0]], axis=0)
    lbl = f(lb_logits).reshape(2, 2, 8, 128).transpose(3, 0, 1, 2)
    hgain = f(hg_out_gain)[0].reshape(8, 128).T
    sk = f(attn_sink)[0]
    sink = np.zeros((128, 8), np.float32)
    sink[:64] = sk[0::2][None, :]
    sink[64:] = sk[1::2][None, :]
    w = dict(meta=meta, w_in=f(w_in)[0], w_phg=f(w_proj_hg)[0], w_patt=f(w_proj_att)[0], w_out=f(w_out)[0], w_ff1=f(w_ff1)[0],
             w_ff2=f(w_ff2)[0], gpre=f(gpre), gpost=f(gpost), lbl=f(lbl), hgain=f(hgain), sink=sink)
    return {"weights": w, "tables": const_tables()}


_CACHE = {}


def run_layout(seqs, core_plan, NT, NPRE, shared, full=False):
    key = (NT, NPRE)
    if key not in _CACHE:
        _CACHE[key] = build(NT, NPRE)
    nc = _CACHE[key]
    in_maps = []
    for (si, start) in core_plan:
        S = seqs[si].shape[0]
        in_maps.append(core_inputs(seqs[si], start, NT * TT, start == 0, start + NT * TT == S, NPRE, None, shared))
    res = run_bass_kernel_spmd(nc, in_maps, core_ids=list(range(len(core_plan))))
    if full:
        return res.results
    return [r["y"] for r in res.results]


def kernel(x_prompt, x_sample, meta_tokens, w_in, w_proj_hg, w_proj_att, w_out, w_ff1, w_ff2,
           g_pre_mix, g_post_mix, g_pre_ff, g_post_ff, lb_logits, hg_out_gain, attn_sink):
    x_prompt = np.asarray(x_prompt, dtype=np.float32)
    x_sample = np.asarray(x_sample, dtype=np.float32)
    shared = shared_inputs(meta_tokens, w_in, w_proj_hg, w_proj_att, w_out, w_ff1, w_ff2, g_pre_mix, g_post_mix,
                           g_pre_ff, g_post_ff, lb_logits, hg_out_gain, attn_sink)
    seqs = [x_prompt[b] for b in range(4)] + [x_sample[0]]
    plan = [(b, 0) for b in range(4)] + [(4, c * 4096) for c in range(4)]
    outs = run_layout(seqs, plan, 8, 24, shared)
    y_prompt = np.stack(outs[:4], axis=0)
    y_sample = np.concatenate(outs[4:], axis=0)[None]
    return (y_prompt, y_sample)
```
